# Optimizing a Trainium2 kernel written in Bass

```python
import jax, jax.numpy as jnp
from jax import lax
import numpy as np

D_MODEL = 1024
BATCH = 32
SEQ = 2048
DEPTH = 1
DEC_BATCH = 16
DEC_SEQ = 64
PAST_LEN = 1024

CHUNK = 64
ML_HEADS = 4
ML_DHEAD = D_MODEL // ML_HEADS
ML_WIDTH = ML_HEADS * ML_DHEAD
CONV_W = 4
ATT_HEADS = 16
ATT_DHEAD = D_MODEL // ATT_HEADS
ATT_WIDTH = ATT_HEADS * ATT_DHEAD
BAND_CHUNKS = 8
MAX_REL = 128
D_FF = ((8 * D_MODEL) // 3 + 127) // 128 * 128
ALPHA = (2.0 * DEPTH) ** 0.25
BETA = (8.0 * DEPTH) ** -0.25
LN_EPS = 1e-5
NEG_INF = -1e30
SEG_SIZES = (ML_WIDTH,) * 4 + (ML_HEADS,) * 2 + (ATT_WIDTH,) * 3 + (D_MODEL,) * 2
SPLIT_IDX = [int(s) for s in np.cumsum(SEG_SIZES)[:-1]]
IN_WIDTH = int(sum(SEG_SIZES))

kernel_name = 'hybrid_streaming_mlstm_bandattn_step'


def layer_norm(x, g, b):
    xf = x.astype(jnp.float32)
    mu = jnp.mean(xf, axis=-1, keepdims=True)
    var = jnp.mean(jnp.square(xf - mu), axis=-1, keepdims=True)
    y = (xf - mu) * lax.rsqrt(var + LN_EPS) * g.astype(jnp.float32) + b.astype(jnp.float32)
    return y.astype(x.dtype)


def swiglu_ffn(x, w_gu, w_down):
    gate, up = jnp.split(x @ w_gu, 2, axis=-1)
    return (jax.nn.silu(gate) * up) @ w_down


def causal_dwconv(x_pad, w, b):
    t = x_pad.shape[1] - (CONV_W - 1)
    out = b
    for j in range(CONV_W):
        out = out + x_pad[:, j:j + t] * w[j]
    return out


def mlstm_chunk(carry, inp):
    C, n, m = carry
    q, k, v, lf, li = inp
    L = q.shape[2]
    bcum = jnp.cumsum(lf, axis=-1)
    causal = jnp.tril(jnp.ones((L, L), dtype=bool))
    dmat = jnp.where(causal, bcum[..., :, None] - bcum[..., None, :] + li[..., None, :], -jnp.inf)
    inter = bcum + m[..., None]
    m_t = jnp.maximum(inter, jnp.max(dmat, axis=-1))
    w_inter = jnp.exp(inter - m_t)
    s = jnp.einsum('bhtd,bhsd->bhts', q, k) * jnp.exp(dmat - m_t[..., None])
    num = w_inter[..., None] * jnp.einsum('bhtd,bhde->bhte', q, C) + jnp.einsum('bhts,bhse->bhte', s, v)
    den = w_inter * jnp.einsum('bhtd,bhd->bht', q, n) + jnp.sum(s, axis=-1)
    h = num / jnp.maximum(jnp.abs(den), jnp.exp(-m_t))[..., None]
    b_last = bcum[..., -1]
    g_s = b_last[..., None] - bcum + li
    m_new = jnp.maximum(b_last + m, jnp.max(g_s, axis=-1))
    decay = jnp.exp(b_last + m - m_new)
    w_s = jnp.exp(g_s - m_new[..., None])
    C_new = decay[..., None, None] * C + jnp.einsum('bhs,bhsd,bhse->bhde', w_s, k, v)
    n_new = decay[..., None] * n + jnp.einsum('bhs,bhsd->bhd', w_s, k)
    return (C_new, n_new, m_new), h


def mlstm_run(q, k, v, li, lf, state):
    b, h, t, _ = q.shape
    L = min(t, CHUNK)
    nc = t // L

    def blocks(a):
        return jnp.moveaxis(a.reshape(a.shape[:2] + (nc, L) + a.shape[3:]), 2, 0)

    state, hs = lax.scan(mlstm_chunk, state, (blocks(q), blocks(k), blocks(v), blocks(lf), blocks(li)))
    hs = jnp.moveaxis(hs, 0, 2).reshape(b, h, t, v.shape[-1])
    return hs, state


def rel_bias_block(table, q_pos, k_pos):
    rel = jnp.clip(q_pos[:, None] - k_pos[None, :], -MAX_REL, MAX_REL) + MAX_REL
    return table[:, rel].astype(jnp.float32)


def band_attention_prompt(q, k, v, table):
    b, t, h, d = q.shape
    nc = t // CHUNK
    pad = BAND_CHUNKS * CHUNK
    band = pad + CHUNK
    kp = jnp.pad(k, ((0, 0), (pad, 0), (0, 0), (0, 0)))
    vp = jnp.pad(v, ((0, 0), (pad, 0), (0, 0), (0, 0)))
    bias = rel_bias_block(table, pad + jnp.arange(CHUNK), jnp.arange(band))

    def one_chunk(c):
        qc = lax.dynamic_slice_in_dim(q, c * CHUNK, CHUNK, axis=1)
        kc = lax.dynamic_slice_in_dim(kp, c * CHUNK, band, axis=1)
        vc = lax.dynamic_slice_in_dim(vp, c * CHUNK, band, axis=1)
        s = jnp.einsum('bqhd,bkhd->bhqk', qc, kc).astype(jnp.float32) + bias
        k_valid = (c - BAND_CHUNKS) * CHUNK + jnp.arange(band) >= 0
        p = jax.nn.softmax(jnp.where(k_valid, s, NEG_INF), axis=-1)
        return jnp.einsum('bhqk,bkhd->bqhd', p.astype(vc.dtype), vc)

    o = lax.map(one_chunk, jnp.arange(nc))
    return jnp.moveaxis(o, 0, 1).reshape(b, t, h, d)


def band_attention_sample(q, k_new, v_new, k_past, v_past, table):
    n_past, t = k_past.shape[1], q.shape[1]
    kk = jnp.concatenate([k_past.astype(k_new.dtype), k_new], axis=1)
    vv = jnp.concatenate([v_past.astype(v_new.dtype), v_new], axis=1)
    bias = rel_bias_block(table, n_past + jnp.arange(t), jnp.arange(n_past + t))
    s = jnp.einsum('bqhd,bkhd->bhqk', q, kk).astype(jnp.float32) + bias
    p = jax.nn.softmax(s, axis=-1)
    return jnp.einsum('bhqk,bkhd->bqhd', p.astype(vv.dtype), vv)


def token_mixer(xn, p, conv_prev, ml_state, att_past):
    b, t, _ = xn.shape
    f32 = jnp.float32
    (ml_q, ml_k, ml_v, ml_o, ml_i, ml_f, a_q, a_k, a_v, g_ml, g_att) = jnp.split(xn @ p['w_in'], SPLIT_IDX, axis=-1)
    qk_pad = jnp.concatenate([conv_prev.astype(xn.dtype), jnp.concatenate([ml_q, ml_k], axis=-1)], axis=1)
    qk = jax.nn.silu(causal_dwconv(qk_pad, p['ml_conv_w'], p['ml_conv_b']))
    new_conv = qk_pad[:, qk_pad.shape[1] - (CONV_W - 1):]
    q_c, k_c = jnp.split(qk, 2, axis=-1)

    def ml_heads(a):
        return a.reshape(b, t, ML_HEADS, ML_DHEAD).transpose(0, 2, 1, 3).astype(f32)

    qm = ml_heads(q_c)
    km = ml_heads(k_c) * (ML_DHEAD ** -0.5)
    vm = ml_heads(ml_v)
    li = (ml_i + p['b_ml_i']).astype(f32).transpose(0, 2, 1)
    lf = jax.nn.log_sigmoid((ml_f + p['b_ml_f']).astype(f32)).transpose(0, 2, 1)
    C0, n0, m0 = ml_state
    h_ml, (C1, n1, m1) = mlstm_run(qm, km, vm, li, lf, (C0.astype(f32), n0.astype(f32), m0.astype(f32)))
    h_ml = h_ml.transpose(0, 2, 1, 3)
    mu = jnp.mean(h_ml, axis=-1, keepdims=True)
    var = jnp.mean(jnp.square(h_ml - mu), axis=-1, keepdims=True)
    h_ml = ((h_ml - mu) * lax.rsqrt(var + LN_EPS)).reshape(b, t, ML_WIDTH)
    h_ml = h_ml * p['ml_norm_g'].astype(f32) * jax.nn.sigmoid(ml_o.astype(f32))
    y_ml = h_ml.astype(xn.dtype) @ p['w_ml_proj']
    qa = a_q.reshape(b, t, ATT_HEADS, ATT_DHEAD) * (ATT_DHEAD ** -0.5)
    ka = a_k.reshape(b, t, ATT_HEADS, ATT_DHEAD)
    va = a_v.reshape(b, t, ATT_HEADS, ATT_DHEAD)
    if att_past is None:
        o = band_attention_prompt(qa, ka, va, p['att_rel_bias'])
        keep = min(BAND_CHUNKS * CHUNK, t)
        k_rows, v_rows = ka[:, t - keep:], va[:, t - keep:]
    else:
        o = band_attention_sample(qa, ka, va, att_past[0], att_past[1], p['att_rel_bias'])
        k_rows, v_rows = ka, va
    y_att = o.reshape(b, t, ATT_WIDTH) @ p['w_att_proj']
    merged = jax.nn.sigmoid(g_ml) * y_ml + jax.nn.sigmoid(g_att) * y_att
    new_state = (new_conv, C1.astype(xn.dtype), n1.astype(xn.dtype), m1.astype(xn.dtype), k_rows, v_rows)
    return merged @ p['w_out'], new_state


def encoder_layer(x, p, conv_prev, ml_state, att_past):
    h = layer_norm(ALPHA * x + 0.5 * swiglu_ffn(x, p['ffn1_w_gu'], p['ffn1_w_down']), p['ln1_g'], p['ln1_b'])
    mix, new_state = token_mixer(h, p, conv_prev, ml_state, att_past)
    h = layer_norm(ALPHA * h + mix, p['ln2_g'], p['ln2_b'])
    h = layer_norm(ALPHA * h + 0.5 * swiglu_ffn(h, p['ffn2_w_gu'], p['ffn2_w_down']), p['ln3_g'], p['ln3_b'])
    return h, new_state


def setup_inputs(seed: int = 0) -> dict:
    key = jax.random.key(seed)
    ks = jax.random.split(key, 32)
    f32 = jnp.float32

    def nrm(k, shape, scale):
        return jax.random.normal(k, shape, f32) * scale

    att_past = min(BAND_CHUNKS * CHUNK, PAST_LEN)
    col_scale = np.ones((IN_WIDTH,), np.float32)
    col_scale[SPLIT_IDX[1]:SPLIT_IDX[2]] = BETA
    col_scale[SPLIT_IDX[7]:SPLIT_IDX[8]] = BETA
    return {
        'x_prompt': nrm(ks[0], (BATCH, SEQ, D_MODEL), 1.0),
        'x_sample': nrm(ks[1], (DEC_BATCH, DEC_SEQ, D_MODEL), 1.0),
        'state_ml_conv': nrm(ks[2], (DEPTH, DEC_BATCH, CONV_W - 1, 2 * ML_WIDTH), 1.0),
        'state_ml_C': nrm(ks[3], (DEPTH, DEC_BATCH, ML_HEADS, ML_DHEAD, ML_DHEAD), 0.1),
        'state_ml_n': jnp.abs(nrm(ks[4], (DEPTH, DEC_BATCH, ML_HEADS, ML_DHEAD), 0.1)),
        'state_ml_m': nrm(ks[5], (DEPTH, DEC_BATCH, ML_HEADS), 0.5),
        'cache_att_k': nrm(ks[6], (DEPTH, DEC_BATCH, att_past, ATT_HEADS, ATT_DHEAD), 1.0),
        'cache_att_v': nrm(ks[7], (DEPTH, DEC_BATCH, att_past, ATT_HEADS, ATT_DHEAD), 1.0),
        'w_in': nrm(ks[8], (DEPTH, D_MODEL, IN_WIDTH), D_MODEL ** -0.5) * jnp.asarray(col_scale),
        'b_ml_i': nrm(ks[9], (DEPTH, ML_HEADS), 0.1),
        'b_ml_f': jnp.linspace(3.0, 6.0, ML_HEADS, dtype=f32) + nrm(ks[10], (DEPTH, ML_HEADS), 0.1),
        'ml_conv_w': nrm(ks[11], (DEPTH, CONV_W, 2 * ML_WIDTH), CONV_W ** -0.5),
        'ml_conv_b': nrm(ks[12], (DEPTH, 2 * ML_WIDTH), 0.02),
        'ml_norm_g': 1.0 + nrm(ks[13], (DEPTH, ML_WIDTH), 0.02),
        'att_rel_bias': nrm(ks[14], (DEPTH, ATT_HEADS, 2 * MAX_REL + 1), 0.5),
        'w_ml_proj': nrm(ks[15], (DEPTH, ML_WIDTH, D_MODEL), BETA * ML_WIDTH ** -0.5),
        'w_att_proj': nrm(ks[16], (DEPTH, ATT_WIDTH, D_MODEL), BETA * ATT_WIDTH ** -0.5),
        'w_out': nrm(ks[17], (DEPTH, D_MODEL, D_MODEL), BETA * D_MODEL ** -0.5),
        'ffn1_w_gu': nrm(ks[18], (DEPTH, D_MODEL, 2 * D_FF), D_MODEL ** -0.5),
        'ffn1_w_down': nrm(ks[19], (DEPTH, D_FF, D_MODEL), BETA * D_FF ** -0.5),
        'ffn2_w_gu': nrm(ks[20], (DEPTH, D_MODEL, 2 * D_FF), D_MODEL ** -0.5),
        'ffn2_w_down': nrm(ks[21], (DEPTH, D_FF, D_MODEL), BETA * D_FF ** -0.5),
        'ln1_g': 1.0 + nrm(ks[22], (DEPTH, D_MODEL), 0.02),
        'ln1_b': nrm(ks[23], (DEPTH, D_MODEL), 0.02),
        'ln2_g': 1.0 + nrm(ks[24], (DEPTH, D_MODEL), 0.02),
        'ln2_b': nrm(ks[25], (DEPTH, D_MODEL), 0.02),
        'ln3_g': 1.0 + nrm(ks[26], (DEPTH, D_MODEL), 0.02),
        'ln3_b': nrm(ks[27], (DEPTH, D_MODEL), 0.02),
    }


def reference(x_prompt, x_sample, state_ml_conv, state_ml_C, state_ml_n, state_ml_m, cache_att_k, cache_att_v,
              w_in, b_ml_i, b_ml_f, ml_conv_w, ml_conv_b, ml_norm_g, att_rel_bias, w_ml_proj, w_att_proj, w_out,
              ffn1_w_gu, ffn1_w_down, ffn2_w_gu, ffn2_w_down, ln1_g, ln1_b, ln2_g, ln2_b, ln3_g, ln3_b):
    f32 = jnp.float32
    bp = x_prompt.shape[0]
    y_prompt, y_sample = x_prompt, x_sample
    prompt_states, sample_states = [], []
    for l in range(DEPTH):
        p = {'w_in': w_in[l], 'b_ml_i': b_ml_i[l], 'b_ml_f': b_ml_f[l], 'ml_conv_w': ml_conv_w[l],
             'ml_conv_b': ml_conv_b[l], 'ml_norm_g': ml_norm_g[l], 'att_rel_bias': att_rel_bias[l],
             'w_ml_proj': w_ml_proj[l], 'w_att_proj': w_att_proj[l], 'w_out': w_out[l],
             'ffn1_w_gu': ffn1_w_gu[l], 'ffn1_w_down': ffn1_w_down[l], 'ffn2_w_gu': ffn2_w_gu[l],
             'ffn2_w_down': ffn2_w_down[l], 'ln1_g': ln1_g[l], 'ln1_b': ln1_b[l], 'ln2_g': ln2_g[l],
             'ln2_b': ln2_b[l], 'ln3_g': ln3_g[l], 'ln3_b': ln3_b[l]}
        conv0 = jnp.zeros((bp, CONV_W - 1, 2 * ML_WIDTH), x_prompt.dtype)
        ml0 = (jnp.zeros((bp, ML_HEADS, ML_DHEAD, ML_DHEAD), f32),
               jnp.zeros((bp, ML_HEADS, ML_DHEAD), f32),
               jnp.zeros((bp, ML_HEADS), f32))
        y_prompt, st_p = encoder_layer(y_prompt, p, conv0, ml0, None)
        y_sample, st_s = encoder_layer(y_sample, p, state_ml_conv[l],
                                       (state_ml_C[l], state_ml_n[l], state_ml_m[l]),
                                       (cache_att_k[l], cache_att_v[l]))
        prompt_states.append(st_p)
        sample_states.append(st_s)
    p_conv, p_C, p_n, p_m, p_k, p_v = [jnp.stack(s) for s in zip(*prompt_states)]
    s_conv, s_C, s_n, s_m, s_k, s_v = [jnp.stack(s) for s in zip(*sample_states)]
    return (y_prompt, y_sample, p_conv, s_conv, p_C, s_C, p_n, s_n, p_m, s_m, p_k, s_k, p_v, s_v)
```

```python
import os
import numpy as np
from contextlib import ExitStack
import concourse.bass as bass
import concourse.mybir as mybir
from concourse.bass_utils import run_bass_kernel_spmd

F32 = mybir.dt.float32
BF16 = mybir.dt.bfloat16
AF = mybir.ActivationFunctionType
ALU = mybir.AluOpType
AX = mybir.AxisListType

D = 1024
KD = 8
FF = 2816
KF = 22
INW = 9224
NH = 4
AH = 16
ALPHA = 2.0 ** 0.25
EPS = 1e-5
EPS_A = EPS / (ALPHA * ALPHA)
NEG = -30000.0
RING = 6
NDSEM = {"sp": 8, "pool": 3}


class Tok:
    __slots__ = ("w", "r")

    def __init__(self):
        self.w = None
        self.r = []


class Op:
    __slots__ = ("eng", "fn", "deps", "sig", "sem", "val", "dma")

    def __init__(self, eng, fn, dma):
        self.eng, self.fn, self.dma = eng, fn, dma
        self.deps = []
        self.sig = False
        self.sem = None
        self.val = 0


class Prog:
    ENG = ("pe", "act", "dve", "pool", "sp")

    def __init__(self):
        self.ops = {e: [] for e in self.ENG}
        self.toks = {}
        self.order = []

    def tk(self, *key):
        t = self.toks.get(key)
        if t is None:
            t = self.toks[key] = Tok()
        return t

    def add(self, eng, fn, reads=(), writes=(), dma=False):
        op = Op(eng, fn, dma)
        excl = [k for k in reads if isinstance(k, tuple) and k[0] == "bank"]
        reads_k = [k for k in reads if not (isinstance(k, tuple) and k[0] == "bank")]
        writes_k = list(writes) + [k for k in excl if k not in writes]
        bank_raw = [self.tk(*k) for k in excl]
        reads = [self.tk(*k) if isinstance(k, tuple) else self.tk(k) for k in reads_k]
        writes = [self.tk(*k) if isinstance(k, tuple) else self.tk(k) for k in writes_k]
        raw = set()
        deps = []
        for t in reads:
            if t.w is not None:
                deps.append(t.w)
                raw.add(id(t.w))
        for t in bank_raw:
            if t.w is not None:
                raw.add(id(t.w))
        for t in writes:
            if t.w is not None:
                deps.append(t.w)
            deps.extend(t.r)
        seen = set()
        for d in deps:
            if id(d) in seen or d is op:
                continue
            seen.add(id(d))
            if (not d.dma) and (not dma) and d.eng == eng:
                if eng == "pe" or id(d) not in raw:
                    continue
            op.deps.append(d)
            d.sig = True
        for t in writes:
            t.w = op
            t.r = []
        for t in reads:
            if t.w is not op:
                t.r.append(op)
        self.ops[eng].append(op)
        self.order.append(op)
        return op

    def assign(self, sems, dsems):
        cnt = {e: 0 for e in self.ENG}
        dcnt = {e: [0] * len(dsems.get(e, [])) for e in self.ENG}
        dlast = {e: [None] * len(dsems.get(e, [])) for e in self.ENG}
        dnext = {e: 0 for e in self.ENG}
        for op in self.order:
            if op.dma:
                k = dnext[op.eng]
                dnext[op.eng] = (k + 1) % len(dsems[op.eng])
                if dlast[op.eng][k] is not None:
                    op.deps.append(dlast[op.eng][k])
                dcnt[op.eng][k] += 16
                op.sem, op.val = dsems[op.eng][k], dcnt[op.eng][k]
                dlast[op.eng][k] = op
                op.sig = True
            elif op.sig:
                cnt[op.eng] += 1
                op.sem, op.val = sems[op.eng], cnt[op.eng]

    def emit(self, eng, e):
        waited = {}
        for op in self.ops[eng]:
            for d in op.deps:
                key = id(d.sem)
                if waited.get(key, 0) < d.val:
                    e.wait_ge(d.sem, d.val)
                    waited[key] = d.val
            ins = op.fn(e)
            if op.sig:
                ins.then_inc(op.sem, 16 if op.dma else 1)


def build(NSP, SEQ, NSS, STOP=99):
    NT = 2
    G = NT * 128
    NG = SEQ // G
    NTS = SEQ // 128
    KEEP_T = min(4, NTS)
    nc = bass.Bass("TRN2", target_bir_lowering=False)
    P = Prog()

    def din(name, shape, dt=F32):
        return nc.dram_tensor(name, list(shape), dt, kind="ExternalInput").ap()

    def dout(name, shape):
        return nc.dram_tensor(name, list(shape), F32, kind="ExternalOutput").ap()

    def dscr(name, shape, dt=BF16):
        return nc.dram_tensor(name, list(shape), dt, kind="Internal").ap()

    x_p = din("x_p", [NSP, SEQ, D])
    x_s = din("x_s", [NSS, 64, D])
    st_conv = din("st_conv", [NSS, 3, 2048])
    st_C = din("st_C", [NSS, NH, 256, 256])
    st_n = din("st_n", [NSS, NH, 256])
    st_m = din("st_m", [NSS, NH])
    ck = din("ck", [NSS, 512, D])
    cv = din("cv", [NSS, 512, D])
    wnames = {"gu1": (D, 2 * FF), "dn1": (FF, D), "win": (D, INW), "wml": (D, D), "wat": (D, D),
              "wout": (D, D), "gu2": (D, 2 * FF), "dn2": (FF, D)}
    NOW = os.environ.get("KNOW", "") == "1"
    wf = {k: (dscr("w_" + k, v, F32) if NOW else din("w_" + k, v)) for k, v in wnames.items()}
    wb = {k: dscr("wb_" + k, v) for k, v in wnames.items()}
    lnp = din("lnp", [128, 6, D])
    gbias = din("gbias", [128, 8])
    conv_w = din("conv_w", [4, 2048])
    conv_b = din("conv_b", [2048])
    norm_g = din("norm_g", [D])
    btg = din("btg", [128, AH, 256])
    btc = din("btc", [128, AH])
    c_ident = din("c_ident", [128, 128])
    c_tri = din("c_tri", [128, 128])
    c_mhi = din("c_mhi", [128, 256])
    c_mb0 = din("c_mb0", [128, 128])
    c_i4 = din("c_i4", [4, 4])

    y_p = dout("y_p", [NSP, SEQ, D])
    y_s = dout("y_s", [NSS, 64, D])
    o_conv = {"p": dout("p_conv", [NSP, 3, 2048]), "s": dout("s_conv", [NSS, 3, 2048])}
    o_C = {"p": dout("p_C", [NSP, NH, 256, 256]), "s": dout("s_C", [NSS, NH, 256, 256])}
    o_n = {"p": dout("p_n", [NSP, NH, 256]), "s": dout("s_n", [NSS, NH, 256])}
    o_m = {"p": dout("p_m", [NSP, NH]), "s": dout("s_m", [NSS, NH])}
    o_k = {"p": dout("p_k", [NSP, KEEP_T * 128, D]), "s": dout("s_k", [NSS, 64, D])}
    o_v = {"p": dout("p_v", [NSP, KEEP_T * 128, D]), "s": dout("s_v", [NSS, 64, D])}

    es = ExitStack()
    with es:
        def sb(name, shape, dt=F32):
            return es.enter_context(nc.sbuf_tensor(name, list(shape), dt))

        resid = sb("resid", [128, NT, D])
        curT = [sb("curTA", [128, KD, G], BF16), sb("curTB", [128, KD, G], BF16)]
        actT = sb("actT", [128, KF, G], BF16)
        qk_raw = sb("qk_raw", [128, 16, 3 + G], BF16)
        qkc = sb("qkc", [128, 16, G], BF16)
        v_tm = sb("v_tm", [128, NT, D], BF16)
        sigo = sb("sigo", [128, KD, G], BF16)
        aqT = sb("aqT", [128, KD, G], BF16)
        kring = sb("kring", [128, KD, RING, 128], BF16)
        vring = sb("vring", [128, RING, D], BF16)
        wslab = [sb("wslab%d" % i, [128, 4096], BF16) for i in range(3)]
        btm = sb("btm", [128, AH, 256], BF16)
        mb0 = sb("mb0", [128, 128], BF16)
        Cst = sb("Cst", [128, 8, 256])
        nst = sb("nst", [128, 8])
        Cs = [sb("CsA", [128, 8, 256], BF16), sb("CsB", [128, 8, 256], BF16)]
        ns = [sb("nsA", [128, 8], BF16), sb("nsB", [128, 8], BF16)]
        lnt = sb("lnt", [128, 6, D])
        ident = sb("ident", [128, 128], BF16)
        identf = sb("identf", [128, 128])
        tri = sb("tri", [128, 128])
        tri64 = sb("tri64", [128, 128])
        onesf = sb("onesf", [128, 128])
        onesb = sb("onesb", [128, 2], BF16)
        i4 = sb("i4", [4, 4])
        mhalf = sb("mhalf", [128, 4])
        dummy = sb("dummyb", [4, 32])
        vecs = sb("vecs", [128, 128])
        vT = sb("vT", [128, 88])
        nT = sb("nT", [8, 128])
        cw = vT[:, 0:64].rearrange("p (j c) -> p j c", c=16)
        cb = vT[:, 64:80]
        ngh = vT[:, 80:88]
        gb = sb("gb", [128, 8])
        th = [sb("th%d" % i, [128, 512]) for i in range(2)]
        uu = [sb("uu%d" % i, [128, 512]) for i in range(2)]
        xn = sb("xn", [128, D])
        hb = sb("hb", [128, D], BF16)
        stg = [sb("stg%d" % i, [128, D]) for i in range(2)]
        cstage = sb("cstage", [128, 2048])
        cacc = [sb("cacc%d" % i, [128, G]) for i in range(2)]
        cth = [sb("cth%d" % i, [128, G]) for i in range(2)]
        pexp = [sb("pexp%d" % i, [128, 640], BF16) for i in range(2)]
        pts = [sb("pts%d" % i, [128, 5, 128], BF16) for i in range(2)]
        o_tm = sb("o_tm", [128, D], BF16)
        h_tm = sb("h_tm", [128, D], BF16)
        ks_tm = sb("ks_tm", [128, 8, 128], BF16)
        ptm = sb("ptm", [128, 4, 128], BF16)
        qz = [sb("qzA", [128, 8, 128], BF16), sb("qzB", [128, 8, 128], BF16)]
        zg = sb("zg", [128, NT, 8])
        spt = sb("spt", [128, NT, 4])
        nbc = sb("nbc", [128, NT, 4])
        agt = sb("agt", [128, NT, 4])
        e8 = sb("e8", [128, 8])
        ucl = sb("ucl", [128, 8])
        u32 = sb("u32", [128, 4])
        mcol = sb("mcol", [128, 4])
        NCH = 2 * NT
        amaxT = sb("amaxT", [4, NCH])
        blT = sb("blT", [4, NCH])
        MD = sb("MD", [4, 2, NCH])
        dx = sb("dx", [4, 4, 2, NCH])
        mT = sb("mT", [4, 1])
        Mb = sb("Mb", [128, 4, NCH])
        rho = sb("rho", [128, 4, NCH])
        rhoh = sb("rhoh", [128, 4, NCH])
        dens = sb("dens", [128, 4])
        sm = {n: sb("sm_" + n, [128, 4]) for n in ("dabs", "dn", "rden", "t1", "rs", "nb2")}
        bst = sb("bst", [128, 4, 6])
        mv = sb("mv", [128, 4, 2])
        lst = sb("lst", [128, 2, 6])
        lmv = sb("lmv", [128, 2])
        lsm = {n: sb("lsm_" + n, [128, 1]) for n in ("ve", "rstd", "nmr")}
        nmax = sb("nmax", [128, AH])
        rsum = sb("rsum", [128, AH])
        rinv = sb("rinv", [128, AH])
        tmp_c = sb("tmp_c", [128, AH])

        NBK = 6
        ps = es.enter_context(nc.psum_tensor("ps", [128, NBK, 512], F32))
        pt = es.enter_context(nc.psum_tensor("pt", [128, 2, 1024], BF16))

        class _PB:
            def __getitem__(self, k):
                return pt[k[0], k[1] - 100, k[2]]
        psb = _PB()

        dsems = {e: [es.enter_context(nc.semaphore("d_%s%d" % (e, i))) for i in range(n)]
                 for e, n in NDSEM.items()}
        sems = {e: es.enter_context(nc.semaphore("s_" + e)) for e in ("pe", "act", "dve", "pool")}

        bank_ptr = [0]

        pinned = set()
        tb_ptr = [0]

        def bank(n=1):
            b = bank_ptr[0]
            for _ in range(2 * NBK):
                if n == 2 and b % 2:
                    b = (b + 1) % NBK
                if all(((b + i) % NBK) not in pinned for i in range(n)) and b + n <= NBK:
                    break
                b = (b + 1) % NBK
            else:
                raise RuntimeError("no free psum bank")
            bank_ptr[0] = (b + n) % NBK
            return b

        def tbank():
            tb_ptr[0] ^= 1
            return 100 + tb_ptr[0]

        def BK(b):
            return ("bank", b)

        flip = [0]

        def evac_eng():
            flip[0] ^= 1
            return "act" if flip[0] else "dve"

        def copy_op(eng, out, in_, reads, writes, scale=None):
            if eng == "act":
                if scale is None:
                    P.add("act", lambda e: e.activation(out=out, in_=in_, func=AF.Copy), reads, writes)
                else:
                    P.add("act", lambda e: e.activation(out=out, in_=in_, func=AF.Copy, scale=scale), reads, writes)
            else:
                if scale is None:
                    P.add(eng, lambda e: e.tensor_copy(out=out, in_=in_), reads, writes)
                else:
                    P.add(eng, lambda e: e.tensor_scalar(out=out, in0=in_, scalar1=scale, scalar2=None,
                                                         op0=ALU.mult), reads, writes)

        def dma(eng, out, in_, reads, writes, slow=False):
            if slow:
                P.add(eng, lambda e: e.dma_start(out=out, in_=in_, allow_slow_non_contiguous=True),
                      reads, writes, dma=True)
            else:
                P.add(eng, lambda e: e.dma_start(out=out, in_=in_), reads, writes, dma=True)

        slab_i = [0]

        def load_slab(pieces):
            i = slab_i[0] % 3
            slab_i[0] += 1
            off = 0
            views = []
            for (wk, k0, nk, c0, ncol) in pieces:
                v = wslab[i][:, off:off + nk * ncol].rearrange("p (k n) -> p k n", n=ncol)
                src = wb[wk].rearrange("(k p) n -> p k n", p=128)[:, k0:k0 + nk, c0:c0 + ncol]
                dma("sp", v, src, [("wb", wk)], [("wslab", i)])
                views.append(v)
                off += nk * ncol
            assert off <= 4096
            return views, ("wslab", i)

        def mm_group(out, pairs, reads, writes, first=True, last=True):
            def fn(e):
                n = len(pairs)
                ins = None
                for i, (l, r) in enumerate(pairs):
                    ins = e.matmul(out, lhsT=l, rhs=r, start=(first and i == 0), stop=(last and i == n - 1))
                return ins
            P.add("pe", fn, reads, writes)

        def transposes(b, src_fn, n, reads, dtype_bf=True, cols=128):
            def fn(e):
                ins = None
                for i in range(n):
                    ins = e.transpose(out=psb[:, b, i * 128:(i + 1) * 128], in_=src_fn(i), identity=ident[:])
                return ins
            P.add("pe", fn, reads + ["ident"], [BK(b)])

        def setup():
            fst = [(cstage[:], ["cstage"]), (resid[:].rearrange("p a n -> p (a n)"), [("resid", 0), ("resid", 1)])]
            bst_ = [(actT[:, 0:8, :].rearrange("p a n -> p (a n)"), [("actT", j) for j in range(8)]),
                    (actT[:, 8:16, :].rearrange("p a n -> p (a n)"), [("actT", j) for j in range(8, 16)])]
            ci_ = 0
            SKIP = os.environ.get("KSKIP", "").split(",")
            for wk in ("gu1", "dn1", "win", "wml", "wat", "wout", "gu2", "dn2"):
                if "W" in SKIP:
                    break
                rows, ncol = wnames[wk]
                parts = []
                for r0 in range(0, rows, 128):
                    for c0 in range(0, ncol, 2048):
                        w_ = min(2048, ncol - c0)
                        fa, ft = fst[ci_ % 2]
                        ba, bt = bst_[ci_ % 2]
                        dma("sp", fa[:, 0:w_], wf[wk][r0:r0 + 128, c0:c0 + w_], [], ft)
                        copy_op(("act", "dve", "pool")[ci_ % 3], ba[:, 0:w_], fa[:, 0:w_], ft, bt)
                        tk_ = ("wbp", wk, len(parts))
                        dma("sp", wb[wk][r0:r0 + 128, c0:c0 + w_], ba[:, 0:w_], bt, [tk_])
                        parts.append(tk_)
                        ci_ += 1
                wi_ = list(wnames).index(wk)
                dma("sp", dummy[:, wi_ * 4:(wi_ + 1) * 4], c_i4[:, :], parts, [("wb", wk)])
            dma("sp", identf[:], c_ident[:, :], [], ["identf"])
            dma("sp", tri[:], c_tri[:, :], [], ["tri"])
            P.add("dve", lambda e: e.tensor_copy(out=i4[:], in_=identf[0:4, 0:4]), ["identf"], ["i4"])
            if "B" not in SKIP:
                dma("sp", lnt[:], lnp[:, :, :], [], ["lnt"])
                dma("sp", gb[:], gbias[:, :], [], ["gb"])
            if "V" not in SKIP:
                dma("sp", vecs[0:64, :], conv_w.rearrange("j (c p) -> (j c) p", p=128), [], ["vecs"])
                dma("sp", vecs[64:80, :], conv_b.rearrange("(c p) -> c p", p=128), [], ["vecs"])
                dma("sp", vecs[80:88, :], norm_g.rearrange("(c p) -> c p", p=128), [], ["vecs"])
                bv = bank()
                P.add("pe", lambda e: e.transpose(out=ps[:, bv, 0:88], in_=vecs[0:88, :], identity=identf[0:88, 0:88]),
                      ["vecs", "identf"], [BK(bv)])
                P.add("dve", lambda e: e.tensor_copy(out=vT[:], in_=ps[:, bv, 0:88]), [BK(bv)], ["cw", "cb", "ngh"])
                P.add("dve", lambda e: e.tensor_scalar(out=ngh, in0=ngh, scalar1=0.5, scalar2=None,
                                                       op0=ALU.mult), ["ngh"], ["ngh"])
            P.add("dve", lambda e: e.tensor_copy(out=ident[:], in_=identf[:]), ["identf"], ["ident"])
            P.add("dve", lambda e: e.tensor_scalar(out=tri64[:], in0=tri[:], scalar1=1.0 / 64.0, scalar2=None,
                                                   op0=ALU.mult), ["tri"], ["tri64"])
            if "M" not in SKIP:
                P.add("pool", lambda e: e.memset(onesf[:], 1.0), [], ["onesf"])
                P.add("pool", lambda e: e.memset(onesb[:], 1.0), [], ["onesb"])
                P.add("pool", lambda e: e.memset(mhalf[:], -0.5), [], ["mhalf"])
                P.add("pool", lambda e: e.memset(qz[0][:], 0.0), [], ["qz"])
                P.add("pool", lambda e: e.memset(qz[1][:], 0.0), [], ["qz"])
                P.add("pool", lambda e: e.memset(dx[:], 0.0), [], ["dx"])
                P.add("pool", lambda e: e.memset(MD[:], 0.0), [], ["MD"])
            if "T" not in SKIP:
                dma("sp", tmp_c[:], btc[:, :], [], ["tmp_c"])
                dma("sp", xn[:, 0:256], c_mhi[:, :], [], ["xn"])
                dma("sp", xn[:, 256:384], c_mb0[:, :], [], ["xn"])
                P.add("dve", lambda e: e.tensor_copy(out=mb0[:], in_=xn[:, 256:384]), ["xn"], ["mb0"])
            for hh in range(2):
                if "T" in SKIP:
                    break
                dma("sp", cstage[:].rearrange("p (a n) -> p a n", n=256), btg[:, hh * 8:(hh + 1) * 8, :], [], ["cstage"])
                for h8 in range(8):
                    h = hh * 8 + h8
                    P.add("dve", lambda e, h=h, h8=h8: e.scalar_tensor_tensor(
                        out=btm[:, h, :], in0=cstage[:, h8 * 256:(h8 + 1) * 256], scalar=tmp_c[:, h:h + 1], in1=xn[:, 0:256],
                        op0=ALU.subtract, op1=ALU.add), ["cstage", "tmp_c", "xn"], ["btm"])

        def layernorm(t, li, eps, cout, final, ydst, nrows):
            R = ("resid", t)
            def f_stats(e):
                e.bn_stats(out=lst[:, 0, :], in_=resid[:, t, 0:512])
                return e.bn_stats(out=lst[:, 1, :], in_=resid[:, t, 512:1024])
            P.add("dve", f_stats, [R], ["lst"])
            P.add("dve", lambda e: e.bn_aggr(out=lmv[:], in_=lst[:].rearrange("p a b -> p (a b)")), ["lst"], ["lmv"])
            P.add("dve", lambda e: e.tensor_scalar(out=lsm["ve"][:], in0=lmv[:, 1:2], scalar1=eps, scalar2=None,
                                                   op0=ALU.add), ["lmv"], ["l_ve"])
            P.add("act", lambda e: e.activation(out=lsm["ve"][:], in_=lsm["ve"][:], func=AF.Ln), ["l_ve"], ["l_ve"])
            P.add("act", lambda e: e.activation(out=lsm["rstd"][:], in_=lsm["ve"][:], func=AF.Exp, scale=-0.5),
                  ["l_ve"], ["l_rstd"])
            P.add("dve", lambda e: e.scalar_tensor_tensor(out=lsm["nmr"][:], in0=lmv[:, 0:1], scalar=-1.0,
                                                          in1=lsm["rstd"][:], op0=ALU.mult, op1=ALU.mult),
                  ["lmv", "l_rstd"], ["l_nmr"])
            P.add("act", lambda e: e.activation(out=xn[:], in_=resid[:, t, :], func=AF.Identity,
                                                bias=lsm["nmr"][:], scale=lsm["rstd"][:]),
                  [R, "l_rstd", "l_nmr"], ["xn"])
            P.add("pool", lambda e: e.tensor_tensor(out=xn[:], in0=xn[:], in1=lnt[:, 2 * li, :], op=ALU.mult),
                  ["xn", "lnt"], ["xn"])
            P.add("pool", lambda e: e.tensor_tensor(out=resid[:, t, :], in0=xn[:], in1=lnt[:, 2 * li + 1, :],
                                                    op=ALU.add), ["xn", "lnt"], [R])
            if final:
                dma("sp", ydst, resid[0:nrows, t, :], [R], [("yout", t)])
            else:
                to_fm(resid[:, t, :], [R], cout, t)

        def to_fm(src, reads, cout, t):
            copy_op(evac_eng(), hb[:], src, reads, ["hb"])
            b = tbank()
            transposes(b, lambda i: hb[:, i * 128:(i + 1) * 128], 8, ["hb"])
            copy_op(evac_eng(), curT[cout][:, :, t * 128:(t + 1) * 128],
                    psb[:, b, :].rearrange("p (k n) -> p k n", n=128), [BK(b)], [("curT", cout, t)])

        def ffn(gk, dk, li, cin, cout, ntl, final, ydsts, nrows):
            Gt = ntl * 128
            cin_r = [("curT", cin, t) for t in range(ntl)]
            for s0 in range(0, KF, 2):
                views, wt = load_slab([(gk, 0, 8, s0 * 128, 256), (gk, 0, 8, FF + s0 * 128, 256)])
                for jj in range(2):
                    j = s0 + jj
                    b = bank()
                    mm_group(ps[:, b, 0:Gt], [(views[0][:, k, jj * 128:(jj + 1) * 128], curT[cin][:, k, 0:Gt])
                                              for k in range(8)], cin_r + [wt], [BK(b)])
                    mm_group(ps[:, b, 256:256 + Gt], [(views[1][:, k, jj * 128:(jj + 1) * 128], curT[cin][:, k, 0:Gt])
                                                      for k in range(8)], cin_r + [wt], [BK(b)])
                    i = j % 2
                    P.add("act", lambda e, b=b, i=i: e.activation(out=th[i][:, 0:Gt], in_=ps[:, b, 0:Gt],
                                                                  func=AF.Tanh, scale=0.5), [BK(b)], [("th", i)])
                    P.add("dve", lambda e, b=b, i=i: e.scalar_tensor_tensor(
                        out=uu[i][:, 0:Gt], in0=th[i][:, 0:Gt], scalar=1.0, in1=ps[:, b, 0:Gt],
                        op0=ALU.add, op1=ALU.mult), [("th", i), BK(b)], [("uu", i)])
                    P.add("dve", lambda e, b=b, i=i, j=j: e.tensor_tensor(
                        out=actT[:, j, 0:Gt], in0=uu[i][:, 0:Gt], in1=ps[:, b, 256:256 + Gt], op=ALU.mult),
                        [("uu", i), BK(b)], [("actT", j)])
            csc = 0.25 / ALPHA
            kparts = [(0, 8), (8, 7), (15, 7)]
            for half in range(2):
                banks = [bank() for _ in range(ntl)]
                for pi, (k0, nk) in enumerate(kparts):
                    views, wt = load_slab([(dk, k0, nk, half * 512, 512)])
                    for t in range(ntl):
                        mm_group(ps[:, banks[t], :],
                                 [(actT[:, k0 + kk, t * 128:(t + 1) * 128], views[0][:, kk, :]) for kk in range(nk)],
                                 [("actT", k0 + kk) for kk in range(nk)] + [wt], [BK(banks[t])],
                                 first=(pi == 0), last=(pi == len(kparts) - 1))
                for t in range(ntl):
                    P.add("dve", lambda e, t=t, bb=banks[t], half=half: e.scalar_tensor_tensor(
                        out=resid[:, t, half * 512:(half + 1) * 512], in0=ps[:, bb, :], scalar=csc,
                        in1=resid[:, t, half * 512:(half + 1) * 512], op0=ALU.mult, op1=ALU.add),
                        [BK(banks[t]), ("resid", t)], [("resid", t)])
            for t in range(ntl):
                layernorm(t, li, EPS_A, cout, final, ydsts[t] if final else None, nrows)

        def group(tiles):
            ntl = len(tiles)
            Gt = ntl * 128
            nrows = tiles[0]["nrows"]
            if STOP <= 1:
                return True
            for t, tl in enumerate(tiles):
                if nrows < 128:
                    P.add("pool", lambda e, t=t: e.memset(resid[64:128, t, :], 0.0), [], [("resid", t)])
                dma("sp", resid[0:nrows, t, :], tl["xsrc"], [], [("resid", t)])
                to_fm(resid[:, t, :], [("resid", t)], 0, t)
            if STOP <= 2:
                return True
            ffn("gu1", "dn1", 0, 0, 1, ntl, False, None, nrows)
            h1 = 1
            h1_r = [("curT", 1, t) for t in range(ntl)]
            if STOP <= 3:
                return True
            tl0 = tiles[0]
            if tl0["first"]:
                if tl0["kind"] == "p":
                    P.add("pool", lambda e: e.memset(qk_raw[:, :, 0:3], 0.0), [], [("qk_raw", c) for c in range(16)])
                else:
                    dma("sp", vecs[0:48, :], st_conv[tl0["b"]].rearrange("r (c p) -> (r c) p", p=128), [], ["vecs"])
                    dma("sp", vecs[48:56, :], st_n[tl0["b"]].rearrange("h (j p) -> (h j) p", p=128), [], ["vecs"])
                    bv = bank()
                    P.add("pe", lambda e, bv=bv: e.transpose(out=ps[:, bv, 0:56], in_=vecs[0:56, :],
                                                             identity=identf[0:56, 0:56]), ["vecs", "identf"], [BK(bv)])
                    P.add("dve", lambda e, bv=bv: e.tensor_copy(
                        out=qk_raw[:, :, 0:3], in_=ps[:, bv, 0:48].rearrange("p (r c) -> p c r", c=16)), [BK(bv)],
                        [("qk_raw", c) for c in range(16)])
                    P.add("dve", lambda e, bv=bv: e.tensor_copy(out=nst[:], in_=ps[:, bv, 48:56]), [BK(bv)], ["nst"])
            for s in range(8):
                views, wt = load_slab([("win", 0, 8, s * 256, 256)])
                for jj in range(2):
                    c = s * 2 + jj
                    b = bank()
                    mm_group(ps[:, b, 0:Gt], [(views[0][:, k, jj * 128:(jj + 1) * 128], curT[h1][:, k, 0:Gt])
                                              for k in range(8)], h1_r + [wt], [BK(b)])
                    copy_op(evac_eng(), qk_raw[:, c, 3:3 + Gt], ps[:, b, 0:Gt], [BK(b)], [("qk_raw", c)])
                for t, tl in enumerate(tiles):
                    if tl["conv_out"]:
                        b = bank()
                        mm_group(ps[:, b, 0:256], [(curT[h1][:, k, t * 128:(t + 1) * 128], views[0][:, k, :])
                                                   for k in range(8)], h1_r + [wt], [BK(b)])
                        copy_op(evac_eng(), cstage[:, s * 256:(s + 1) * 256], ps[:, b, 0:256], [BK(b)], ["cstage"])
            for t, tl in enumerate(tiles):
                if tl["conv_out"]:
                    r0 = tl["nrows"] - 3
                    dma("sp", o_conv[tl["kind"]][tl["b"]], cstage[r0:r0 + 3, :], ["cstage"], ["cstage_out"])
            if STOP <= 3.1:
                return True
            for s in range(2):
                views, wt = load_slab([("win", 0, 8, 2048 + s * 512, 512)])
                for t in range(ntl):
                    b = bank()
                    mm_group(ps[:, b, :], [(curT[h1][:, k, t * 128:(t + 1) * 128], views[0][:, k, :]) for k in range(8)],
                             h1_r + [wt], [BK(b)])
                    copy_op(evac_eng(), v_tm[:, t, s * 512:(s + 1) * 512], ps[:, b, :], [BK(b)], [("v_tm", t)])
            if STOP <= 3.2:
                return True
            for s in range(4):
                views, wt = load_slab([("win", 0, 8, 3072 + s * 256, 256)])
                for jj in range(2):
                    c = s * 2 + jj
                    b = bank()
                    mm_group(ps[:, b, 0:Gt], [(views[0][:, k, jj * 128:(jj + 1) * 128], curT[h1][:, k, 0:Gt])
                                              for k in range(8)], h1_r + [wt], [BK(b)])
                    i = c % 2
                    P.add("act", lambda e, b=b, i=i: e.activation(out=th[i][:, 0:Gt], in_=ps[:, b, 0:Gt],
                                                                  func=AF.Tanh, scale=0.5), [BK(b)], [("th", i)])
                    P.add("dve", lambda e, c=c, i=i: e.tensor_scalar(
                        out=sigo[:, c, 0:Gt], in0=th[i][:, 0:Gt], scalar1=ngh[:, c:c + 1], scalar2=ngh[:, c:c + 1],
                        op0=ALU.mult, op1=ALU.add), [("th", i), "ngh"], [("sigo", c)])
            if STOP <= 3.3:
                return True
            views, wt = load_slab([("win", 0, 8, 4096, 8)])
            for t in range(ntl):
                b = bank()
                mm_group(ps[:, b, 0:8], [(curT[h1][:, k, t * 128:(t + 1) * 128], views[0][:, k, :]) for k in range(8)],
                         h1_r + [wt], [BK(b)])
                P.add("dve", lambda e, t=t, b=b: e.tensor_tensor(out=zg[:, t, :], in0=ps[:, b, 0:8], in1=gb[:],
                                                                  op=ALU.add), [BK(b), "gb"], [("zg", t)])
            if STOP <= 3.4:
                return True
            for s in range(0 if "Q" in os.environ.get("KSKIP", "") else 4):
                views, wt = load_slab([("win", 0, 8, 4104 + s * 256, 256)])
                for jj in range(2):
                    c = s * 2 + jj
                    b = bank()
                    mm_group(ps[:, b, 0:Gt], [(views[0][:, k, jj * 128:(jj + 1) * 128], curT[h1][:, k, 0:Gt])
                                              for k in range(8)], h1_r + [wt], [BK(b)])
                    copy_op(evac_eng(), aqT[:, c, 0:Gt], ps[:, b, 0:Gt], [BK(b)], [("aqT", c)], scale=0.125)
            if STOP <= 3.5:
                return True
            for s in range(4):
                views, wt = load_slab([("win", 0, 8, (0 if "Z" in os.environ.get("KSKIP", "") else 5128) + s * 256, 256)])
                for jj in range(2):
                    c = s * 2 + jj
                    b = bank()
                    mm_group(ps[:, b, 0:Gt], [(views[0][:, k, jj * 128:(jj + 1) * 128], curT[h1][:, k, 0:Gt])
                                              for k in range(8)], h1_r + [wt], [BK(b)])
                    for t, tl in enumerate(tiles):
                        copy_op(evac_eng(), (sigo[:, c, t * 128:(t + 1) * 128] if "R" in os.environ.get("KSKIP", "")
                                             else kring[:, c, tl["own_slot"], :]), ps[:, b, t * 128:(t + 1) * 128],
                                [BK(b)], [("kring", tl["own_slot"])])
                for t, tl in enumerate(tiles):
                    if tl["kv_out"] is not None and "K" not in os.environ.get("KSKIP", ""):
                        b = bank()
                        mm_group(ps[:, b, 0:256], [(curT[h1][:, k, t * 128:(t + 1) * 128], views[0][:, k, :])
                                                   for k in range(8)], h1_r + [wt], [BK(b)])
                        copy_op(evac_eng(), stg[t][:, s * 256:(s + 1) * 256], ps[:, b, 0:256], [BK(b)], [("stg", t)])
            for t, tl in enumerate(tiles):
                if tl["kv_out"] is not None and "D" not in os.environ.get("KSKIP", ""):
                    r0 = tl["kv_out"]
                    dma("sp", o_k[tl["kind"]][tl["b"], r0:r0 + tl["nrows"], :], stg[t][0:tl["nrows"], :],
                        [("stg", t)], [("stg_out", t)])
            if STOP <= 3.6:
                return True
            for s in range(2):
                views, wt = load_slab([("win", 0, 8, 6152 + s * 512, 512)])
                for t, tl in enumerate(tiles):
                    b = bank()
                    mm_group(ps[:, b, :], [(curT[h1][:, k, t * 128:(t + 1) * 128], views[0][:, k, :]) for k in range(8)],
                             h1_r + [wt], [BK(b)])
                    copy_op("act", vring[:, tl["own_slot"], s * 512:(s + 1) * 512], ps[:, b, :], [BK(b)],
                            [("vring", tl["own_slot"])])
                    if tl["kv_out"] is not None:
                        copy_op("dve", cstage[:, t * 1024 + s * 512:t * 1024 + (s + 1) * 512], ps[:, b, :], [BK(b)],
                                ["cstage"])
            for t, tl in enumerate(tiles):
                if tl["kv_out"] is not None:
                    r0 = tl["kv_out"]
                    dma("sp", o_v[tl["kind"]][tl["b"], r0:r0 + tl["nrows"], :],
                        cstage[0:tl["nrows"], t * 1024:(t + 1) * 1024], ["cstage"], ["cstage_out"])

            if STOP <= 4:
                return True
            for c in range(16):
                i = c % 2
                P.add("dve", lambda e, c=c, i=i: e.tensor_scalar(
                    out=cacc[i][:, 0:Gt], in0=qk_raw[:, c, 0:Gt], scalar1=cw[:, 0, c:c + 1], scalar2=cb[:, c:c + 1],
                    op0=ALU.mult, op1=ALU.add), [("qk_raw", c), "cw", "cb"], [("cacc", i)])
                for j in range(1, 4):
                    P.add("dve", lambda e, c=c, i=i, j=j: e.scalar_tensor_tensor(
                        out=cacc[i][:, 0:Gt], in0=qk_raw[:, c, j:j + Gt], scalar=cw[:, j, c:c + 1], in1=cacc[i][:, 0:Gt],
                        op0=ALU.mult, op1=ALU.add), [("qk_raw", c), "cw", ("cacc", i)], [("cacc", i)])
                P.add("act", lambda e, i=i: e.activation(out=cth[i][:, 0:Gt], in_=cacc[i][:, 0:Gt], func=AF.Tanh,
                                                         scale=0.5), [("cacc", i)], [("cth", i)])
                P.add("dve", lambda e, c=c, i=i: e.scalar_tensor_tensor(
                    out=qkc[:, c, 0:Gt], in0=cth[i][:, 0:Gt], scalar=1.0, in1=cacc[i][:, 0:Gt],
                    op0=ALU.add, op1=ALU.mult), [("cth", i), ("cacc", i)], [("qkc", c)])
            if tiles[-1]["kind"] == "p" and not tiles[-1]["last"]:
                P.add("pool", lambda e: e.tensor_copy(out=qk_raw[:, :, 0:3], in_=qk_raw[:, :, Gt:Gt + 3]),
                      [("qk_raw", c) for c in range(16)], [("qk_raw", c) for c in range(16)])

            if STOP <= 5:
                return True
            nch = 2 * ntl
            zr = [("zg", t) for t in range(ntl)]
            P.add("act", lambda e: e.activation(out=spt[:, 0:ntl, :], in_=zg[:, 0:ntl, 4:8], func=AF.Exp, scale=-1.0),
                  zr, ["spt"])
            P.add("act", lambda e: e.activation(out=spt[:, 0:ntl, :], in_=spt[:, 0:ntl, :], func=AF.Ln, bias=1.0),
                  ["spt"], ["spt"])
            for t in range(ntl):
                b = bank()
                mm_group(ps[:, b, 0:4], [(tri[:], spt[:, t, :])], ["tri", "spt"], [BK(b)])
                P.add("dve", lambda e, t=t, b=b: e.tensor_copy(out=nbc[:, t, :], in_=ps[:, b, 0:4]), [BK(b)], [("nbc", t)])
                P.add("dve", lambda e, t=t: e.tensor_tensor(out=agt[:, t, :], in0=zg[:, t, 0:4], in1=nbc[:, t, :],
                                                            op=ALU.add), [("zg", t), ("nbc", t)], [("agt", t)])
                b2 = bank()

                def ftr(e, t=t, b2=b2):
                    e.transpose(out=ps[0:4, b2, 0:128], in_=agt[:, t, :], identity=identf[:])
                    return e.transpose(out=ps[0:4, b2, 128:256], in_=spt[:, t, :], identity=identf[:])
                P.add("pe", ftr, [("agt", t), "spt", "identf"], [BK(b2)])
                P.add("dve", lambda e, t=t, b2=b2: e.tensor_reduce(
                    out=amaxT[:, 2 * t:2 * t + 2], in_=ps[0:4, b2, 0:128].rearrange("p (c s) -> p c s", s=64),
                    axis=AX.X, op=ALU.max), [BK(b2)], ["amaxT"])
                P.add("dve", lambda e, t=t, b2=b2: e.tensor_reduce(
                    out=blT[:, 2 * t:2 * t + 2], in_=ps[0:4, b2, 128:256].rearrange("p (c s) -> p c s", s=64),
                    axis=AX.X, op=ALU.add, negate=True), [BK(b2)], ["blT"])
            for t, tl in enumerate(tiles):
                if tl["first"]:
                    if tl["kind"] == "p":
                        P.add("dve", lambda e: e.memset(mT[:], 0.0), [], ["mT"])
                    else:
                        dma("sp", mT[:], st_m[tl["b"]].rearrange("(h o) -> h o", o=1), [], ["mT"], slow=True)
                for ci in range(2):
                    c = 2 * t + ci
                    P.add("dve", lambda e, c=c: e.tensor_tensor(out=MD[:, 0, c:c + 1], in0=mT[:], in1=amaxT[:, c:c + 1],
                                                                op=ALU.max), ["mT", "amaxT"], ["MD"])
                    P.add("dve", lambda e, c=c: e.tensor_tensor(out=MD[:, 1, c:c + 1], in0=mT[:], in1=MD[:, 0, c:c + 1],
                                                                op=ALU.subtract), ["mT", "MD"], ["MD"])
                    if ci < tl["nchunks"]:
                        P.add("dve", lambda e, c=c: e.tensor_tensor(out=mT[:], in0=blT[:, c:c + 1], in1=MD[:, 0, c:c + 1],
                                                                    op=ALU.add), ["blT", "MD"], ["mT"])
            for hh in range(4):
                P.add("dve", lambda e, hh=hh: e.tensor_scalar(out=dx[:, hh, :, 0:nch], in0=MD[:, :, 0:nch],
                                                              scalar1=i4[:, hh:hh + 1], scalar2=None, op0=ALU.mult),
                      ["MD", "i4"], ["dx"])
            b = bank()
            mm_group(ps[:, b, 0:8 * NCH], [(onesf[0:4, :], dx[:].rearrange("p a b c -> p (a b c)"))],
                     ["onesf", "dx"], [BK(b)])
            pv = ps[:, b, 0:8 * NCH].rearrange("p (a b c) -> p a b c", a=4, b=2)
            P.add("dve", lambda e, pv=pv: e.tensor_copy(out=Mb[:], in_=pv[:, :, 0, :]), [BK(b)], ["Mb"])
            P.add("act", lambda e, pv=pv: e.activation(out=rho[:], in_=pv[:, :, 1, :], func=AF.Exp), [BK(b)], ["rho"])
            P.add("dve", lambda e: e.tensor_scalar(out=rhoh[:], in0=rho[:], scalar1=0.5, scalar2=None, op0=ALU.mult),
                  ["rho"], ["rhoh"])

            if STOP <= 6:
                return True
            for t, tl in enumerate(tiles):
                tc = slice(t * 128, (t + 1) * 128)
                for ci in range(2):
                    c = 2 * t + ci
                    P.add("dve", lambda e, ci=ci, c=c: e.tensor_copy(out=mcol[ci * 64:(ci + 1) * 64, :],
                                                                     in_=Mb[ci * 64:(ci + 1) * 64, :, c]), ["Mb"], ["mcol"])
                P.add("dve", lambda e, t=t: e.tensor_tensor(out=e8[:, 0:4], in0=agt[:, t, :], in1=mcol[:], op=ALU.subtract),
                      [("agt", t), "mcol"], ["e8"])
                P.add("dve", lambda e, t=t: e.tensor_tensor(out=e8[:, 4:8], in0=nbc[:, t, :], in1=mcol[:], op=ALU.subtract),
                      [("nbc", t), "mcol"], ["e8"])
                P.add("act", lambda e: e.activation(out=ucl[:], in_=e8[:], func=AF.Exp), ["e8"], ["ucl"])
                P.add("dve", lambda e: e.tensor_scalar(out=u32[:], in0=ucl[:, 0:4], scalar1=1.0 / 32.0, scalar2=None,
                                                       op0=ALU.mult), ["ucl"], ["u32"])
                bS = bank()
                for h in range(4):
                    mm_group(ps[:, bS, h * 128:(h + 1) * 128],
                             [(qkc[:, 8 + 2 * h + j, tc], qkc[:, 2 * h + j, tc]) for j in range(2)],
                             [("qkc", cc) for cc in (8 + 2 * h, 9 + 2 * h, 2 * h, 2 * h + 1)], [BK(bS)])
                for h in range(4):
                    P.add("dve", lambda e, h=h, bS=bS: e.scalar_tensor_tensor(
                        out=ptm[:, h, :], in0=ps[:, bS, h * 128:(h + 1) * 128], scalar=ucl[:, h:h + 1], in1=tri64[:],
                        op0=ALU.mult, op1=ALU.mult), [BK(bS), "ucl", "tri64"], ["ptm"])
                bK = tbank()
                transposes(bK, lambda i, tc=tc: qkc[:, 8 + i, tc], 8, [("qkc", 8 + i) for i in range(8)])
                for h in range(4):
                    P.add("act", lambda e, h=h, bK=bK: e.activation(
                        out=ks_tm[:, 2 * h:2 * h + 2, :],
                        in_=psb[:, bK, 2 * h * 128:(2 * h + 2) * 128].rearrange("p (a n) -> p a n", n=128),
                        func=AF.Copy, scale=u32[:, h:h + 1]), [BK(bK), "u32"], ["ks_tm"])
                P.add("pool", lambda e, t=t: e.tensor_copy(out=qz[0][:, :, 0:64], in_=qkc[:, 0:8, t * 128:t * 128 + 64]),
                      [("qkc", i) for i in range(8)], ["qz"])
                P.add("pool", lambda e, t=t: e.tensor_copy(out=qz[1][:, :, 64:128], in_=qkc[:, 0:8, t * 128 + 64:t * 128 + 128]),
                      [("qkc", i) for i in range(8)], ["qz"])
                if tl["first"]:
                    if tl["kind"] == "p":
                        P.add("pool", lambda e: e.memset(Cst[:], 0.0), [], ["Cst"])
                        P.add("pool", lambda e: e.memset(nst[:], 0.0), [], ["nst"])
                    else:
                        dma("sp", Cst[:], st_C[tl["b"]].rearrange("h (j p) e -> p (h j) e", p=128), [], ["Cst"])
                for ci in range(2):
                    c = 2 * t + ci
                    r = slice(ci * 64, (ci + 1) * 64)
                    for h in range(4):
                        P.add("act", lambda e, h=h, ci=ci, c=c: e.activation(
                            out=Cs[ci][:, 2 * h:2 * h + 2, :], in_=Cst[:, 2 * h:2 * h + 2, :], func=AF.Copy,
                            scale=rhoh[:, h, c:c + 1]), ["Cst", "rhoh"], [("Cs", ci)])
                        P.add("dve", lambda e, h=h, ci=ci, c=c: e.tensor_scalar(
                            out=ns[ci][:, 2 * h:2 * h + 2], in0=nst[:, 2 * h:2 * h + 2], scalar1=rhoh[:, h, c:c + 1],
                            scalar2=None, op0=ALU.mult), ["nst", "rhoh"], [("ns", ci)])
                    if ci < tl["nchunks"]:
                        for hp in range(2):
                            bb = [bank(), bank()]
                            for hi in range(2):
                                h = 2 * hp + hi
                                for j in range(2):
                                    mm_group(ps[:, bb[hi], j * 256:(j + 1) * 256],
                                             [(ks_tm[r, 2 * h + j, :], v_tm[r, t, h * 256:(h + 1) * 256])],
                                             ["ks_tm", ("v_tm", t)], [BK(bb[hi])])
                            for hi in range(2):
                                h = 2 * hp + hi
                                P.add("dve", lambda e, h=h, c=c, bq=bb[hi]: e.scalar_tensor_tensor(
                                    out=Cst[:, 2 * h:2 * h + 2, :], in0=Cst[:, 2 * h:2 * h + 2, :], scalar=rho[:, h, c:c + 1],
                                    in1=ps[:, bq, :].rearrange("p (a n) -> p a n", n=256), op0=ALU.mult, op1=ALU.add),
                                    ["Cst", "rho", BK(bb[hi])], ["Cst"])
                        bn_ = bank()
                        for h in range(4):
                            for j in range(2):
                                mm_group(ps[:, bn_, 2 * h + j:2 * h + j + 1], [(ks_tm[r, 2 * h + j, :], onesb[r, 0:1])],
                                         ["ks_tm", "onesb"], [BK(bn_)])
                        for h in range(4):
                            P.add("dve", lambda e, h=h, c=c, bn_=bn_: e.scalar_tensor_tensor(
                                out=nst[:, 2 * h:2 * h + 2], in0=nst[:, 2 * h:2 * h + 2], scalar=rho[:, h, c:c + 1],
                                in1=ps[:, bn_, 2 * h:2 * h + 2], op0=ALU.mult, op1=ALU.add),
                                ["nst", "rho", BK(bn_)], ["nst"])
                for h in range(4):
                    bo = bank()
                    pairs = [(qz[0][:, 2 * h + j, :], Cs[0][:, 2 * h + j, :]) for j in range(2)] + \
                            [(qz[1][:, 2 * h + j, :], Cs[1][:, 2 * h + j, :]) for j in range(2)] + \
                            [(ptm[:, h, :], v_tm[:, t, h * 256:(h + 1) * 256])]
                    mm_group(ps[:, bo, 0:256], pairs, ["qz", ("Cs", 0), ("Cs", 1), "ptm", ("v_tm", t)], [BK(bo)])
                    pairs2 = [(qz[0][:, 2 * h + j, :], ns[0][:, 2 * h + j:2 * h + j + 1]) for j in range(2)] + \
                             [(qz[1][:, 2 * h + j, :], ns[1][:, 2 * h + j:2 * h + j + 1]) for j in range(2)] + \
                             [(ptm[:, h, :], onesb[:, 0:1])]
                    mm_group(ps[:, bo, 256:257], pairs2, ["qz", ("ns", 0), ("ns", 1), "ptm", "onesb"], [BK(bo)])
                    P.add("dve", lambda e, h=h, bo=bo: e.tensor_copy(out=dens[:, h:h + 1], in_=ps[:, bo, 256:257]),
                          [BK(bo)], ["dens"])
                    P.add("dve", lambda e, h=h, bo=bo: e.bn_stats(out=bst[:, h, :], in_=ps[:, bo, 0:256]), [BK(bo)], ["bst"])
                    P.add("dve", lambda e, h=h: e.bn_aggr(out=mv[:, h, :], in_=bst[:, h, :]), ["bst"], ["mv"])
                    if h == 0:
                        pass
                    tl.setdefault("_bo", []).append(bo)
                S = sm
                P.add("dve", lambda e: e.tensor_tensor(out=S["dabs"][:], in0=dens[:], in1=ucl[:, 4:8], op=ALU.max),
                      ["dens", "ucl"], ["s_dabs"])
                P.add("dve", lambda e: e.scalar_tensor_tensor(out=S["dn"][:], in0=dens[:], scalar=-1.0, in1=S["dabs"][:],
                                                              op0=ALU.mult, op1=ALU.max), ["dens", "s_dabs"], ["s_dn"])
                P.add("dve", lambda e: e.reciprocal(out=S["rden"][:], in_=S["dn"][:]), ["s_dn"], ["s_rden"])
                P.add("dve", lambda e: e.tensor_tensor(out=S["t1"][:], in0=S["rden"][:], in1=S["rden"][:], op=ALU.mult),
                      ["s_rden"], ["s_t1"])
                P.add("dve", lambda e: e.tensor_tensor(out=S["t1"][:], in0=S["t1"][:], in1=mv[:, :, 1], op=ALU.mult),
                      ["s_t1", "mv"], ["s_t1"])
                P.add("dve", lambda e: e.tensor_scalar(out=S["t1"][:], in0=S["t1"][:], scalar1=EPS, scalar2=None,
                                                       op0=ALU.add), ["s_t1"], ["s_t1"])
                P.add("act", lambda e: e.activation(out=S["t1"][:], in_=S["t1"][:], func=AF.Ln), ["s_t1"], ["s_t1"])
                P.add("act", lambda e: e.activation(out=S["rs"][:], in_=S["t1"][:], func=AF.Exp, scale=-0.5),
                      ["s_t1"], ["s_rs"])
                P.add("dve", lambda e: e.tensor_tensor(out=S["rs"][:], in0=S["rs"][:], in1=S["rden"][:], op=ALU.mult),
                      ["s_rs", "s_rden"], ["s_rs"])
                P.add("dve", lambda e: e.scalar_tensor_tensor(out=S["nb2"][:], in0=mv[:, :, 0], scalar=-1.0, in1=S["rs"][:],
                                                              op0=ALU.mult, op1=ALU.mult), ["mv", "s_rs"], ["s_nb2"])
                for h in range(4):
                    bo = tl["_bo"][h]
                    P.add("act", lambda e, h=h, bo=bo: e.activation(
                        out=h_tm[:, h * 256:(h + 1) * 256], in_=ps[:, bo, 0:256], func=AF.Identity,
                        bias=S["nb2"][:, h:h + 1], scale=S["rs"][:, h:h + 1]), [BK(bo), "s_rs", "s_nb2"], ["h_tm"])
                bT = tbank()
                transposes(bT, lambda i: h_tm[:, i * 128:(i + 1) * 128], 8, ["h_tm"])
                P.add("dve", lambda e, bT=bT, tc=tc: e.tensor_tensor(
                    out=actT[:, 0:8, tc], in0=psb[:, bT, :].rearrange("p (k n) -> p k n", n=128), in1=sigo[:, :, tc],
                    op=ALU.mult), [BK(bT)] + [("sigo", c) for c in range(8)], [("actT", c) for c in range(8)])

                if STOP <= 7:
                    continue
                slots = tl["slots"]
                valid = [kt for kt in range(5) if slots[kt] is not None]
                j0 = valid[0] * 128
                bO = bank(2)
                pinned.update((bO, bO + 1))
                for h in range(AH):
                    hp = slice((h % 2) * 64, (h % 2) * 64 + 64)
                    c = h // 2
                    bS2 = bank(2)
                    S2 = ps[:, bS2:bS2 + 2, :].rearrange("p a n -> p (a n)")

                    def fS(e, h=h, hp=hp, c=c, S2=S2, tc=tc):
                        ins = None
                        for kt in valid:
                            extra = None
                            if kt == 0:
                                extra = mb0[:]
                            elif kt == 3:
                                extra = btm[:, h, 0:128]
                            elif kt == 4:
                                extra = btm[:, h, 128:256]
                            ins = e.matmul(S2[:, kt * 128:(kt + 1) * 128], lhsT=aqT[hp, c, tc],
                                           rhs=kring[hp, c, slots[kt], :], start=True, stop=(extra is None))
                            if extra is not None:
                                ins = e.matmul(S2[:, kt * 128:(kt + 1) * 128], lhsT=ident[:], rhs=extra,
                                               start=False, stop=True)
                        return ins
                    P.add("pe", fS, [("aqT", c), "ident", "mb0", "btm"] + [("kring", slots[kt]) for kt in valid],
                          [BK(bS2), BK(bS2 + 1)])
                    P.add("dve", lambda e, h=h, S2=S2: e.tensor_reduce(out=nmax[:, h:h + 1], in_=S2[:, j0:640], axis=AX.X,
                                                                       op=ALU.max, negate=True),
                          [BK(bS2), BK(bS2 + 1)], [("nmax", h)])
                    i = h % 2
                    P.add("act", lambda e, h=h, S2=S2, i=i: e.activation(
                        out=pexp[i][:, j0:640], in_=S2[:, j0:640], func=AF.Exp, bias=nmax[:, h:h + 1],
                        accum_out=rsum[:, h:h + 1]), [BK(bS2), BK(bS2 + 1), ("nmax", h)], [("pexp", i), ("rsum", h)])
                    bP = tbank()

                    def fT(e, i=i, bP=bP):
                        ins = None
                        for kt in valid:
                            ins = e.transpose(out=psb[:, bP, kt * 128:(kt + 1) * 128], in_=pexp[i][:, kt * 128:(kt + 1) * 128],
                                              identity=ident[:])
                        return ins
                    P.add("pe", fT, [("pexp", i), "ident"], [BK(bP)])
                    copy_op(evac_eng(), pts[i][:, valid[0]:5, :],
                            psb[:, bP, j0:640].rearrange("p (k n) -> p k n", n=128), [BK(bP)], [("pts", i)])
                    ob = bO + (h // 8)
                    oc = (h % 8) * 64
                    mm_group(ps[:, ob, oc:oc + 64],
                             [(pts[i][:, kt, :], vring[:, slots[kt], h * 64:(h + 1) * 64]) for kt in valid],
                             [("pts", i)] + [("vring", slots[kt]) for kt in valid], [BK(ob)])
                P.add("dve", lambda e: e.reciprocal(out=rinv[:], in_=rsum[:]), [("rsum", h) for h in range(AH)], ["rinv"])
                for h in range(AH):
                    ob = bO + (h // 8)
                    oc = (h % 8) * 64
                    copy_op(evac_eng(), o_tm[:, h * 64:(h + 1) * 64], ps[:, ob, oc:oc + 64], [BK(ob), "rinv"], ["o_tm"],
                            scale=rinv[:, h:h + 1])
                pinned.clear()
                bT2 = tbank()
                transposes(bT2, lambda i: o_tm[:, i * 128:(i + 1) * 128], 8, ["o_tm"])
                copy_op(evac_eng(), actT[:, 8:16, tc], psb[:, bT2, :].rearrange("p (k n) -> p k n", n=128),
                        [BK(bT2)], [("actT", c) for c in range(8, 16)])

                if tl["last"]:
                    kd, bi = tl["kind"], tl["b"]
                    dma("sp", o_C[kd][bi].rearrange("h (j p) e -> p (h j) e", p=128), Cst[:], ["Cst"], ["Cout"])
                    bn2 = bank()
                    P.add("pe", lambda e, bn2=bn2: e.transpose(out=ps[0:8, bn2, 0:128], in_=nst[:], identity=identf[:]),
                          ["nst", "identf"], [BK(bn2)])
                    P.add("dve", lambda e, bn2=bn2: e.tensor_copy(out=nT[:], in_=ps[0:8, bn2, 0:128]), [BK(bn2)], ["nT"])
                    dma("sp", o_n[kd][bi].rearrange("h (j p) -> (h j) p", p=128), nT[:], ["nT"], ["nout"])
                    dma("sp", o_m[kd][bi].rearrange("(h o) -> h o", o=1), mT[:], ["mT"], ["mout"], slow=True)

            if STOP <= 8:
                return True
            for q4 in range(4):
                vX, wtX = load_slab([("wml", 0, 8, q4 * 256, 256), ("wat", 0, 8, q4 * 256, 256)])
                vY, wtY = load_slab([("win", 0, 8, 7176 + q4 * 256, 256), ("win", 0, 8, 8200 + q4 * 256, 256)])
                for jj in range(2):
                    c = q4 * 2 + jj
                    b1 = bank()
                    b2 = bank()
                    cs = slice(jj * 128, (jj + 1) * 128)
                    mm_group(ps[:, b1, 0:Gt], [(vY[0][:, k, cs], curT[h1][:, k, 0:Gt]) for k in range(8)], h1_r + [wtY], [BK(b1)])
                    mm_group(ps[:, b1, 256:256 + Gt], [(vY[1][:, k, cs], curT[h1][:, k, 0:Gt]) for k in range(8)],
                             h1_r + [wtY], [BK(b1)])
                    mm_group(ps[:, b2, 0:Gt], [(vX[0][:, k, cs], actT[:, k, 0:Gt]) for k in range(8)],
                             [("actT", k) for k in range(8)] + [wtX], [BK(b2)])
                    mm_group(ps[:, b2, 256:256 + Gt], [(vX[1][:, k, cs], actT[:, 8 + k, 0:Gt]) for k in range(8)],
                             [("actT", 8 + k) for k in range(8)] + [wtX], [BK(b2)])
                    i = c % 2
                    P.add("act", lambda e, b1=b1, i=i: e.activation(out=th[i][:], in_=ps[:, b1, :], func=AF.Tanh, scale=0.5),
                          [BK(b1)], [("th", i)])
                    P.add("dve", lambda e, b2=b2, i=i: e.scalar_tensor_tensor(
                        out=uu[i][:], in0=th[i][:], scalar=1.0, in1=ps[:, b2, :], op0=ALU.add, op1=ALU.mult),
                        [("th", i), BK(b2)], [("uu", i)])
                    P.add("dve", lambda e, c=c, i=i: e.tensor_tensor(
                        out=curT[0][:, c, 0:Gt], in0=uu[i][:, 0:Gt], in1=uu[i][:, 256:256 + Gt], op=ALU.add),
                        [("uu", i)], [("curT", 0, t) for t in range(ntl)])
            m_r = [("curT", 0, t) for t in range(ntl)]
            csc2 = 0.5 / ALPHA
            for half in range(2):
                views, wt = load_slab([("wout", 0, 8, half * 512, 512)])
                for t in range(ntl):
                    b = bank()
                    mm_group(ps[:, b, :], [(curT[0][:, k, t * 128:(t + 1) * 128], views[0][:, k, :]) for k in range(8)],
                             m_r + [wt], [BK(b)])
                    P.add("dve", lambda e, t=t, b=b, half=half: e.scalar_tensor_tensor(
                        out=resid[:, t, half * 512:(half + 1) * 512], in0=ps[:, b, :], scalar=csc2,
                        in1=resid[:, t, half * 512:(half + 1) * 512], op0=ALU.mult, op1=ALU.add),
                        [BK(b), ("resid", t)], [("resid", t)])
            for t in range(ntl):
                layernorm(t, 1, EPS_A, 1, False, None, nrows)
            if STOP <= 9:
                return True
            ffn("gu2", "dn2", 2, 1, 0, ntl, True, [tl["ydst"] for tl in tiles], nrows)

        setup()
        for b in range(NSP):
            for g in range(NG):
                tiles = []
                for t in range(NT):
                    Tg = g * NT + t
                    slots = [((Tg - 4 + kt) % RING) if (Tg - 4 + kt) >= 0 else None for kt in range(5)]
                    kvo = None
                    if Tg >= NTS - KEEP_T:
                        kvo = (Tg - (NTS - KEEP_T)) * 128
                    tiles.append(dict(xsrc=x_p[b, Tg * 128:(Tg + 1) * 128, :], nrows=128,
                                      ydst=y_p[b, Tg * 128:(Tg + 1) * 128, :], kind="p", b=b, Tg=Tg,
                                      slots=slots, own_slot=Tg % RING, first=(Tg == 0),
                                      conv_out=(Tg == NTS - 1), kv_out=kvo, nchunks=2, last=(Tg == NTS - 1)))
                if group(tiles):
                    break
            if STOP < 99:
                break
        for b in range(NSS if STOP >= 99 else 0):
            for kt in range(4):
                dma("pool", hb[:], ck[b, kt * 128:(kt + 1) * 128, :], [], ["hb"])
                bb = tbank()
                transposes(bb, lambda i: hb[:, i * 128:(i + 1) * 128], 8, ["hb"])
                copy_op(evac_eng(), kring[:, :, kt, :], psb[:, bb, :].rearrange("p (k n) -> p k n", n=128),
                        [BK(bb)], [("kring", kt)])
                dma("pool", vring[:, kt, :], cv[b, kt * 128:(kt + 1) * 128, :], [], [("vring", kt)])
            group([dict(xsrc=x_s[b], nrows=64, ydst=y_s[b], kind="s", b=b, Tg=None, slots=[0, 1, 2, 3, 4], own_slot=4,
                        first=True, conv_out=True, kv_out=0, nchunks=1, last=True)])

        P.assign(sems, dsems)
        block = es.enter_context(nc.Block())

        def fin_waits(e, eng):
            last = {}
            for op in P.ops[eng]:
                if op.dma:
                    last[id(op.sem)] = op
            for op in last.values():
                e.wait_ge(op.sem, op.val)

        @block.sync
        def _(e):
            P.emit("sp", e)
            fin_waits(e, "sp")
            fin_waits(e, "pool")

        @block.gpsimd
        def _(e):
            P.emit("pool", e)

        @block.vector
        def _(e):
            P.emit("dve", e)

        @block.scalar
        def _(e):
            P.emit("act", e)

        @block.tensor
        def _(e):
            P.emit("pe", e)
    return nc


def _consts():
    ident = np.eye(128, dtype=np.float32)
    s = np.arange(128)[:, None]
    t = np.arange(128)[None, :]
    tri = ((s // 64 == t // 64) & (s <= t)).astype(np.float32)
    q = np.arange(128)[:, None]
    jhi = 384 + np.arange(256)[None, :]
    mhi = np.where((q < 64) & (jhi >= 576), NEG, 0.0).astype(np.float32)
    jlo = np.arange(128)[None, :]
    mb0 = np.where((q >= 64) & (jlo < 64), NEG, 0.0).astype(np.float32)
    i4 = np.eye(4, dtype=np.float32)
    rel = np.clip(q + 512 - jhi, -128, 128) + 128
    return ident, tri, mhi, mb0, i4, rel


_CACHE = {}


def kernel(x_prompt, x_sample, state_ml_conv, state_ml_C, state_ml_n, state_ml_m, cache_att_k, cache_att_v,
           w_in, b_ml_i, b_ml_f, ml_conv_w, ml_conv_b, ml_norm_g, att_rel_bias, w_ml_proj, w_att_proj, w_out,
           ffn1_w_gu, ffn1_w_down, ffn2_w_gu, ffn2_w_down, ln1_g, ln1_b, ln2_g, ln2_b, ln3_g, ln3_b):
    NC = 8
    f = lambda a: np.ascontiguousarray(np.asarray(a, dtype=np.float32))
    B, SEQ, _ = x_prompt.shape
    BS = x_sample.shape[0]
    NSP, NSS = B // NC, BS // NC
    key = (NSP, SEQ, NSS)
    if key not in _CACHE:
        _CACHE[key] = build(NSP, SEQ, NSS, STOP=float(os.environ.get('KSTOP', '99')))
    nc = _CACHE[key]
    ident, tri, mhi, mb0, i4, rel = _consts()
    table = f(att_rel_bias)[0]
    btg = np.ascontiguousarray(table[:, rel].transpose(1, 0, 2))
    btc = np.ascontiguousarray(np.broadcast_to(table[:, 256][None, :], (128, AH)))
    shared = {
        "w_gu1": f(ffn1_w_gu)[0], "w_dn1": f(ffn1_w_down)[0], "w_win": f(w_in)[0], "w_wml": f(w_ml_proj)[0],
        "w_wat": f(w_att_proj)[0], "w_wout": f(w_out)[0], "w_gu2": f(ffn2_w_gu)[0], "w_dn2": f(ffn2_w_down)[0],
        "lnp": np.ascontiguousarray(np.broadcast_to(
            np.stack([f(ln1_g)[0], f(ln1_b)[0], f(ln2_g)[0], f(ln2_b)[0], f(ln3_g)[0], f(ln3_b)[0]])[None], (128, 6, D))),
        "gbias": np.ascontiguousarray(np.broadcast_to(np.concatenate([f(b_ml_i)[0], f(b_ml_f)[0]])[None], (128, 8))),
        "conv_w": f(ml_conv_w)[0], "conv_b": f(ml_conv_b)[0], "norm_g": f(ml_norm_g)[0],
        "btg": btg, "btc": btc, "c_ident": ident, "c_tri": tri, "c_mhi": mhi, "c_mb0": mb0, "c_i4": i4,
    }
    xp, xs = f(x_prompt), f(x_sample)
    sc, sC, sn, smm = f(state_ml_conv)[0], f(state_ml_C)[0], f(state_ml_n)[0], f(state_ml_m)[0]
    kk = f(cache_att_k)[0].reshape(BS, 512, D)
    vv = f(cache_att_v)[0].reshape(BS, 512, D)
    if os.environ.get("KNOW", "") == "1":
        shared = {k: v for k, v in shared.items() if not k.startswith("w_")}
    in_maps = []
    for c in range(NC):
        m = dict(shared)
        ps_, ss_ = slice(c * NSP, (c + 1) * NSP), slice(c * NSS, (c + 1) * NSS)
        m.update({"x_p": xp[ps_], "x_s": xs[ss_], "st_conv": sc[ss_], "st_C": sC[ss_], "st_n": sn[ss_],
                  "st_m": smm[ss_], "ck": kk[ss_], "cv": vv[ss_]})
        in_maps.append(m)
    res = run_bass_kernel_spmd(nc, in_maps, core_ids=list(range(NC)))
    R = res.results

    def cat(name):
        return np.concatenate([np.asarray(R[c][name], dtype=np.float32) for c in range(NC)], axis=0)
    keep = cat("p_k").shape[1]
    outs = (
        cat("y_p"), cat("y_s"),
        cat("p_conv")[None], cat("s_conv")[None],
        cat("p_C")[None], cat("s_C")[None],
        cat("p_n")[None], cat("s_n")[None],
        cat("p_m")[None], cat("s_m")[None],
        cat("p_k").reshape(1, B, keep, AH, 64), cat("s_k").reshape(1, BS, 64, AH, 64),
        cat("p_v").reshape(1, B, keep, AH, 64), cat("s_v").reshape(1, BS, 64, AH, 64),
    )
    return outs
```

```python
import os
import numpy as np
from contextlib import ExitStack
import concourse.bass as bass
import concourse.mybir as mybir
from concourse.bass_utils import run_bass_kernel_spmd

F32 = mybir.dt.float32
BF16 = mybir.dt.bfloat16
AF = mybir.ActivationFunctionType
ALU = mybir.AluOpType
AX = mybir.AxisListType

D = 1024
KD = 8
FF = 2816
KF = 22
INW = 9224
NH = 4
AH = 16
ALPHA = 2.0 ** 0.25
EPS = 1e-5
EPS_A = EPS / (ALPHA * ALPHA)
NEG = -30000.0
RING = 6
NDSEM = {"sp": 8, "pool": 3}


class Tok:
    __slots__ = ("w", "r")

    def __init__(self):
        self.w = None
        self.r = []


class Op:
    __slots__ = ("eng", "fn", "deps", "sig", "sem", "val", "dma")

    def __init__(self, eng, fn, dma):
        self.eng, self.fn, self.dma = eng, fn, dma
        self.deps = []
        self.sig = False
        self.sem = None
        self.val = 0


class Prog:
    ENG = ("pe", "act", "dve", "pool", "sp")

    def __init__(self):
        self.ops = {e: [] for e in self.ENG}
        self.toks = {}
        self.order = []

    def tk(self, *key):
        t = self.toks.get(key)
        if t is None:
            t = self.toks[key] = Tok()
        return t

    def add(self, eng, fn, reads=(), writes=(), dma=False):
        op = Op(eng, fn, dma)
        excl = [k for k in reads if isinstance(k, tuple) and k[0] == "bank"]
        reads_k = [k for k in reads if not (isinstance(k, tuple) and k[0] == "bank")]
        writes_k = list(writes) + [k for k in excl if k not in writes]
        bank_raw = [self.tk(*k) for k in excl]
        reads = [self.tk(*k) if isinstance(k, tuple) else self.tk(k) for k in reads_k]
        writes = [self.tk(*k) if isinstance(k, tuple) else self.tk(k) for k in writes_k]
        raw = set()
        deps = []
        for t in reads:
            if t.w is not None:
                deps.append(t.w)
                raw.add(id(t.w))
        for t in bank_raw:
            if t.w is not None:
                raw.add(id(t.w))
        for t in writes:
            if t.w is not None:
                deps.append(t.w)
            deps.extend(t.r)
        seen = set()
        for d in deps:
            if id(d) in seen or d is op:
                continue
            seen.add(id(d))
            if (not d.dma) and (not dma) and d.eng == eng:
                if eng == "pe" or id(d) not in raw:
                    continue
            op.deps.append(d)
            d.sig = True
        for t in writes:
            t.w = op
            t.r = []
        for t in reads:
            if t.w is not op:
                t.r.append(op)
        self.ops[eng].append(op)
        self.order.append(op)
        return op

    def assign(self, sems, dsems):
        cnt = {e: 0 for e in self.ENG}
        dcnt = {e: [0] * len(dsems.get(e, [])) for e in self.ENG}
        dlast = {e: [None] * len(dsems.get(e, [])) for e in self.ENG}
        dnext = {e: 0 for e in self.ENG}
        for op in self.order:
            if op.dma:
                k = dnext[op.eng]
                dnext[op.eng] = (k + 1) % len(dsems[op.eng])
                if dlast[op.eng][k] is not None:
                    op.deps.append(dlast[op.eng][k])
                dcnt[op.eng][k] += 16
                op.sem, op.val = dsems[op.eng][k], dcnt[op.eng][k]
                dlast[op.eng][k] = op
                op.sig = True
            elif op.sig:
                cnt[op.eng] += 1
                op.sem, op.val = sems[op.eng], cnt[op.eng]

    def emit(self, eng, e):
        waited = {}
        for op in self.ops[eng]:
            for d in op.deps:
                key = id(d.sem)
                if waited.get(key, 0) < d.val:
                    e.wait_ge(d.sem, d.val)
                    waited[key] = d.val
            ins = op.fn(e)
            if op.sig:
                ins.then_inc(op.sem, 16 if op.dma else 1)


def build(NSP, SEQ, NSS, STOP=99):
    NT = 2
    G = NT * 128
    NG = SEQ // G
    NTS = SEQ // 128
    KEEP_T = min(4, NTS)
    nc = bass.Bass("TRN2", target_bir_lowering=False)
    P = Prog()

    def din(name, shape, dt=F32):
        return nc.dram_tensor(name, list(shape), dt, kind="ExternalInput").ap()

    def dout(name, shape):
        return nc.dram_tensor(name, list(shape), F32, kind="ExternalOutput").ap()

    def dscr(name, shape, dt=BF16):
        return nc.dram_tensor(name, list(shape), dt, kind="Internal").ap()

    x_p = din("x_p", [NSP, SEQ, D])
    x_s = din("x_s", [NSS, 64, D])
    st_conv = din("st_conv", [NSS, 3, 2048])
    st_C = din("st_C", [NSS, NH, 256, 256])
    st_n = din("st_n", [NSS, NH, 256])
    st_m = din("st_m", [NSS, NH])
    ck = din("ck", [NSS, 512, D])
    cv = din("cv", [NSS, 512, D])
    wnames = {"gu1": (D, 2 * FF), "dn1": (FF, D), "win": (D, INW), "wml": (D, D), "wat": (D, D),
              "wout": (D, D), "gu2": (D, 2 * FF), "dn2": (FF, D)}
    NOW = os.environ.get("KNOW", "") == "1"
    wf = {k: (dscr("w_" + k, v, F32) if NOW else din("w_" + k, v)) for k, v in wnames.items()}
    wb = {k: dscr("wb_" + k, v) for k, v in wnames.items()}
    lnp = din("lnp", [128, 6, D])
    gbias = din("gbias", [128, 8])
    conv_w = din("conv_w", [4, 2048])
    conv_b = din("conv_b", [2048])
    norm_g = din("norm_g", [D])
    btg = din("btg", [128, AH, 256])
    btc = din("btc", [128, AH])
    c_ident = din("c_ident", [128, 128])
    c_tri = din("c_tri", [128, 128])
    c_mhi = din("c_mhi", [128, 256])
    c_mb0 = din("c_mb0", [128, 128])
    c_i4 = din("c_i4", [4, 4])

    y_p = dout("y_p", [NSP, SEQ, D])
    y_s = dout("y_s", [NSS, 64, D])
    o_conv = {"p": dout("p_conv", [NSP, 3, 2048]), "s": dout("s_conv", [NSS, 3, 2048])}
    o_C = {"p": dout("p_C", [NSP, NH, 256, 256]), "s": dout("s_C", [NSS, NH, 256, 256])}
    o_n = {"p": dout("p_n", [NSP, NH, 256]), "s": dout("s_n", [NSS, NH, 256])}
    o_m = {"p": dout("p_m", [NSP, NH]), "s": dout("s_m", [NSS, NH])}
    o_k = {"p": dout("p_k", [NSP, KEEP_T * 128, D]), "s": dout("s_k", [NSS, 64, D])}
    o_v = {"p": dout("p_v", [NSP, KEEP_T * 128, D]), "s": dout("s_v", [NSS, 64, D])}

    es = ExitStack()
    with es:
        def sb(name, shape, dt=F32):
            return es.enter_context(nc.sbuf_tensor(name, list(shape), dt))

        resid = sb("resid", [128, NT, D])
        curT = [sb("curTA", [128, KD, G], BF16), sb("curTB", [128, KD, G], BF16)]
        actT = sb("actT", [128, KF, G], BF16)
        qk_raw = sb("qk_raw", [128, 16, 3 + G], BF16)
        qkc = sb("qkc", [128, 16, G], BF16)
        v_tm = sb("v_tm", [128, NT, D], BF16)
        sigo = sb("sigo", [128, KD, G], BF16)
        aqT = sb("aqT", [128, KD, G], BF16)
        kring = sb("kring", [128, KD, RING, 128], BF16)
        vring = sb("vring", [128, RING, D], BF16)
        wslab = [sb("wslab%d" % i, [128, 4096], BF16) for i in range(3)]
        btm = sb("btm", [128, AH, 256], BF16)
        mb0 = sb("mb0", [128, 128], BF16)
        Cst = sb("Cst", [128, 8, 256])
        nst = sb("nst", [128, 8])
        Cs = [sb("CsA", [128, 8, 256], BF16), sb("CsB", [128, 8, 256], BF16)]
        ns = [sb("nsA", [128, 8], BF16), sb("nsB", [128, 8], BF16)]
        lnt = sb("lnt", [128, 6, D])
        ident = sb("ident", [128, 128], BF16)
        identf = sb("identf", [128, 128])
        tri = sb("tri", [128, 128])
        tri64 = sb("tri64", [128, 128])
        onesf = sb("onesf", [128, 128])
        onesb = sb("onesb", [128, 2], BF16)
        i4 = sb("i4", [4, 4])
        mhalf = sb("mhalf", [128, 4])
        dummy = sb("dummyb", [4, 32])
        vecs = sb("vecs", [128, 128])
        vT = sb("vT", [128, 88])
        nT = sb("nT", [8, 128])
        cw = vT[:, 0:64].rearrange("p (j c) -> p j c", c=16)
        cb = vT[:, 64:80]
        ngh = vT[:, 80:88]
        gb = sb("gb", [128, 8])
        th = [sb("th%d" % i, [128, 512]) for i in range(2)]
        uu = [sb("uu%d" % i, [128, 512]) for i in range(2)]
        xn = sb("xn", [128, D])
        hb = sb("hb", [128, D], BF16)
        stg = [sb("stg%d" % i, [128, D]) for i in range(2)]
        cstage = sb("cstage", [128, 2048])
        cacc = [sb("cacc%d" % i, [128, G]) for i in range(2)]
        cth = [sb("cth%d" % i, [128, G]) for i in range(2)]
        pexp = [sb("pexp%d" % i, [128, 640], BF16) for i in range(2)]
        pts = [sb("pts%d" % i, [128, 5, 128], BF16) for i in range(2)]
        o_tm = sb("o_tm", [128, D], BF16)
        h_tm = sb("h_tm", [128, D], BF16)
        ks_tm = sb("ks_tm", [128, 8, 128], BF16)
        ptm = sb("ptm", [128, 4, 128], BF16)
        qz = [sb("qzA", [128, 8, 128], BF16), sb("qzB", [128, 8, 128], BF16)]
        zg = sb("zg", [128, NT, 8])
        spt = sb("spt", [128, NT, 4])
        nbc = sb("nbc", [128, NT, 4])
        agt = sb("agt", [128, NT, 4])
        e8 = sb("e8", [128, 8])
        ucl = sb("ucl", [128, 8])
        u32 = sb("u32", [128, 4])
        mcol = sb("mcol", [128, 4])
        NCH = 2 * NT
        amaxT = sb("amaxT", [4, NCH])
        blT = sb("blT", [4, NCH])
        MD = sb("MD", [4, 2, NCH])
        dx = sb("dx", [4, 4, 2, NCH])
        mT = sb("mT", [4, 1])
        Mb = sb("Mb", [128, 4, NCH])
        rho = sb("rho", [128, 4, NCH])
        rhoh = sb("rhoh", [128, 4, NCH])
        dens = sb("dens", [128, 4])
        sm = {n: sb("sm_" + n, [128, 4]) for n in ("dabs", "dn", "rden", "t1", "rs", "nb2")}
        bst = sb("bst", [128, 4, 6])
        mv = sb("mv", [128, 4, 2])
        lst = sb("lst", [128, 2, 6])
        lmv = sb("lmv", [128, 2])
        lsm = {n: sb("lsm_" + n, [128, 1]) for n in ("ve", "rstd", "nmr")}
        nmax = sb("nmax", [128, AH])
        rsum = sb("rsum", [128, AH])
        rinv = sb("rinv", [128, AH])
        tmp_c = sb("tmp_c", [128, AH])

        NBK = 6
        ps = es.enter_context(nc.psum_tensor("ps", [128, NBK, 512], F32))
        pt = es.enter_context(nc.psum_tensor("pt", [128, 2, 1024], BF16))

        class _PB:
            def __getitem__(self, k):
                return pt[k[0], k[1] - 100, k[2]]
        psb = _PB()

        dsems = {e: [es.enter_context(nc.semaphore("d_%s%d" % (e, i))) for i in range(n)]
                 for e, n in NDSEM.items()}
        sems = {e: es.enter_context(nc.semaphore("s_" + e)) for e in ("pe", "act", "dve", "pool")}

        bank_ptr = [0]

        pinned = set()
        tb_ptr = [0]

        def bank(n=1):
            b = bank_ptr[0]
            for _ in range(2 * NBK):
                if n == 2 and b % 2:
                    b = (b + 1) % NBK
                if all(((b + i) % NBK) not in pinned for i in range(n)) and b + n <= NBK:
                    break
                b = (b + 1) % NBK
            else:
                raise RuntimeError("no free psum bank")
            bank_ptr[0] = (b + n) % NBK
            return b

        def tbank():
            tb_ptr[0] ^= 1
            return 100 + tb_ptr[0]

        def BK(b):
            return ("bank", b)

        flip = [0]

        def evac_eng():
            flip[0] ^= 1
            return "act" if flip[0] else "dve"

        def copy_op(eng, out, in_, reads, writes, scale=None):
            if eng == "act":
                if scale is None:
                    P.add("act", lambda e: e.activation(out=out, in_=in_, func=AF.Copy), reads, writes)
                else:
                    P.add("act", lambda e: e.activation(out=out, in_=in_, func=AF.Copy, scale=scale), reads, writes)
            else:
                if scale is None:
                    P.add(eng, lambda e: e.tensor_copy(out=out, in_=in_), reads, writes)
                else:
                    P.add(eng, lambda e: e.tensor_scalar(out=out, in0=in_, scalar1=scale, scalar2=None,
                                                         op0=ALU.mult), reads, writes)

        def dma(eng, out, in_, reads, writes, slow=False):
            if slow:
                P.add(eng, lambda e: e.dma_start(out=out, in_=in_, allow_slow_non_contiguous=True),
                      reads, writes, dma=True)
            else:
                P.add(eng, lambda e: e.dma_start(out=out, in_=in_), reads, writes, dma=True)

        slab_i = [0]

        def load_slab(pieces):
            i = slab_i[0] % 3
            slab_i[0] += 1
            off = 0
            views = []
            for (wk, k0, nk, c0, ncol) in pieces:
                v = wslab[i][:, off:off + nk * ncol].rearrange("p (k n) -> p k n", n=ncol)
                src = wb[wk].rearrange("(k p) n -> p k n", p=128)[:, k0:k0 + nk, c0:c0 + ncol]
                dma("sp", v, src, [("wb", wk)], [("wslab", i)])
                views.append(v)
                off += nk * ncol
            assert off <= 4096
            return views, ("wslab", i)

        def mm_group(out, pairs, reads, writes, first=True, last=True):
            def fn(e):
                n = len(pairs)
                ins = None
                for i, (l, r) in enumerate(pairs):
                    ins = e.matmul(out, lhsT=l, rhs=r, start=(first and i == 0), stop=(last and i == n - 1))
                return ins
            P.add("pe", fn, reads, writes)

        def transposes(b, src_fn, n, reads, dtype_bf=True, cols=128):
            def fn(e):
                ins = None
                for i in range(n):
                    ins = e.transpose(out=psb[:, b, i * 128:(i + 1) * 128], in_=src_fn(i), identity=ident[:])
                return ins
            P.add("pe", fn, reads + ["ident"], [BK(b)])

        def setup():
            fst = [(cstage[:], ["cstage"]), (resid[:].rearrange("p a n -> p (a n)"), [("resid", 0), ("resid", 1)])]
            bst_ = [(actT[:, 0:8, :].rearrange("p a n -> p (a n)"), [("actT", j) for j in range(8)]),
                    (actT[:, 8:16, :].rearrange("p a n -> p (a n)"), [("actT", j) for j in range(8, 16)])]
            ci_ = 0
            SKIP = os.environ.get("KSKIP", "").split(",")
            for wk in ("gu1", "dn1", "win", "wml", "wat", "wout", "gu2", "dn2"):
                if "W" in SKIP:
                    break
                rows, ncol = wnames[wk]
                parts = []
                for r0 in range(0, rows, 128):
                    for c0 in range(0, ncol, 2048):
                        w_ = min(2048, ncol - c0)
                        fa, ft = fst[ci_ % 2]
                        ba, bt = bst_[ci_ % 2]
                        dma("sp", fa[:, 0:w_], wf[wk][r0:r0 + 128, c0:c0 + w_], [], ft)
                        copy_op(("act", "dve", "pool")[ci_ % 3], ba[:, 0:w_], fa[:, 0:w_], ft, bt)
                        tk_ = ("wbp", wk, len(parts))
                        dma("sp", wb[wk][r0:r0 + 128, c0:c0 + w_], ba[:, 0:w_], bt, [tk_])
                        parts.append(tk_)
                        ci_ += 1
                wi_ = list(wnames).index(wk)
                dma("sp", dummy[:, wi_ * 4:(wi_ + 1) * 4], c_i4[:, :], parts, [("wb", wk)])
            dma("sp", identf[:], c_ident[:, :], [], ["identf"])
            dma("sp", tri[:], c_tri[:, :], [], ["tri"])
            P.add("dve", lambda e: e.tensor_copy(out=i4[:], in_=identf[0:4, 0:4]), ["identf"], ["i4"])
            if "B" not in SKIP:
                dma("sp", lnt[:], lnp[:, :, :], [], ["lnt"])
                dma("sp", gb[:], gbias[:, :], [], ["gb"])
            if "V" not in SKIP:
                dma("sp", vecs[0:64, :], conv_w.rearrange("j (c p) -> (j c) p", p=128), [], ["vecs"])
                dma("sp", vecs[64:80, :], conv_b.rearrange("(c p) -> c p", p=128), [], ["vecs"])
                dma("sp", vecs[80:88, :], norm_g.rearrange("(c p) -> c p", p=128), [], ["vecs"])
                bv = bank()
                P.add("pe", lambda e: e.transpose(out=ps[:, bv, 0:88], in_=vecs[0:88, :], identity=identf[0:88, 0:88]),
                      ["vecs", "identf"], [BK(bv)])
                P.add("dve", lambda e: e.tensor_copy(out=vT[:], in_=ps[:, bv, 0:88]), [BK(bv)], ["cw", "cb", "ngh"])
                P.add("dve", lambda e: e.tensor_scalar(out=ngh, in0=ngh, scalar1=0.5, scalar2=None,
                                                       op0=ALU.mult), ["ngh"], ["ngh"])
            P.add("dve", lambda e: e.tensor_copy(out=ident[:], in_=identf[:]), ["identf"], ["ident"])
            P.add("dve", lambda e: e.tensor_scalar(out=tri64[:], in0=tri[:], scalar1=1.0 / 64.0, scalar2=None,
                                                   op0=ALU.mult), ["tri"], ["tri64"])
            if "M" not in SKIP:
                P.add("pool", lambda e: e.memset(onesf[:], 1.0), [], ["onesf"])
                P.add("pool", lambda e: e.memset(onesb[:], 1.0), [], ["onesb"])
                P.add("pool", lambda e: e.memset(mhalf[:], -0.5), [], ["mhalf"])
                P.add("pool", lambda e: e.memset(qz[0][:], 0.0), [], ["qz"])
                P.add("pool", lambda e: e.memset(qz[1][:], 0.0), [], ["qz"])
                P.add("pool", lambda e: e.memset(dx[:], 0.0), [], ["dx"])
                P.add("pool", lambda e: e.memset(MD[:], 0.0), [], ["MD"])
            if "T" not in SKIP:
                dma("sp", tmp_c[:], btc[:, :], [], ["tmp_c"])
                dma("sp", xn[:, 0:256], c_mhi[:, :], [], ["xn"])
                dma("sp", xn[:, 256:384], c_mb0[:, :], [], ["xn"])
                P.add("dve", lambda e: e.tensor_copy(out=mb0[:], in_=xn[:, 256:384]), ["xn"], ["mb0"])
            for hh in range(2):
                if "T" in SKIP:
                    break
                dma("sp", cstage[:].rearrange("p (a n) -> p a n", n=256), btg[:, hh * 8:(hh + 1) * 8, :], [], ["cstage"])
                for h8 in range(8):
                    h = hh * 8 + h8
                    P.add("dve", lambda e, h=h, h8=h8: e.scalar_tensor_tensor(
                        out=btm[:, h, :], in0=cstage[:, h8 * 256:(h8 + 1) * 256], scalar=tmp_c[:, h:h + 1], in1=xn[:, 0:256],
                        op0=ALU.subtract, op1=ALU.add), ["cstage", "tmp_c", "xn"], ["btm"])

        def layernorm(t, li, eps, cout, final, ydst, nrows):
            R = ("resid", t)
            def f_stats(e):
                e.bn_stats(out=lst[:, 0, :], in_=resid[:, t, 0:512])
                return e.bn_stats(out=lst[:, 1, :], in_=resid[:, t, 512:1024])
            P.add("dve", f_stats, [R], ["lst"])
            P.add("dve", lambda e: e.bn_aggr(out=lmv[:], in_=lst[:].rearrange("p a b -> p (a b)")), ["lst"], ["lmv"])
            P.add("dve", lambda e: e.tensor_scalar(out=lsm["ve"][:], in0=lmv[:, 1:2], scalar1=eps, scalar2=None,
                                                   op0=ALU.add), ["lmv"], ["l_ve"])
            P.add("act", lambda e: e.activation(out=lsm["ve"][:], in_=lsm["ve"][:], func=AF.Ln), ["l_ve"], ["l_ve"])
            P.add("act", lambda e: e.activation(out=lsm["rstd"][:], in_=lsm["ve"][:], func=AF.Exp, scale=-0.5),
                  ["l_ve"], ["l_rstd"])
            P.add("dve", lambda e: e.scalar_tensor_tensor(out=lsm["nmr"][:], in0=lmv[:, 0:1], scalar=-1.0,
                                                          in1=lsm["rstd"][:], op0=ALU.mult, op1=ALU.mult),
                  ["lmv", "l_rstd"], ["l_nmr"])
            P.add("act", lambda e: e.activation(out=xn[:], in_=resid[:, t, :], func=AF.Identity,
                                                bias=lsm["nmr"][:], scale=lsm["rstd"][:]),
                  [R, "l_rstd", "l_nmr"], ["xn"])
            P.add("pool", lambda e: e.tensor_tensor(out=xn[:], in0=xn[:], in1=lnt[:, 2 * li, :], op=ALU.mult),
                  ["xn", "lnt"], ["xn"])
            P.add("pool", lambda e: e.tensor_tensor(out=resid[:, t, :], in0=xn[:], in1=lnt[:, 2 * li + 1, :],
                                                    op=ALU.add), ["xn", "lnt"], [R])
            if final:
                dma("sp", ydst, resid[0:nrows, t, :], [R], [("yout", t)])
            else:
                to_fm(resid[:, t, :], [R], cout, t)

        def to_fm(src, reads, cout, t):
            copy_op(evac_eng(), hb[:], src, reads, ["hb"])
            b = tbank()
            transposes(b, lambda i: hb[:, i * 128:(i + 1) * 128], 8, ["hb"])
            copy_op(evac_eng(), curT[cout][:, :, t * 128:(t + 1) * 128],
                    psb[:, b, :].rearrange("p (k n) -> p k n", n=128), [BK(b)], [("curT", cout, t)])

        def ffn(gk, dk, li, cin, cout, ntl, final, ydsts, nrows):
            Gt = ntl * 128
            cin_r = [("curT", cin, t) for t in range(ntl)]
            for s0 in range(0, KF, 2):
                views, wt = load_slab([(gk, 0, 8, s0 * 128, 256), (gk, 0, 8, FF + s0 * 128, 256)])
                for jj in range(2):
                    j = s0 + jj
                    b = bank()
                    mm_group(ps[:, b, 0:Gt], [(views[0][:, k, jj * 128:(jj + 1) * 128], curT[cin][:, k, 0:Gt])
                                              for k in range(8)], cin_r + [wt], [BK(b)])
                    mm_group(ps[:, b, 256:256 + Gt], [(views[1][:, k, jj * 128:(jj + 1) * 128], curT[cin][:, k, 0:Gt])
                                                      for k in range(8)], cin_r + [wt], [BK(b)])
                    i = j % 2
                    P.add("act", lambda e, b=b, i=i: e.activation(out=th[i][:, 0:Gt], in_=ps[:, b, 0:Gt],
                                                                  func=AF.Tanh, scale=0.5), [BK(b)], [("th", i)])
                    P.add("dve", lambda e, b=b, i=i: e.scalar_tensor_tensor(
                        out=uu[i][:, 0:Gt], in0=th[i][:, 0:Gt], scalar=1.0, in1=ps[:, b, 0:Gt],
                        op0=ALU.add, op1=ALU.mult), [("th", i), BK(b)], [("uu", i)])
                    P.add("dve", lambda e, b=b, i=i, j=j: e.tensor_tensor(
                        out=actT[:, j, 0:Gt], in0=uu[i][:, 0:Gt], in1=ps[:, b, 256:256 + Gt], op=ALU.mult),
                        [("uu", i), BK(b)], [("actT", j)])
            csc = 0.25 / ALPHA
            kparts = [(0, 8), (8, 7), (15, 7)]
            for half in range(2):
                banks = [bank() for _ in range(ntl)]
                for pi, (k0, nk) in enumerate(kparts):
                    views, wt = load_slab([(dk, k0, nk, half * 512, 512)])
                    for t in range(ntl):
                        mm_group(ps[:, banks[t], :],
                                 [(actT[:, k0 + kk, t * 128:(t + 1) * 128], views[0][:, kk, :]) for kk in range(nk)],
                                 [("actT", k0 + kk) for kk in range(nk)] + [wt], [BK(banks[t])],
                                 first=(pi == 0), last=(pi == len(kparts) - 1))
                for t in range(ntl):
                    P.add("dve", lambda e, t=t, bb=banks[t], half=half: e.scalar_tensor_tensor(
                        out=resid[:, t, half * 512:(half + 1) * 512], in0=ps[:, bb, :], scalar=csc,
                        in1=resid[:, t, half * 512:(half + 1) * 512], op0=ALU.mult, op1=ALU.add),
                        [BK(banks[t]), ("resid", t)], [("resid", t)])
            for t in range(ntl):
                layernorm(t, li, EPS_A, cout, final, ydsts[t] if final else None, nrows)

        def group(tiles):
            ntl = len(tiles)
            Gt = ntl * 128
            nrows = tiles[0]["nrows"]
            if STOP <= 1:
                return True
            for t, tl in enumerate(tiles):
                if nrows < 128:
                    P.add("pool", lambda e, t=t: e.memset(resid[64:128, t, :], 0.0), [], [("resid", t)])
                dma("sp", resid[0:nrows, t, :], tl["xsrc"], [], [("resid", t)])
                to_fm(resid[:, t, :], [("resid", t)], 0, t)
            if STOP <= 2:
                return True
            ffn("gu1", "dn1", 0, 0, 1, ntl, False, None, nrows)
            h1 = 1
            h1_r = [("curT", 1, t) for t in range(ntl)]
            if STOP <= 3:
                return True
            tl0 = tiles[0]
            if tl0["first"]:
                if tl0["kind"] == "p":
                    P.add("pool", lambda e: e.memset(qk_raw[:, :, 0:3], 0.0), [], [("qk_raw", c) for c in range(16)])
                else:
                    dma("sp", vecs[0:48, :], st_conv[tl0["b"]].rearrange("r (c p) -> (r c) p", p=128), [], ["vecs"])
                    dma("sp", vecs[48:56, :], st_n[tl0["b"]].rearrange("h (j p) -> (h j) p", p=128), [], ["vecs"])
                    bv = bank()
                    P.add("pe", lambda e, bv=bv: e.transpose(out=ps[:, bv, 0:56], in_=vecs[0:56, :],
                                                             identity=identf[0:56, 0:56]), ["vecs", "identf"], [BK(bv)])
                    P.add("dve", lambda e, bv=bv: e.tensor_copy(
                        out=qk_raw[:, :, 0:3], in_=ps[:, bv, 0:48].rearrange("p (r c) -> p c r", c=16)), [BK(bv)],
                        [("qk_raw", c) for c in range(16)])
                    P.add("dve", lambda e, bv=bv: e.tensor_copy(out=nst[:], in_=ps[:, bv, 48:56]), [BK(bv)], ["nst"])
            for s in range(8):
                views, wt = load_slab([("win", 0, 8, s * 256, 256)])
                for jj in range(2):
                    c = s * 2 + jj
                    b = bank()
                    mm_group(ps[:, b, 0:Gt], [(views[0][:, k, jj * 128:(jj + 1) * 128], curT[h1][:, k, 0:Gt])
                                              for k in range(8)], h1_r + [wt], [BK(b)])
                    copy_op(evac_eng(), qk_raw[:, c, 3:3 + Gt], ps[:, b, 0:Gt], [BK(b)], [("qk_raw", c)])
                for t, tl in enumerate(tiles):
                    if tl["conv_out"]:
                        b = bank()
                        mm_group(ps[:, b, 0:256], [(curT[h1][:, k, t * 128:(t + 1) * 128], views[0][:, k, :])
                                                   for k in range(8)], h1_r + [wt], [BK(b)])
                        copy_op(evac_eng(), cstage[:, s * 256:(s + 1) * 256], ps[:, b, 0:256], [BK(b)], ["cstage"])
            for t, tl in enumerate(tiles):
                if tl["conv_out"]:
                    r0 = tl["nrows"] - 3
                    dma("sp", o_conv[tl["kind"]][tl["b"]], cstage[r0:r0 + 3, :], ["cstage"], ["cstage_out"])
            conv_todo = list(range(16))

            def emit_conv(c):
                i = c % 2
                P.add("dve", lambda e: e.tensor_scalar(
                    out=cacc[i][:, 0:Gt], in0=qk_raw[:, c, 0:Gt], scalar1=cw[:, 0, c:c + 1], scalar2=cb[:, c:c + 1],
                    op0=ALU.mult, op1=ALU.add), [("qk_raw", c), "cw", "cb"], [("cacc", i)])
                for j in range(1, 4):
                    P.add("dve", lambda e, j=j: e.scalar_tensor_tensor(
                        out=cacc[i][:, 0:Gt], in0=qk_raw[:, c, j:j + Gt], scalar=cw[:, j, c:c + 1], in1=cacc[i][:, 0:Gt],
                        op0=ALU.mult, op1=ALU.add), [("qk_raw", c), "cw", ("cacc", i)], [("cacc", i)])
                P.add("act", lambda e: e.activation(out=cth[i][:, 0:Gt], in_=cacc[i][:, 0:Gt], func=AF.Tanh,
                                                    scale=0.5), [("cacc", i)], [("cth", i)])
                P.add("dve", lambda e: e.scalar_tensor_tensor(
                    out=qkc[:, c, 0:Gt], in0=cth[i][:, 0:Gt], scalar=1.0, in1=cacc[i][:, 0:Gt],
                    op0=ALU.add, op1=ALU.mult), [("cth", i), ("cacc", i)], [("qkc", c)])

            def conv_step():
                if conv_todo and STOP > 4:
                    emit_conv(conv_todo.pop(0))

            if STOP <= 3.1:
                return True
            for s in range(2):
                views, wt = load_slab([("win", 0, 8, 2048 + s * 512, 512)])
                for t in range(ntl):
                    b = bank()
                    mm_group(ps[:, b, :], [(curT[h1][:, k, t * 128:(t + 1) * 128], views[0][:, k, :]) for k in range(8)],
                             h1_r + [wt], [BK(b)])
                    copy_op(evac_eng(), v_tm[:, t, s * 512:(s + 1) * 512], ps[:, b, :], [BK(b)], [("v_tm", t)])
                conv_step()
            if STOP <= 3.2:
                return True
            for s in range(4):
                views, wt = load_slab([("win", 0, 8, 3072 + s * 256, 256)])
                for jj in range(2):
                    c = s * 2 + jj
                    b = bank()
                    mm_group(ps[:, b, 0:Gt], [(views[0][:, k, jj * 128:(jj + 1) * 128], curT[h1][:, k, 0:Gt])
                                              for k in range(8)], h1_r + [wt], [BK(b)])
                    i = c % 2
                    P.add("act", lambda e, b=b, i=i: e.activation(out=th[i][:, 0:Gt], in_=ps[:, b, 0:Gt],
                                                                  func=AF.Tanh, scale=0.5), [BK(b)], [("th", i)])
                    P.add("dve", lambda e, c=c, i=i: e.tensor_scalar(
                        out=sigo[:, c, 0:Gt], in0=th[i][:, 0:Gt], scalar1=ngh[:, c:c + 1], scalar2=ngh[:, c:c + 1],
                        op0=ALU.mult, op1=ALU.add), [("th", i), "ngh"], [("sigo", c)])
                conv_step()
            if STOP <= 3.3:
                return True
            views, wt = load_slab([("win", 0, 8, 4096, 8)])
            for t in range(ntl):
                b = bank()
                mm_group(ps[:, b, 0:8], [(curT[h1][:, k, t * 128:(t + 1) * 128], views[0][:, k, :]) for k in range(8)],
                         h1_r + [wt], [BK(b)])
                P.add("dve", lambda e, t=t, b=b: e.tensor_tensor(out=zg[:, t, :], in0=ps[:, b, 0:8], in1=gb[:],
                                                                  op=ALU.add), [BK(b), "gb"], [("zg", t)])
            if STOP <= 3.4:
                return True
            for s in range(0 if "Q" in os.environ.get("KSKIP", "") else 4):
                views, wt = load_slab([("win", 0, 8, 4104 + s * 256, 256)])
                for jj in range(2):
                    c = s * 2 + jj
                    b = bank()
                    mm_group(ps[:, b, 0:Gt], [(views[0][:, k, jj * 128:(jj + 1) * 128], curT[h1][:, k, 0:Gt])
                                              for k in range(8)], h1_r + [wt], [BK(b)])
                    copy_op(evac_eng(), aqT[:, c, 0:Gt], ps[:, b, 0:Gt], [BK(b)], [("aqT", c)], scale=0.125)
                conv_step()
            if STOP <= 3.5:
                return True
            for s in range(4):
                views, wt = load_slab([("win", 0, 8, (0 if "Z" in os.environ.get("KSKIP", "") else 5128) + s * 256, 256)])
                for jj in range(2):
                    c = s * 2 + jj
                    b = bank()
                    mm_group(ps[:, b, 0:Gt], [(views[0][:, k, jj * 128:(jj + 1) * 128], curT[h1][:, k, 0:Gt])
                                              for k in range(8)], h1_r + [wt], [BK(b)])
                    for t, tl in enumerate(tiles):
                        copy_op(evac_eng(), (sigo[:, c, t * 128:(t + 1) * 128] if "R" in os.environ.get("KSKIP", "")
                                             else kring[:, c, tl["own_slot"], :]), ps[:, b, t * 128:(t + 1) * 128],
                                [BK(b)], [("kring", tl["own_slot"])])
                for t, tl in enumerate(tiles):
                    if tl["kv_out"] is not None and "K" not in os.environ.get("KSKIP", ""):
                        b = bank()
                        mm_group(ps[:, b, 0:256], [(curT[h1][:, k, t * 128:(t + 1) * 128], views[0][:, k, :])
                                                   for k in range(8)], h1_r + [wt], [BK(b)])
                        copy_op(evac_eng(), stg[t][:, s * 256:(s + 1) * 256], ps[:, b, 0:256], [BK(b)], [("stg", t)])
                conv_step()
            for t, tl in enumerate(tiles):
                if tl["kv_out"] is not None and "D" not in os.environ.get("KSKIP", ""):
                    r0 = tl["kv_out"]
                    dma("sp", o_k[tl["kind"]][tl["b"], r0:r0 + tl["nrows"], :], stg[t][0:tl["nrows"], :],
                        [("stg", t)], [("stg_out", t)])
            if STOP <= 3.6:
                return True
            for s in range(2):
                views, wt = load_slab([("win", 0, 8, 6152 + s * 512, 512)])
                for t, tl in enumerate(tiles):
                    b = bank()
                    mm_group(ps[:, b, :], [(curT[h1][:, k, t * 128:(t + 1) * 128], views[0][:, k, :]) for k in range(8)],
                             h1_r + [wt], [BK(b)])
                    copy_op("act", vring[:, tl["own_slot"], s * 512:(s + 1) * 512], ps[:, b, :], [BK(b)],
                            [("vring", tl["own_slot"])])
                    if tl["kv_out"] is not None:
                        copy_op("dve", cstage[:, t * 1024 + s * 512:t * 1024 + (s + 1) * 512], ps[:, b, :], [BK(b)],
                                ["cstage"])
                conv_step()
            for t, tl in enumerate(tiles):
                if tl["kv_out"] is not None:
                    r0 = tl["kv_out"]
                    dma("sp", o_v[tl["kind"]][tl["b"], r0:r0 + tl["nrows"], :],
                        cstage[0:tl["nrows"], t * 1024:(t + 1) * 1024], ["cstage"], ["cstage_out"])

            if STOP <= 4:
                return True
            while conv_todo:
                emit_conv(conv_todo.pop(0))
            for c in []:
                i = c % 2
                P.add("dve", lambda e, c=c, i=i: e.tensor_scalar(
                    out=cacc[i][:, 0:Gt], in0=qk_raw[:, c, 0:Gt], scalar1=cw[:, 0, c:c + 1], scalar2=cb[:, c:c + 1],
                    op0=ALU.mult, op1=ALU.add), [("qk_raw", c), "cw", "cb"], [("cacc", i)])
                for j in range(1, 4):
                    P.add("dve", lambda e, c=c, i=i, j=j: e.scalar_tensor_tensor(
                        out=cacc[i][:, 0:Gt], in0=qk_raw[:, c, j:j + Gt], scalar=cw[:, j, c:c + 1], in1=cacc[i][:, 0:Gt],
                        op0=ALU.mult, op1=ALU.add), [("qk_raw", c), "cw", ("cacc", i)], [("cacc", i)])
                P.add("act", lambda e, i=i: e.activation(out=cth[i][:, 0:Gt], in_=cacc[i][:, 0:Gt], func=AF.Tanh,
                                                         scale=0.5), [("cacc", i)], [("cth", i)])
                P.add("dve", lambda e, c=c, i=i: e.scalar_tensor_tensor(
                    out=qkc[:, c, 0:Gt], in0=cth[i][:, 0:Gt], scalar=1.0, in1=cacc[i][:, 0:Gt],
                    op0=ALU.add, op1=ALU.mult), [("cth", i), ("cacc", i)], [("qkc", c)])
            if tiles[-1]["kind"] == "p" and not tiles[-1]["last"]:
                P.add("pool", lambda e: e.tensor_copy(out=qk_raw[:, :, 0:3], in_=qk_raw[:, :, Gt:Gt + 3]),
                      [("qk_raw", c) for c in range(16)], [("qk_raw", c) for c in range(16)])

            if STOP <= 5:
                return True
            nch = 2 * ntl
            zr = [("zg", t) for t in range(ntl)]
            P.add("act", lambda e: e.activation(out=spt[:, 0:ntl, :], in_=zg[:, 0:ntl, 4:8], func=AF.Exp, scale=-1.0),
                  zr, ["spt"])
            P.add("act", lambda e: e.activation(out=spt[:, 0:ntl, :], in_=spt[:, 0:ntl, :], func=AF.Ln, bias=1.0),
                  ["spt"], ["spt"])
            for t in range(ntl):
                b = bank()
                mm_group(ps[:, b, 0:4], [(tri[:], spt[:, t, :])], ["tri", "spt"], [BK(b)])
                P.add("dve", lambda e, t=t, b=b: e.tensor_copy(out=nbc[:, t, :], in_=ps[:, b, 0:4]), [BK(b)], [("nbc", t)])
                P.add("dve", lambda e, t=t: e.tensor_tensor(out=agt[:, t, :], in0=zg[:, t, 0:4], in1=nbc[:, t, :],
                                                            op=ALU.add), [("zg", t), ("nbc", t)], [("agt", t)])
                b2 = bank()

                def ftr(e, t=t, b2=b2):
                    e.transpose(out=ps[0:4, b2, 0:128], in_=agt[:, t, :], identity=identf[:])
                    return e.transpose(out=ps[0:4, b2, 128:256], in_=spt[:, t, :], identity=identf[:])
                P.add("pe", ftr, [("agt", t), "spt", "identf"], [BK(b2)])
                P.add("dve", lambda e, t=t, b2=b2: e.tensor_reduce(
                    out=amaxT[:, 2 * t:2 * t + 2], in_=ps[0:4, b2, 0:128].rearrange("p (c s) -> p c s", s=64),
                    axis=AX.X, op=ALU.max), [BK(b2)], ["amaxT"])
                P.add("dve", lambda e, t=t, b2=b2: e.tensor_reduce(
                    out=blT[:, 2 * t:2 * t + 2], in_=ps[0:4, b2, 128:256].rearrange("p (c s) -> p c s", s=64),
                    axis=AX.X, op=ALU.add, negate=True), [BK(b2)], ["blT"])
            for t, tl in enumerate(tiles):
                if tl["first"]:
                    if tl["kind"] == "p":
                        P.add("dve", lambda e: e.memset(mT[:], 0.0), [], ["mT"])
                    else:
                        dma("sp", mT[:], st_m[tl["b"]].rearrange("(h o) -> h o", o=1), [], ["mT"], slow=True)
                for ci in range(2):
                    c = 2 * t + ci
                    P.add("dve", lambda e, c=c: e.tensor_tensor(out=MD[:, 0, c:c + 1], in0=mT[:], in1=amaxT[:, c:c + 1],
                                                                op=ALU.max), ["mT", "amaxT"], ["MD"])
                    P.add("dve", lambda e, c=c: e.tensor_tensor(out=MD[:, 1, c:c + 1], in0=mT[:], in1=MD[:, 0, c:c + 1],
                                                                op=ALU.subtract), ["mT", "MD"], ["MD"])
                    if ci < tl["nchunks"]:
                        P.add("dve", lambda e, c=c: e.tensor_tensor(out=mT[:], in0=blT[:, c:c + 1], in1=MD[:, 0, c:c + 1],
                                                                    op=ALU.add), ["blT", "MD"], ["mT"])
            for hh in range(4):
                P.add("dve", lambda e, hh=hh: e.tensor_scalar(out=dx[:, hh, :, 0:nch], in0=MD[:, :, 0:nch],
                                                              scalar1=i4[:, hh:hh + 1], scalar2=None, op0=ALU.mult),
                      ["MD", "i4"], ["dx"])
            b = bank()
            mm_group(ps[:, b, 0:8 * NCH], [(onesf[0:4, :], dx[:].rearrange("p a b c -> p (a b c)"))],
                     ["onesf", "dx"], [BK(b)])
            pv = ps[:, b, 0:8 * NCH].rearrange("p (a b c) -> p a b c", a=4, b=2)
            P.add("dve", lambda e, pv=pv: e.tensor_copy(out=Mb[:], in_=pv[:, :, 0, :]), [BK(b)], ["Mb"])
            P.add("act", lambda e, pv=pv: e.activation(out=rho[:], in_=pv[:, :, 1, :], func=AF.Exp), [BK(b)], ["rho"])
            P.add("dve", lambda e: e.tensor_scalar(out=rhoh[:], in0=rho[:], scalar1=0.5, scalar2=None, op0=ALU.mult),
                  ["rho"], ["rhoh"])

            if STOP <= 6:
                return True
            for t, tl in enumerate(tiles):
                tc = slice(t * 128, (t + 1) * 128)
                for ci in range(2):
                    c = 2 * t + ci
                    P.add("dve", lambda e, ci=ci, c=c: e.tensor_copy(out=mcol[ci * 64:(ci + 1) * 64, :],
                                                                     in_=Mb[ci * 64:(ci + 1) * 64, :, c]), ["Mb"], ["mcol"])
                P.add("dve", lambda e, t=t: e.tensor_tensor(out=e8[:, 0:4], in0=agt[:, t, :], in1=mcol[:], op=ALU.subtract),
                      [("agt", t), "mcol"], ["e8"])
                P.add("dve", lambda e, t=t: e.tensor_tensor(out=e8[:, 4:8], in0=nbc[:, t, :], in1=mcol[:], op=ALU.subtract),
                      [("nbc", t), "mcol"], ["e8"])
                P.add("act", lambda e: e.activation(out=ucl[:], in_=e8[:], func=AF.Exp), ["e8"], ["ucl"])
                P.add("dve", lambda e: e.tensor_scalar(out=u32[:], in0=ucl[:, 0:4], scalar1=1.0 / 32.0, scalar2=None,
                                                       op0=ALU.mult), ["ucl"], ["u32"])
                bS = bank()
                for h in range(4):
                    mm_group(ps[:, bS, h * 128:(h + 1) * 128],
                             [(qkc[:, 8 + 2 * h + j, tc], qkc[:, 2 * h + j, tc]) for j in range(2)],
                             [("qkc", cc) for cc in (8 + 2 * h, 9 + 2 * h, 2 * h, 2 * h + 1)], [BK(bS)])
                for h in range(4):
                    P.add("dve", lambda e, h=h, bS=bS: e.scalar_tensor_tensor(
                        out=ptm[:, h, :], in0=ps[:, bS, h * 128:(h + 1) * 128], scalar=ucl[:, h:h + 1], in1=tri64[:],
                        op0=ALU.mult, op1=ALU.mult), [BK(bS), "ucl", "tri64"], ["ptm"])
                bK = tbank()
                transposes(bK, lambda i, tc=tc: qkc[:, 8 + i, tc], 8, [("qkc", 8 + i) for i in range(8)])
                for h in range(4):
                    P.add("act", lambda e, h=h, bK=bK: e.activation(
                        out=ks_tm[:, 2 * h:2 * h + 2, :],
                        in_=psb[:, bK, 2 * h * 128:(2 * h + 2) * 128].rearrange("p (a n) -> p a n", n=128),
                        func=AF.Copy, scale=u32[:, h:h + 1]), [BK(bK), "u32"], ["ks_tm"])
                P.add("pool", lambda e, t=t: e.tensor_copy(out=qz[0][:, :, 0:64], in_=qkc[:, 0:8, t * 128:t * 128 + 64]),
                      [("qkc", i) for i in range(8)], ["qz"])
                P.add("pool", lambda e, t=t: e.tensor_copy(out=qz[1][:, :, 64:128], in_=qkc[:, 0:8, t * 128 + 64:t * 128 + 128]),
                      [("qkc", i) for i in range(8)], ["qz"])
                if tl["first"]:
                    if tl["kind"] == "p":
                        P.add("pool", lambda e: e.memset(Cst[:], 0.0), [], ["Cst"])
                        P.add("pool", lambda e: e.memset(nst[:], 0.0), [], ["nst"])
                    else:
                        dma("sp", Cst[:], st_C[tl["b"]].rearrange("h (j p) e -> p (h j) e", p=128), [], ["Cst"])
                for ci in range(2):
                    c = 2 * t + ci
                    r = slice(ci * 64, (ci + 1) * 64)
                    for h in range(4):
                        P.add("act", lambda e, h=h, ci=ci, c=c: e.activation(
                            out=Cs[ci][:, 2 * h:2 * h + 2, :], in_=Cst[:, 2 * h:2 * h + 2, :], func=AF.Copy,
                            scale=rhoh[:, h, c:c + 1]), ["Cst", "rhoh"], [("Cs", ci)])
                        P.add("dve", lambda e, h=h, ci=ci, c=c: e.tensor_scalar(
                            out=ns[ci][:, 2 * h:2 * h + 2], in0=nst[:, 2 * h:2 * h + 2], scalar1=rhoh[:, h, c:c + 1],
                            scalar2=None, op0=ALU.mult), ["nst", "rhoh"], [("ns", ci)])
                    if ci < tl["nchunks"]:
                        for hp in range(2):
                            bb = [bank(), bank()]
                            for hi in range(2):
                                h = 2 * hp + hi
                                for j in range(2):
                                    mm_group(ps[:, bb[hi], j * 256:(j + 1) * 256],
                                             [(ks_tm[r, 2 * h + j, :], v_tm[r, t, h * 256:(h + 1) * 256])],
                                             ["ks_tm", ("v_tm", t)], [BK(bb[hi])])
                            for hi in range(2):
                                h = 2 * hp + hi
                                P.add("dve", lambda e, h=h, c=c, bq=bb[hi]: e.scalar_tensor_tensor(
                                    out=Cst[:, 2 * h:2 * h + 2, :], in0=Cst[:, 2 * h:2 * h + 2, :], scalar=rho[:, h, c:c + 1],
                                    in1=ps[:, bq, :].rearrange("p (a n) -> p a n", n=256), op0=ALU.mult, op1=ALU.add),
                                    ["Cst", "rho", BK(bb[hi])], ["Cst"])
                        bn_ = bank()
                        for h in range(4):
                            for j in range(2):
                                mm_group(ps[:, bn_, 2 * h + j:2 * h + j + 1], [(ks_tm[r, 2 * h + j, :], onesb[r, 0:1])],
                                         ["ks_tm", "onesb"], [BK(bn_)])
                        for h in range(4):
                            P.add("dve", lambda e, h=h, c=c, bn_=bn_: e.scalar_tensor_tensor(
                                out=nst[:, 2 * h:2 * h + 2], in0=nst[:, 2 * h:2 * h + 2], scalar=rho[:, h, c:c + 1],
                                in1=ps[:, bn_, 2 * h:2 * h + 2], op0=ALU.mult, op1=ALU.add),
                                ["nst", "rho", BK(bn_)], ["nst"])
                for h in range(4):
                    bo = bank()
                    pairs = [(qz[0][:, 2 * h + j, :], Cs[0][:, 2 * h + j, :]) for j in range(2)] + \
                            [(qz[1][:, 2 * h + j, :], Cs[1][:, 2 * h + j, :]) for j in range(2)] + \
                            [(ptm[:, h, :], v_tm[:, t, h * 256:(h + 1) * 256])]
                    mm_group(ps[:, bo, 0:256], pairs, ["qz", ("Cs", 0), ("Cs", 1), "ptm", ("v_tm", t)], [BK(bo)])
                    pairs2 = [(qz[0][:, 2 * h + j, :], ns[0][:, 2 * h + j:2 * h + j + 1]) for j in range(2)] + \
                             [(qz[1][:, 2 * h + j, :], ns[1][:, 2 * h + j:2 * h + j + 1]) for j in range(2)] + \
                             [(ptm[:, h, :], onesb[:, 0:1])]
                    mm_group(ps[:, bo, 256:257], pairs2, ["qz", ("ns", 0), ("ns", 1), "ptm", "onesb"], [BK(bo)])
                    P.add("dve", lambda e, h=h, bo=bo: e.tensor_copy(out=dens[:, h:h + 1], in_=ps[:, bo, 256:257]),
                          [BK(bo)], ["dens"])
                    P.add("dve", lambda e, h=h, bo=bo: e.bn_stats(out=bst[:, h, :], in_=ps[:, bo, 0:256]), [BK(bo)], ["bst"])
                    P.add("dve", lambda e, h=h: e.bn_aggr(out=mv[:, h, :], in_=bst[:, h, :]), ["bst"], ["mv"])
                    if h == 0:
                        pass
                    tl.setdefault("_bo", []).append(bo)
                S = sm
                P.add("dve", lambda e: e.tensor_tensor(out=S["dabs"][:], in0=dens[:], in1=ucl[:, 4:8], op=ALU.max),
                      ["dens", "ucl"], ["s_dabs"])
                P.add("dve", lambda e: e.scalar_tensor_tensor(out=S["dn"][:], in0=dens[:], scalar=-1.0, in1=S["dabs"][:],
                                                              op0=ALU.mult, op1=ALU.max), ["dens", "s_dabs"], ["s_dn"])
                P.add("dve", lambda e: e.reciprocal(out=S["rden"][:], in_=S["dn"][:]), ["s_dn"], ["s_rden"])
                P.add("dve", lambda e: e.tensor_tensor(out=S["t1"][:], in0=S["rden"][:], in1=S["rden"][:], op=ALU.mult),
                      ["s_rden"], ["s_t1"])
                P.add("dve", lambda e: e.tensor_tensor(out=S["t1"][:], in0=S["t1"][:], in1=mv[:, :, 1], op=ALU.mult),
                      ["s_t1", "mv"], ["s_t1"])
                P.add("dve", lambda e: e.tensor_scalar(out=S["t1"][:], in0=S["t1"][:], scalar1=EPS, scalar2=None,
                                                       op0=ALU.add), ["s_t1"], ["s_t1"])
                P.add("act", lambda e: e.activation(out=S["t1"][:], in_=S["t1"][:], func=AF.Ln), ["s_t1"], ["s_t1"])
                P.add("act", lambda e: e.activation(out=S["rs"][:], in_=S["t1"][:], func=AF.Exp, scale=-0.5),
                      ["s_t1"], ["s_rs"])
                P.add("dve", lambda e: e.tensor_tensor(out=S["rs"][:], in0=S["rs"][:], in1=S["rden"][:], op=ALU.mult),
                      ["s_rs", "s_rden"], ["s_rs"])
                P.add("dve", lambda e: e.scalar_tensor_tensor(out=S["nb2"][:], in0=mv[:, :, 0], scalar=-1.0, in1=S["rs"][:],
                                                              op0=ALU.mult, op1=ALU.mult), ["mv", "s_rs"], ["s_nb2"])
                for h in range(4):
                    bo = tl["_bo"][h]
                    P.add("act", lambda e, h=h, bo=bo: e.activation(
                        out=h_tm[:, h * 256:(h + 1) * 256], in_=ps[:, bo, 0:256], func=AF.Identity,
                        bias=S["nb2"][:, h:h + 1], scale=S["rs"][:, h:h + 1]), [BK(bo), "s_rs", "s_nb2"], ["h_tm"])
                bT = tbank()
                transposes(bT, lambda i: h_tm[:, i * 128:(i + 1) * 128], 8, ["h_tm"])
                P.add("dve", lambda e, bT=bT, tc=tc: e.tensor_tensor(
                    out=actT[:, 0:8, tc], in0=psb[:, bT, :].rearrange("p (k n) -> p k n", n=128), in1=sigo[:, :, tc],
                    op=ALU.mult), [BK(bT)] + [("sigo", c) for c in range(8)], [("actT", c) for c in range(8)])

                if STOP <= 7:
                    continue
                slots = tl["slots"]
                valid = [kt for kt in range(5) if slots[kt] is not None]
                j0 = valid[0] * 128
                bO = bank(2)
                pinned.update((bO, bO + 1))
                for h in range(AH):
                    hp = slice((h % 2) * 64, (h % 2) * 64 + 64)
                    c = h // 2
                    bS2 = bank(2)
                    S2 = ps[:, bS2:bS2 + 2, :].rearrange("p a n -> p (a n)")

                    def fS(e, h=h, hp=hp, c=c, S2=S2, tc=tc):
                        ins = None
                        for kt in valid:
                            extra = None
                            if kt == 0:
                                extra = mb0[:]
                            elif kt == 3:
                                extra = btm[:, h, 0:128]
                            elif kt == 4:
                                extra = btm[:, h, 128:256]
                            ins = e.matmul(S2[:, kt * 128:(kt + 1) * 128], lhsT=aqT[hp, c, tc],
                                           rhs=kring[hp, c, slots[kt], :], start=True, stop=(extra is None))
                            if extra is not None:
                                ins = e.matmul(S2[:, kt * 128:(kt + 1) * 128], lhsT=ident[:], rhs=extra,
                                               start=False, stop=True)
                        return ins
                    P.add("pe", fS, [("aqT", c), "ident", "mb0", "btm"] + [("kring", slots[kt]) for kt in valid],
                          [BK(bS2), BK(bS2 + 1)])
                    P.add("dve", lambda e, h=h, S2=S2: e.tensor_reduce(out=nmax[:, h:h + 1], in_=S2[:, j0:640], axis=AX.X,
                                                                       op=ALU.max, negate=True),
                          [BK(bS2), BK(bS2 + 1)], [("nmax", h)])
                    i = h % 2
                    P.add("act", lambda e, h=h, S2=S2, i=i: e.activation(
                        out=pexp[i][:, j0:640], in_=S2[:, j0:640], func=AF.Exp, bias=nmax[:, h:h + 1],
                        accum_out=rsum[:, h:h + 1]), [BK(bS2), BK(bS2 + 1), ("nmax", h)], [("pexp", i), ("rsum", h)])
                    bP = tbank()

                    def fT(e, i=i, bP=bP):
                        ins = None
                        for kt in valid:
                            ins = e.transpose(out=psb[:, bP, kt * 128:(kt + 1) * 128], in_=pexp[i][:, kt * 128:(kt + 1) * 128],
                                              identity=ident[:])
                        return ins
                    P.add("pe", fT, [("pexp", i), "ident"], [BK(bP)])
                    copy_op(evac_eng(), pts[i][:, valid[0]:5, :],
                            psb[:, bP, j0:640].rearrange("p (k n) -> p k n", n=128), [BK(bP)], [("pts", i)])
                    ob = bO + (h // 8)
                    oc = (h % 8) * 64
                    mm_group(ps[:, ob, oc:oc + 64],
                             [(pts[i][:, kt, :], vring[:, slots[kt], h * 64:(h + 1) * 64]) for kt in valid],
                             [("pts", i)] + [("vring", slots[kt]) for kt in valid], [BK(ob)])
                P.add("dve", lambda e: e.reciprocal(out=rinv[:], in_=rsum[:]), [("rsum", h) for h in range(AH)], ["rinv"])
                for h in range(AH):
                    ob = bO + (h // 8)
                    oc = (h % 8) * 64
                    copy_op(evac_eng(), o_tm[:, h * 64:(h + 1) * 64], ps[:, ob, oc:oc + 64], [BK(ob), "rinv"], ["o_tm"],
                            scale=rinv[:, h:h + 1])
                pinned.clear()
                bT2 = tbank()
                transposes(bT2, lambda i: o_tm[:, i * 128:(i + 1) * 128], 8, ["o_tm"])
                copy_op(evac_eng(), actT[:, 8:16, tc], psb[:, bT2, :].rearrange("p (k n) -> p k n", n=128),
                        [BK(bT2)], [("actT", c) for c in range(8, 16)])

                if tl["last"]:
                    kd, bi = tl["kind"], tl["b"]
                    dma("sp", o_C[kd][bi].rearrange("h (j p) e -> p (h j) e", p=128), Cst[:], ["Cst"], ["Cout"])
                    bn2 = bank()
                    P.add("pe", lambda e, bn2=bn2: e.transpose(out=ps[0:8, bn2, 0:128], in_=nst[:], identity=identf[:]),
                          ["nst", "identf"], [BK(bn2)])
                    P.add("dve", lambda e, bn2=bn2: e.tensor_copy(out=nT[:], in_=ps[0:8, bn2, 0:128]), [BK(bn2)], ["nT"])
                    dma("sp", o_n[kd][bi].rearrange("h (j p) -> (h j) p", p=128), nT[:], ["nT"], ["nout"])
                    dma("sp", o_m[kd][bi].rearrange("(h o) -> h o", o=1), mT[:], ["mT"], ["mout"], slow=True)

            if STOP <= 8:
                return True
            for q4 in range(4):
                vX, wtX = load_slab([("wml", 0, 8, q4 * 256, 256), ("wat", 0, 8, q4 * 256, 256)])
                vY, wtY = load_slab([("win", 0, 8, 7176 + q4 * 256, 256), ("win", 0, 8, 8200 + q4 * 256, 256)])
                for jj in range(2):
                    c = q4 * 2 + jj
                    b1 = bank()
                    b2 = bank()
                    cs = slice(jj * 128, (jj + 1) * 128)
                    mm_group(ps[:, b1, 0:Gt], [(vY[0][:, k, cs], curT[h1][:, k, 0:Gt]) for k in range(8)], h1_r + [wtY], [BK(b1)])
                    mm_group(ps[:, b1, 256:256 + Gt], [(vY[1][:, k, cs], curT[h1][:, k, 0:Gt]) for k in range(8)],
                             h1_r + [wtY], [BK(b1)])
                    mm_group(ps[:, b2, 0:Gt], [(vX[0][:, k, cs], actT[:, k, 0:Gt]) for k in range(8)],
                             [("actT", k) for k in range(8)] + [wtX], [BK(b2)])
                    mm_group(ps[:, b2, 256:256 + Gt], [(vX[1][:, k, cs], actT[:, 8 + k, 0:Gt]) for k in range(8)],
                             [("actT", 8 + k) for k in range(8)] + [wtX], [BK(b2)])
                    i = c % 2
                    P.add("act", lambda e, b1=b1, i=i: e.activation(out=th[i][:], in_=ps[:, b1, :], func=AF.Tanh, scale=0.5),
                          [BK(b1)], [("th", i)])
                    P.add("dve", lambda e, b2=b2, i=i: e.scalar_tensor_tensor(
                        out=uu[i][:], in0=th[i][:], scalar=1.0, in1=ps[:, b2, :], op0=ALU.add, op1=ALU.mult),
                        [("th", i), BK(b2)], [("uu", i)])
                    P.add("dve", lambda e, c=c, i=i: e.tensor_tensor(
                        out=curT[0][:, c, 0:Gt], in0=uu[i][:, 0:Gt], in1=uu[i][:, 256:256 + Gt], op=ALU.add),
                        [("uu", i)], [("curT", 0, t) for t in range(ntl)])
            m_r = [("curT", 0, t) for t in range(ntl)]
            csc2 = 0.5 / ALPHA
            for half in range(2):
                views, wt = load_slab([("wout", 0, 8, half * 512, 512)])
                for t in range(ntl):
                    b = bank()
                    mm_group(ps[:, b, :], [(curT[0][:, k, t * 128:(t + 1) * 128], views[0][:, k, :]) for k in range(8)],
                             m_r + [wt], [BK(b)])
                    P.add("dve", lambda e, t=t, b=b, half=half: e.scalar_tensor_tensor(
                        out=resid[:, t, half * 512:(half + 1) * 512], in0=ps[:, b, :], scalar=csc2,
                        in1=resid[:, t, half * 512:(half + 1) * 512], op0=ALU.mult, op1=ALU.add),
                        [BK(b), ("resid", t)], [("resid", t)])
            for t in range(ntl):
                layernorm(t, 1, EPS_A, 1, False, None, nrows)
            if STOP <= 9:
                return True
            ffn("gu2", "dn2", 2, 1, 0, ntl, True, [tl["ydst"] for tl in tiles], nrows)

        setup()
        for b in range(NSP):
            for g in range(NG):
                tiles = []
                for t in range(NT):
                    Tg = g * NT + t
                    slots = [((Tg - 4 + kt) % RING) if (Tg - 4 + kt) >= 0 else None for kt in range(5)]
                    kvo = None
                    if Tg >= NTS - KEEP_T:
                        kvo = (Tg - (NTS - KEEP_T)) * 128
                    tiles.append(dict(xsrc=x_p[b, Tg * 128:(Tg + 1) * 128, :], nrows=128,
                                      ydst=y_p[b, Tg * 128:(Tg + 1) * 128, :], kind="p", b=b, Tg=Tg,
                                      slots=slots, own_slot=Tg % RING, first=(Tg == 0),
                                      conv_out=(Tg == NTS - 1), kv_out=kvo, nchunks=2, last=(Tg == NTS - 1)))
                if group(tiles):
                    break
            if STOP < 99:
                break
        for b in range(NSS if STOP >= 99 else 0):
            for kt in range(4):
                dma("pool", hb[:], ck[b, kt * 128:(kt + 1) * 128, :], [], ["hb"])
                bb = tbank()
                transposes(bb, lambda i: hb[:, i * 128:(i + 1) * 128], 8, ["hb"])
                copy_op(evac_eng(), kring[:, :, kt, :], psb[:, bb, :].rearrange("p (k n) -> p k n", n=128),
                        [BK(bb)], [("kring", kt)])
                dma("pool", vring[:, kt, :], cv[b, kt * 128:(kt + 1) * 128, :], [], [("vring", kt)])
            group([dict(xsrc=x_s[b], nrows=64, ydst=y_s[b], kind="s", b=b, Tg=None, slots=[0, 1, 2, 3, 4], own_slot=4,
                        first=True, conv_out=True, kv_out=0, nchunks=1, last=True)])

        P.assign(sems, dsems)
        block = es.enter_context(nc.Block())

        def fin_waits(e, eng):
            last = {}
            for op in P.ops[eng]:
                if op.dma:
                    last[id(op.sem)] = op
            for op in last.values():
                e.wait_ge(op.sem, op.val)

        @block.sync
        def _(e):
            P.emit("sp", e)
            fin_waits(e, "sp")
            fin_waits(e, "pool")

        @block.gpsimd
        def _(e):
            P.emit("pool", e)

        @block.vector
        def _(e):
            P.emit("dve", e)

        @block.scalar
        def _(e):
            P.emit("act", e)

        @block.tensor
        def _(e):
            P.emit("pe", e)
    return nc


def _consts():
    ident = np.eye(128, dtype=np.float32)
    s = np.arange(128)[:, None]
    t = np.arange(128)[None, :]
    tri = ((s // 64 == t // 64) & (s <= t)).astype(np.float32)
    q = np.arange(128)[:, None]
    jhi = 384 + np.arange(256)[None, :]
    mhi = np.where((q < 64) & (jhi >= 576), NEG, 0.0).astype(np.float32)
    jlo = np.arange(128)[None, :]
    mb0 = np.where((q >= 64) & (jlo < 64), NEG, 0.0).astype(np.float32)
    i4 = np.eye(4, dtype=np.float32)
    rel = np.clip(q + 512 - jhi, -128, 128) + 128
    return ident, tri, mhi, mb0, i4, rel


_CACHE = {}


def kernel(x_prompt, x_sample, state_ml_conv, state_ml_C, state_ml_n, state_ml_m, cache_att_k, cache_att_v,
           w_in, b_ml_i, b_ml_f, ml_conv_w, ml_conv_b, ml_norm_g, att_rel_bias, w_ml_proj, w_att_proj, w_out,
           ffn1_w_gu, ffn1_w_down, ffn2_w_gu, ffn2_w_down, ln1_g, ln1_b, ln2_g, ln2_b, ln3_g, ln3_b):
    NC = 8
    f = lambda a: np.ascontiguousarray(np.asarray(a, dtype=np.float32))
    B, SEQ, _ = x_prompt.shape
    BS = x_sample.shape[0]
    NSP, NSS = B // NC, BS // NC
    key = (NSP, SEQ, NSS)
    if key not in _CACHE:
        _CACHE[key] = build(NSP, SEQ, NSS, STOP=float(os.environ.get('KSTOP', '99')))
    nc = _CACHE[key]
    ident, tri, mhi, mb0, i4, rel = _consts()
    table = f(att_rel_bias)[0]
    btg = np.ascontiguousarray(table[:, rel].transpose(1, 0, 2))
    btc = np.ascontiguousarray(np.broadcast_to(table[:, 256][None, :], (128, AH)))
    shared = {
        "w_gu1": f(ffn1_w_gu)[0], "w_dn1": f(ffn1_w_down)[0], "w_win": f(w_in)[0], "w_wml": f(w_ml_proj)[0],
        "w_wat": f(w_att_proj)[0], "w_wout": f(w_out)[0], "w_gu2": f(ffn2_w_gu)[0], "w_dn2": f(ffn2_w_down)[0],
        "lnp": np.ascontiguousarray(np.broadcast_to(
            np.stack([f(ln1_g)[0], f(ln1_b)[0], f(ln2_g)[0], f(ln2_b)[0], f(ln3_g)[0], f(ln3_b)[0]])[None], (128, 6, D))),
        "gbias": np.ascontiguousarray(np.broadcast_to(np.concatenate([f(b_ml_i)[0], f(b_ml_f)[0]])[None], (128, 8))),
        "conv_w": f(ml_conv_w)[0], "conv_b": f(ml_conv_b)[0], "norm_g": f(ml_norm_g)[0],
        "btg": btg, "btc": btc, "c_ident": ident, "c_tri": tri, "c_mhi": mhi, "c_mb0": mb0, "c_i4": i4,
    }
    xp, xs = f(x_prompt), f(x_sample)
    sc, sC, sn, smm = f(state_ml_conv)[0], f(state_ml_C)[0], f(state_ml_n)[0], f(state_ml_m)[0]
    kk = f(cache_att_k)[0].reshape(BS, 512, D)
    vv = f(cache_att_v)[0].reshape(BS, 512, D)
    if os.environ.get("KNOW", "") == "1":
        shared = {k: v for k, v in shared.items() if not k.startswith("w_")}
    in_maps = []
    for c in range(NC):
        m = dict(shared)
        ps_, ss_ = slice(c * NSP, (c + 1) * NSP), slice(c * NSS, (c + 1) * NSS)
        m.update({"x_p": xp[ps_], "x_s": xs[ss_], "st_conv": sc[ss_], "st_C": sC[ss_], "st_n": sn[ss_],
                  "st_m": smm[ss_], "ck": kk[ss_], "cv": vv[ss_]})
        in_maps.append(m)
    res = run_bass_kernel_spmd(nc, in_maps, core_ids=list(range(NC)))
    R = res.results

    def cat(name):
        return np.concatenate([np.asarray(R[c][name], dtype=np.float32) for c in range(NC)], axis=0)
    keep = cat("p_k").shape[1]
    outs = (
        cat("y_p"), cat("y_s"),
        cat("p_conv")[None], cat("s_conv")[None],
        cat("p_C")[None], cat("s_C")[None],
        cat("p_n")[None], cat("s_n")[None],
        cat("p_m")[None], cat("s_m")[None],
        cat("p_k").reshape(1, B, keep, AH, 64), cat("s_k").reshape(1, BS, 64, AH, 64),
        cat("p_v").reshape(1, B, keep, AH, 64), cat("s_v").reshape(1, BS, 64, AH, 64),
    )
    return outs
```

```python
import os
import numpy as np
from contextlib import ExitStack
import concourse.bass as bass
import concourse.mybir as mybir
from concourse.bass_utils import run_bass_kernel_spmd

F32 = mybir.dt.float32
BF16 = mybir.dt.bfloat16
AF = mybir.ActivationFunctionType
ALU = mybir.AluOpType
AX = mybir.AxisListType

D = 1024
KD = 8
FF = 2816
KF = 22
INW = 9224
NH = 4
AH = 16
ALPHA = 2.0 ** 0.25
EPS = 1e-5
EPS_A = EPS / (ALPHA * ALPHA)
NEG = -30000.0
RING = 6
NDSEM = {"sp": 8, "pool": 3}


class Tok:
    __slots__ = ("w", "r")

    def __init__(self):
        self.w = None
        self.r = []


class Op:
    __slots__ = ("eng", "fn", "deps", "sig", "sem", "val", "dma")

    def __init__(self, eng, fn, dma):
        self.eng, self.fn, self.dma = eng, fn, dma
        self.deps = []
        self.sig = False
        self.sem = None
        self.val = 0


class Prog:
    ENG = ("pe", "act", "dve", "pool", "sp")

    def __init__(self):
        self.ops = {e: [] for e in self.ENG}
        self.toks = {}
        self.order = []

    def tk(self, *key):
        t = self.toks.get(key)
        if t is None:
            t = self.toks[key] = Tok()
        return t

    def add(self, eng, fn, reads=(), writes=(), dma=False):
        op = Op(eng, fn, dma)
        excl = [k for k in reads if isinstance(k, tuple) and k[0] == "bank"]
        reads_k = [k for k in reads if not (isinstance(k, tuple) and k[0] == "bank")]
        writes_k = list(writes) + [k for k in excl if k not in writes]
        bank_raw = [self.tk(*k) for k in excl]
        reads = [self.tk(*k) if isinstance(k, tuple) else self.tk(k) for k in reads_k]
        writes = [self.tk(*k) if isinstance(k, tuple) else self.tk(k) for k in writes_k]
        raw = set()
        deps = []
        for t in reads:
            if t.w is not None:
                deps.append(t.w)
                raw.add(id(t.w))
        for t in bank_raw:
            if t.w is not None:
                raw.add(id(t.w))
        for t in writes:
            if t.w is not None:
                deps.append(t.w)
            deps.extend(t.r)
        seen = set()
        for d in deps:
            if id(d) in seen or d is op:
                continue
            seen.add(id(d))
            if (not d.dma) and (not dma) and d.eng == eng:
                if eng == "pe" or id(d) not in raw:
                    continue
            op.deps.append(d)
            d.sig = True
        for t in writes:
            t.w = op
            t.r = []
        for t in reads:
            if t.w is not op:
                t.r.append(op)
        self.ops[eng].append(op)
        self.order.append(op)
        return op

    def assign(self, sems, dsems):
        cnt = {e: 0 for e in self.ENG}
        dcnt = {e: [0] * len(dsems.get(e, [])) for e in self.ENG}
        dlast = {e: [None] * len(dsems.get(e, [])) for e in self.ENG}
        dnext = {e: 0 for e in self.ENG}
        for op in self.order:
            if op.dma:
                k = dnext[op.eng]
                dnext[op.eng] = (k + 1) % len(dsems[op.eng])
                if dlast[op.eng][k] is not None:
                    op.deps.append(dlast[op.eng][k])
                dcnt[op.eng][k] += 16
                op.sem, op.val = dsems[op.eng][k], dcnt[op.eng][k]
                dlast[op.eng][k] = op
                op.sig = True
            elif op.sig:
                cnt[op.eng] += 1
                op.sem, op.val = sems[op.eng], cnt[op.eng]

    def emit(self, eng, e):
        waited = {}
        for op in self.ops[eng]:
            for d in op.deps:
                key = id(d.sem)
                if waited.get(key, 0) < d.val:
                    e.wait_ge(d.sem, d.val)
                    waited[key] = d.val
            ins = op.fn(e)
            if op.sig:
                ins.then_inc(op.sem, 16 if op.dma else 1)


def build(NSP, SEQ, NSS, STOP=99):
    NT = 2
    G = NT * 128
    NG = SEQ // G
    NTS = SEQ // 128
    KEEP_T = min(4, NTS)
    nc = bass.Bass("TRN2", target_bir_lowering=False)
    P = Prog()

    def din(name, shape, dt=F32):
        return nc.dram_tensor(name, list(shape), dt, kind="ExternalInput").ap()

    def dout(name, shape):
        return nc.dram_tensor(name, list(shape), F32, kind="ExternalOutput").ap()

    def dscr(name, shape, dt=BF16):
        return nc.dram_tensor(name, list(shape), dt, kind="Internal").ap()

    x_p = din("x_p", [NSP, SEQ, D])
    x_s = din("x_s", [NSS, 64, D])
    st_conv = din("st_conv", [NSS, 3, 2048])
    st_C = din("st_C", [NSS, NH, 256, 256])
    st_n = din("st_n", [NSS, NH, 256])
    st_m = din("st_m", [NSS, NH])
    ck = din("ck", [NSS, 512, D])
    cv = din("cv", [NSS, 512, D])
    wnames = {"gu1": (D, 2 * FF), "dn1": (FF, D), "win": (D, INW), "wml": (D, D), "wat": (D, D),
              "wout": (D, D), "gu2": (D, 2 * FF), "dn2": (FF, D)}
    NOW = os.environ.get("KNOW", "") == "1"
    wf = {k: (dscr("w_" + k, v, F32) if NOW else din("w_" + k, v)) for k, v in wnames.items()}
    wb = {k: dscr("wb_" + k, v) for k, v in wnames.items()}
    lnp = din("lnp", [128, 6, D])
    gbias = din("gbias", [128, 8])
    conv_w = din("conv_w", [4, 2048])
    conv_b = din("conv_b", [2048])
    norm_g = din("norm_g", [D])
    btg = din("btg", [128, AH, 256])
    btc = din("btc", [128, AH])
    c_ident = din("c_ident", [128, 128])
    c_tri = din("c_tri", [128, 128])
    c_mhi = din("c_mhi", [128, 256])
    c_mb0 = din("c_mb0", [128, 128])
    c_i4 = din("c_i4", [4, 4])

    y_p = dout("y_p", [NSP, SEQ, D])
    y_s = dout("y_s", [NSS, 64, D])
    o_conv = {"p": dout("p_conv", [NSP, 3, 2048]), "s": dout("s_conv", [NSS, 3, 2048])}
    o_C = {"p": dout("p_C", [NSP, NH, 256, 256]), "s": dout("s_C", [NSS, NH, 256, 256])}
    o_n = {"p": dout("p_n", [NSP, NH, 256]), "s": dout("s_n", [NSS, NH, 256])}
    o_m = {"p": dout("p_m", [NSP, NH]), "s": dout("s_m", [NSS, NH])}
    o_k = {"p": dout("p_k", [NSP, KEEP_T * 128, D]), "s": dout("s_k", [NSS, 64, D])}
    o_v = {"p": dout("p_v", [NSP, KEEP_T * 128, D]), "s": dout("s_v", [NSS, 64, D])}

    es = ExitStack()
    with es:
        def sb(name, shape, dt=F32):
            return es.enter_context(nc.sbuf_tensor(name, list(shape), dt))

        resid = sb("resid", [128, NT, D])
        curT = [sb("curTA", [128, KD, G], BF16), sb("curTB", [128, KD, G], BF16)]
        actT = sb("actT", [128, KF, G], BF16)
        qk_raw = sb("qk_raw", [128, 16, 3 + G], BF16)
        qkc = sb("qkc", [128, 16, G], BF16)
        v_tm = sb("v_tm", [128, NT, D], BF16)
        sigo = sb("sigo", [128, KD, G], BF16)
        aqT = sb("aqT", [128, KD, G], BF16)
        kring = sb("kring", [128, KD, RING, 128], BF16)
        vring = sb("vring", [128, RING, D], BF16)
        wslab = [sb("wslab%d" % i, [128, 4096], BF16) for i in range(3)]
        btm = sb("btm", [128, AH, 256], BF16)
        mb0 = sb("mb0", [128, 128], BF16)
        Cst = sb("Cst", [128, 8, 256])
        nst = sb("nst", [128, 8])
        Cs = [sb("CsA", [128, 8, 256], BF16), sb("CsB", [128, 8, 256], BF16)]
        ns = [sb("nsA", [128, 8], BF16), sb("nsB", [128, 8], BF16)]
        lnt = sb("lnt", [128, 6, D])
        ident = sb("ident", [128, 128], BF16)
        identf = sb("identf", [128, 128])
        tri = sb("tri", [128, 128])
        tri64 = sb("tri64", [128, 128])
        onesf = sb("onesf", [128, 128])
        onesb = sb("onesb", [128, 2], BF16)
        i4 = sb("i4", [4, 4])
        mhalf = sb("mhalf", [128, 4])
        dummy = sb("dummyb", [4, 32])
        vecs = sb("vecs", [128, 128])
        vT = sb("vT", [128, 88])
        nT = sb("nT", [8, 128])
        cw = vT[:, 0:64].rearrange("p (j c) -> p j c", c=16)
        cb = vT[:, 64:80]
        ngh = vT[:, 80:88]
        gb = sb("gb", [128, 8])
        th = [sb("th%d" % i, [128, 512]) for i in range(2)]
        uu = [sb("uu%d" % i, [128, 512]) for i in range(2)]
        xn = sb("xn", [128, D])
        hb = sb("hb", [128, D], BF16)
        stg = [sb("stg%d" % i, [128, D]) for i in range(2)]
        cstage = sb("cstage", [128, 2048])
        cacc = [sb("cacc%d" % i, [128, G]) for i in range(2)]
        cth = [sb("cth%d" % i, [128, G]) for i in range(2)]
        pexp = [sb("pexp%d" % i, [128, 640], BF16) for i in range(2)]
        pts = [sb("pts%d" % i, [128, 5, 128], BF16) for i in range(2)]
        o_tm = sb("o_tm", [128, D], BF16)
        h_tm = sb("h_tm", [128, D], BF16)
        ks_tm = sb("ks_tm", [128, 8, 128], BF16)
        ptm = sb("ptm", [128, 4, 128], BF16)
        qz = [sb("qzA", [128, 8, 128], BF16), sb("qzB", [128, 8, 128], BF16)]
        zg = sb("zg", [128, NT, 8])
        spt = sb("spt", [128, NT, 4])
        nbc = sb("nbc", [128, NT, 4])
        agt = sb("agt", [128, NT, 4])
        e8 = sb("e8", [128, 8])
        ucl = sb("ucl", [128, 8])
        u32 = sb("u32", [128, 4])
        mcol = sb("mcol", [128, 4])
        NCH = 2 * NT
        amaxT = sb("amaxT", [4, NCH])
        blT = sb("blT", [4, NCH])
        MD = sb("MD", [4, 2, NCH])
        dx = sb("dx", [4, 4, 2, NCH])
        mT = sb("mT", [4, 1])
        Mb = sb("Mb", [128, 4, NCH])
        rho = sb("rho", [128, 4, NCH])
        rhoh = sb("rhoh", [128, 4, NCH])
        dens = sb("dens", [128, 4])
        sm = {n: sb("sm_" + n, [128, 4]) for n in ("dabs", "dn", "rden", "t1", "rs", "nb2")}
        bst = sb("bst", [128, 4, 6])
        mv = sb("mv", [128, 4, 2])
        lst = sb("lst", [128, 2, 6])
        lmv = sb("lmv", [128, 2])
        lsm = {n: sb("lsm_" + n, [128, 1]) for n in ("ve", "rstd", "nmr")}
        nmax = sb("nmax", [128, AH])
        rsum = sb("rsum", [128, AH])
        rinv = sb("rinv", [128, AH])
        tmp_c = sb("tmp_c", [128, AH])

        NBK = 6
        ps = es.enter_context(nc.psum_tensor("ps", [128, NBK, 512], F32))
        pt = es.enter_context(nc.psum_tensor("pt", [128, 2, 1024], BF16))

        class _PB:
            def __getitem__(self, k):
                return pt[k[0], k[1] - 100, k[2]]
        psb = _PB()

        dsems = {e: [es.enter_context(nc.semaphore("d_%s%d" % (e, i))) for i in range(n)]
                 for e, n in NDSEM.items()}
        sems = {e: es.enter_context(nc.semaphore("s_" + e)) for e in ("pe", "act", "dve", "pool")}

        bank_ptr = [0]

        pinned = set()
        tb_ptr = [0]

        def bank(n=1):
            b = bank_ptr[0]
            for _ in range(2 * NBK):
                if n == 2 and b % 2:
                    b = (b + 1) % NBK
                if all(((b + i) % NBK) not in pinned for i in range(n)) and b + n <= NBK:
                    break
                b = (b + 1) % NBK
            else:
                raise RuntimeError("no free psum bank")
            bank_ptr[0] = (b + n) % NBK
            return b

        def tbank():
            tb_ptr[0] ^= 1
            return 100 + tb_ptr[0]

        def BK(b):
            return ("bank", b)

        flip = [0]

        def evac_eng():
            flip[0] ^= 1
            return "act" if flip[0] else "dve"

        def copy_op(eng, out, in_, reads, writes, scale=None):
            if eng == "act":
                if scale is None:
                    P.add("act", lambda e: e.activation(out=out, in_=in_, func=AF.Copy), reads, writes)
                else:
                    P.add("act", lambda e: e.activation(out=out, in_=in_, func=AF.Copy, scale=scale), reads, writes)
            else:
                if scale is None:
                    P.add(eng, lambda e: e.tensor_copy(out=out, in_=in_), reads, writes)
                else:
                    P.add(eng, lambda e: e.tensor_scalar(out=out, in0=in_, scalar1=scale, scalar2=None,
                                                         op0=ALU.mult), reads, writes)

        def dma(eng, out, in_, reads, writes, slow=False):
            if slow:
                P.add(eng, lambda e: e.dma_start(out=out, in_=in_, allow_slow_non_contiguous=True),
                      reads, writes, dma=True)
            else:
                P.add(eng, lambda e: e.dma_start(out=out, in_=in_), reads, writes, dma=True)

        slab_i = [0]

        def load_slab(pieces):
            i = slab_i[0] % 3
            slab_i[0] += 1
            off = 0
            views = []
            for (wk, k0, nk, c0, ncol) in pieces:
                v = wslab[i][:, off:off + nk * ncol].rearrange("p (k n) -> p k n", n=ncol)
                src = wb[wk].rearrange("(k p) n -> p k n", p=128)[:, k0:k0 + nk, c0:c0 + ncol]
                dma("sp", v, src, [("wb", wk)], [("wslab", i)])
                views.append(v)
                off += nk * ncol
            assert off <= 4096
            return views, ("wslab", i)

        def mm_group(out, pairs, reads, writes, first=True, last=True):
            def fn(e):
                n = len(pairs)
                ins = None
                for i, (l, r) in enumerate(pairs):
                    ins = e.matmul(out, lhsT=l, rhs=r, start=(first and i == 0), stop=(last and i == n - 1))
                return ins
            P.add("pe", fn, reads, writes)

        def transposes(b, src_fn, n, reads, dtype_bf=True, cols=128):
            def fn(e):
                ins = None
                for i in range(n):
                    ins = e.transpose(out=psb[:, b, i * 128:(i + 1) * 128], in_=src_fn(i), identity=ident[:])
                return ins
            P.add("pe", fn, reads + ["ident"], [BK(b)])

        def setup():
            fst = [(cstage[:], ["cstage"]), (resid[:].rearrange("p a n -> p (a n)"), [("resid", 0), ("resid", 1)])]
            bst_ = [(actT[:, 0:8, :].rearrange("p a n -> p (a n)"), [("actT", j) for j in range(8)]),
                    (actT[:, 8:16, :].rearrange("p a n -> p (a n)"), [("actT", j) for j in range(8, 16)])]
            ci_ = 0
            SKIP = os.environ.get("KSKIP", "").split(",")
            for wk in ("gu1", "dn1", "win", "wml", "wat", "wout", "gu2", "dn2"):
                if "W" in SKIP:
                    break
                rows, ncol = wnames[wk]
                parts = []
                for r0 in range(0, rows, 128):
                    for c0 in range(0, ncol, 2048):
                        w_ = min(2048, ncol - c0)
                        fa, ft = fst[ci_ % 2]
                        ba, bt = bst_[ci_ % 2]
                        dma("sp", fa[:, 0:w_], wf[wk][r0:r0 + 128, c0:c0 + w_], [], ft)
                        copy_op(("act", "dve", "pool")[ci_ % 3], ba[:, 0:w_], fa[:, 0:w_], ft, bt)
                        tk_ = ("wbp", wk, len(parts))
                        dma("sp", wb[wk][r0:r0 + 128, c0:c0 + w_], ba[:, 0:w_], bt, [tk_])
                        parts.append(tk_)
                        ci_ += 1
                wi_ = list(wnames).index(wk)
                dma("sp", dummy[:, wi_ * 4:(wi_ + 1) * 4], c_i4[:, :], parts, [("wb", wk)])
            dma("sp", identf[:], c_ident[:, :], [], ["identf"])
            dma("sp", tri[:], c_tri[:, :], [], ["tri"])
            P.add("dve", lambda e: e.tensor_copy(out=i4[:], in_=identf[0:4, 0:4]), ["identf"], ["i4"])
            if "B" not in SKIP:
                dma("sp", lnt[:], lnp[:, :, :], [], ["lnt"])
                dma("sp", gb[:], gbias[:, :], [], ["gb"])
            if "V" not in SKIP:
                dma("sp", vecs[0:64, :], conv_w.rearrange("j (c p) -> (j c) p", p=128), [], ["vecs"])
                dma("sp", vecs[64:80, :], conv_b.rearrange("(c p) -> c p", p=128), [], ["vecs"])
                dma("sp", vecs[80:88, :], norm_g.rearrange("(c p) -> c p", p=128), [], ["vecs"])
                bv = bank()
                P.add("pe", lambda e: e.transpose(out=ps[:, bv, 0:88], in_=vecs[0:88, :], identity=identf[0:88, 0:88]),
                      ["vecs", "identf"], [BK(bv)])
                P.add("dve", lambda e: e.tensor_copy(out=vT[:], in_=ps[:, bv, 0:88]), [BK(bv)], ["cw", "cb", "ngh"])
                P.add("dve", lambda e: e.tensor_scalar(out=ngh, in0=ngh, scalar1=0.5, scalar2=None,
                                                       op0=ALU.mult), ["ngh"], ["ngh"])
            P.add("dve", lambda e: e.tensor_copy(out=ident[:], in_=identf[:]), ["identf"], ["ident"])
            P.add("dve", lambda e: e.tensor_scalar(out=tri64[:], in0=tri[:], scalar1=1.0 / 64.0, scalar2=None,
                                                   op0=ALU.mult), ["tri"], ["tri64"])
            if "M" not in SKIP:
                P.add("pool", lambda e: e.memset(onesf[:], 1.0), [], ["onesf"])
                P.add("pool", lambda e: e.memset(onesb[:], 1.0), [], ["onesb"])
                P.add("pool", lambda e: e.memset(mhalf[:], -0.5), [], ["mhalf"])
                P.add("pool", lambda e: e.memset(qz[0][:], 0.0), [], ["qz"])
                P.add("pool", lambda e: e.memset(qz[1][:], 0.0), [], ["qz"])
                P.add("pool", lambda e: e.memset(dx[:], 0.0), [], ["dx"])
                P.add("pool", lambda e: e.memset(MD[:], 0.0), [], ["MD"])
            if "T" not in SKIP:
                dma("sp", tmp_c[:], btc[:, :], [], ["tmp_c"])
                dma("sp", xn[:, 0:256], c_mhi[:, :], [], ["xn"])
                dma("sp", xn[:, 256:384], c_mb0[:, :], [], ["xn"])
                P.add("dve", lambda e: e.tensor_copy(out=mb0[:], in_=xn[:, 256:384]), ["xn"], ["mb0"])
            for hh in range(2):
                if "T" in SKIP:
                    break
                dma("sp", cstage[:].rearrange("p (a n) -> p a n", n=256), btg[:, hh * 8:(hh + 1) * 8, :], [], ["cstage"])
                for h8 in range(8):
                    h = hh * 8 + h8
                    P.add("dve", lambda e, h=h, h8=h8: e.scalar_tensor_tensor(
                        out=btm[:, h, :], in0=cstage[:, h8 * 256:(h8 + 1) * 256], scalar=tmp_c[:, h:h + 1], in1=xn[:, 0:256],
                        op0=ALU.subtract, op1=ALU.add), ["cstage", "tmp_c", "xn"], ["btm"])

        def layernorm(t, li, eps, cout, final, ydst, nrows):
            R = ("resid", t)
            def f_stats(e):
                e.bn_stats(out=lst[:, 0, :], in_=resid[:, t, 0:512])
                return e.bn_stats(out=lst[:, 1, :], in_=resid[:, t, 512:1024])
            P.add("dve", f_stats, [R], ["lst"])
            P.add("dve", lambda e: e.bn_aggr(out=lmv[:], in_=lst[:].rearrange("p a b -> p (a b)")), ["lst"], ["lmv"])
            P.add("dve", lambda e: e.tensor_scalar(out=lsm["ve"][:], in0=lmv[:, 1:2], scalar1=eps, scalar2=None,
                                                   op0=ALU.add), ["lmv"], ["l_ve"])
            P.add("act", lambda e: e.activation(out=lsm["ve"][:], in_=lsm["ve"][:], func=AF.Ln), ["l_ve"], ["l_ve"])
            P.add("act", lambda e: e.activation(out=lsm["rstd"][:], in_=lsm["ve"][:], func=AF.Exp, scale=-0.5),
                  ["l_ve"], ["l_rstd"])
            P.add("dve", lambda e: e.scalar_tensor_tensor(out=lsm["nmr"][:], in0=lmv[:, 0:1], scalar=-1.0,
                                                          in1=lsm["rstd"][:], op0=ALU.mult, op1=ALU.mult),
                  ["lmv", "l_rstd"], ["l_nmr"])
            P.add("act", lambda e: e.activation(out=xn[:], in_=resid[:, t, :], func=AF.Identity,
                                                bias=lsm["nmr"][:], scale=lsm["rstd"][:]),
                  [R, "l_rstd", "l_nmr"], ["xn"])
            P.add("dve", lambda e: e.tensor_tensor(out=xn[:], in0=xn[:], in1=lnt[:, 2 * li, :], op=ALU.mult),
                  ["xn", "lnt"], ["xn"])
            P.add("dve", lambda e: e.tensor_tensor(out=resid[:, t, :], in0=xn[:], in1=lnt[:, 2 * li + 1, :],
                                                   op=ALU.add), ["xn", "lnt"], [R])
            if final:
                dma("sp", ydst, resid[0:nrows, t, :], [R], [("yout", t)])
            else:
                to_fm(resid[:, t, :], [R], cout, t)

        def to_fm(src, reads, cout, t):
            copy_op(evac_eng(), hb[:], src, reads, ["hb"])
            b = tbank()
            transposes(b, lambda i: hb[:, i * 128:(i + 1) * 128], 8, ["hb"])
            copy_op(evac_eng(), curT[cout][:, :, t * 128:(t + 1) * 128],
                    psb[:, b, :].rearrange("p (k n) -> p k n", n=128), [BK(b)], [("curT", cout, t)])

        def ffn(gk, dk, li, cin, cout, ntl, final, ydsts, nrows):
            Gt = ntl * 128
            cin_r = [("curT", cin, t) for t in range(ntl)]
            for s0 in range(0, KF, 2):
                views, wt = load_slab([(gk, 0, 8, s0 * 128, 256), (gk, 0, 8, FF + s0 * 128, 256)])
                for jj in range(2):
                    j = s0 + jj
                    b = bank()
                    mm_group(ps[:, b, 0:Gt], [(views[0][:, k, jj * 128:(jj + 1) * 128], curT[cin][:, k, 0:Gt])
                                              for k in range(8)], cin_r + [wt], [BK(b)])
                    mm_group(ps[:, b, 256:256 + Gt], [(views[1][:, k, jj * 128:(jj + 1) * 128], curT[cin][:, k, 0:Gt])
                                                      for k in range(8)], cin_r + [wt], [BK(b)])
                    i = j % 2
                    P.add("act", lambda e, b=b, i=i: e.activation(out=th[i][:, 0:Gt], in_=ps[:, b, 0:Gt],
                                                                  func=AF.Tanh, scale=0.5), [BK(b)], [("th", i)])
                    P.add("dve", lambda e, b=b, i=i: e.scalar_tensor_tensor(
                        out=uu[i][:, 0:Gt], in0=th[i][:, 0:Gt], scalar=1.0, in1=ps[:, b, 0:Gt],
                        op0=ALU.add, op1=ALU.mult), [("th", i), BK(b)], [("uu", i)])
                    P.add("dve", lambda e, b=b, i=i, j=j: e.tensor_tensor(
                        out=actT[:, j, 0:Gt], in0=uu[i][:, 0:Gt], in1=ps[:, b, 256:256 + Gt], op=ALU.mult),
                        [("uu", i), BK(b)], [("actT", j)])
            csc = 0.25 / ALPHA
            kparts = [(0, 8), (8, 7), (15, 7)]
            for half in range(2):
                banks = [bank() for _ in range(ntl)]
                for pi, (k0, nk) in enumerate(kparts):
                    views, wt = load_slab([(dk, k0, nk, half * 512, 512)])
                    for t in range(ntl):
                        mm_group(ps[:, banks[t], :],
                                 [(actT[:, k0 + kk, t * 128:(t + 1) * 128], views[0][:, kk, :]) for kk in range(nk)],
                                 [("actT", k0 + kk) for kk in range(nk)] + [wt], [BK(banks[t])],
                                 first=(pi == 0), last=(pi == len(kparts) - 1))
                for t in range(ntl):
                    P.add("dve", lambda e, t=t, bb=banks[t], half=half: e.scalar_tensor_tensor(
                        out=resid[:, t, half * 512:(half + 1) * 512], in0=ps[:, bb, :], scalar=csc,
                        in1=resid[:, t, half * 512:(half + 1) * 512], op0=ALU.mult, op1=ALU.add),
                        [BK(banks[t]), ("resid", t)], [("resid", t)])
            for t in range(ntl):
                layernorm(t, li, EPS_A, cout, final, ydsts[t] if final else None, nrows)

        def group(tiles):
            ntl = len(tiles)
            Gt = ntl * 128
            nrows = tiles[0]["nrows"]
            if STOP <= 1:
                return True
            for t, tl in enumerate(tiles):
                if nrows < 128:
                    P.add("pool", lambda e, t=t: e.memset(resid[64:128, t, :], 0.0), [], [("resid", t)])
                dma("sp", resid[0:nrows, t, :], tl["xsrc"], [], [("resid", t)])
                to_fm(resid[:, t, :], [("resid", t)], 0, t)
            if STOP <= 2:
                return True
            ffn("gu1", "dn1", 0, 0, 1, ntl, False, None, nrows)
            h1 = 1
            h1_r = [("curT", 1, t) for t in range(ntl)]
            if STOP <= 3:
                return True
            tl0 = tiles[0]
            if tl0["first"]:
                if tl0["kind"] == "p":
                    P.add("pool", lambda e: e.memset(qk_raw[:, :, 0:3], 0.0), [], [("qk_raw", c) for c in range(16)])
                else:
                    dma("sp", vecs[0:48, :], st_conv[tl0["b"]].rearrange("r (c p) -> (r c) p", p=128), [], ["vecs"])
                    dma("sp", vecs[48:56, :], st_n[tl0["b"]].rearrange("h (j p) -> (h j) p", p=128), [], ["vecs"])
                    bv = bank()
                    P.add("pe", lambda e, bv=bv: e.transpose(out=ps[:, bv, 0:56], in_=vecs[0:56, :],
                                                             identity=identf[0:56, 0:56]), ["vecs", "identf"], [BK(bv)])
                    P.add("dve", lambda e, bv=bv: e.tensor_copy(
                        out=qk_raw[:, :, 0:3], in_=ps[:, bv, 0:48].rearrange("p (r c) -> p c r", c=16)), [BK(bv)],
                        [("qk_raw", c) for c in range(16)])
                    P.add("dve", lambda e, bv=bv: e.tensor_copy(out=nst[:], in_=ps[:, bv, 48:56]), [BK(bv)], ["nst"])
            for s in range(8):
                views, wt = load_slab([("win", 0, 8, s * 256, 256)])
                for jj in range(2):
                    c = s * 2 + jj
                    b = bank()
                    mm_group(ps[:, b, 0:Gt], [(views[0][:, k, jj * 128:(jj + 1) * 128], curT[h1][:, k, 0:Gt])
                                              for k in range(8)], h1_r + [wt], [BK(b)])
                    copy_op(evac_eng(), qk_raw[:, c, 3:3 + Gt], ps[:, b, 0:Gt], [BK(b)], [("qk_raw", c)])
                for t, tl in enumerate(tiles):
                    if tl["conv_out"]:
                        b = bank()
                        mm_group(ps[:, b, 0:256], [(curT[h1][:, k, t * 128:(t + 1) * 128], views[0][:, k, :])
                                                   for k in range(8)], h1_r + [wt], [BK(b)])
                        copy_op(evac_eng(), cstage[:, s * 256:(s + 1) * 256], ps[:, b, 0:256], [BK(b)], ["cstage"])
            for t, tl in enumerate(tiles):
                if tl["conv_out"]:
                    r0 = tl["nrows"] - 3
                    dma("sp", o_conv[tl["kind"]][tl["b"]], cstage[r0:r0 + 3, :], ["cstage"], ["cstage_out"])
            conv_todo = list(range(16))

            def emit_conv(c):
                i = c % 2
                P.add("dve", lambda e: e.tensor_scalar(
                    out=cacc[i][:, 0:Gt], in0=qk_raw[:, c, 0:Gt], scalar1=cw[:, 0, c:c + 1], scalar2=cb[:, c:c + 1],
                    op0=ALU.mult, op1=ALU.add), [("qk_raw", c), "cw", "cb"], [("cacc", i)])
                for j in range(1, 4):
                    P.add("dve", lambda e, j=j: e.scalar_tensor_tensor(
                        out=cacc[i][:, 0:Gt], in0=qk_raw[:, c, j:j + Gt], scalar=cw[:, j, c:c + 1], in1=cacc[i][:, 0:Gt],
                        op0=ALU.mult, op1=ALU.add), [("qk_raw", c), "cw", ("cacc", i)], [("cacc", i)])
                P.add("act", lambda e: e.activation(out=cth[i][:, 0:Gt], in_=cacc[i][:, 0:Gt], func=AF.Tanh,
                                                    scale=0.5), [("cacc", i)], [("cth", i)])
                P.add("dve", lambda e: e.scalar_tensor_tensor(
                    out=qkc[:, c, 0:Gt], in0=cth[i][:, 0:Gt], scalar=1.0, in1=cacc[i][:, 0:Gt],
                    op0=ALU.add, op1=ALU.mult), [("cth", i), ("cacc", i)], [("qkc", c)])

            def conv_step():
                if conv_todo and STOP > 4:
                    emit_conv(conv_todo.pop(0))

            if STOP <= 3.1:
                return True
            for s in range(2):
                views, wt = load_slab([("win", 0, 8, 2048 + s * 512, 512)])
                for t in range(ntl):
                    b = bank()
                    mm_group(ps[:, b, :], [(curT[h1][:, k, t * 128:(t + 1) * 128], views[0][:, k, :]) for k in range(8)],
                             h1_r + [wt], [BK(b)])
                    copy_op(evac_eng(), v_tm[:, t, s * 512:(s + 1) * 512], ps[:, b, :], [BK(b)], [("v_tm", t)])
                conv_step()
            if STOP <= 3.2:
                return True
            for s in range(4):
                views, wt = load_slab([("win", 0, 8, 3072 + s * 256, 256)])
                for jj in range(2):
                    c = s * 2 + jj
                    b = bank()
                    mm_group(ps[:, b, 0:Gt], [(views[0][:, k, jj * 128:(jj + 1) * 128], curT[h1][:, k, 0:Gt])
                                              for k in range(8)], h1_r + [wt], [BK(b)])
                    i = c % 2
                    P.add("act", lambda e, b=b, i=i: e.activation(out=th[i][:, 0:Gt], in_=ps[:, b, 0:Gt],
                                                                  func=AF.Tanh, scale=0.5), [BK(b)], [("th", i)])
                    P.add("dve", lambda e, c=c, i=i: e.tensor_scalar(
                        out=sigo[:, c, 0:Gt], in0=th[i][:, 0:Gt], scalar1=ngh[:, c:c + 1], scalar2=ngh[:, c:c + 1],
                        op0=ALU.mult, op1=ALU.add), [("th", i), "ngh"], [("sigo", c)])
                conv_step()
            if STOP <= 3.3:
                return True
            views, wt = load_slab([("win", 0, 8, 4096, 8)])
            for t in range(ntl):
                b = bank()
                mm_group(ps[:, b, 0:8], [(curT[h1][:, k, t * 128:(t + 1) * 128], views[0][:, k, :]) for k in range(8)],
                         h1_r + [wt], [BK(b)])
                P.add("dve", lambda e, t=t, b=b: e.tensor_tensor(out=zg[:, t, :], in0=ps[:, b, 0:8], in1=gb[:],
                                                                  op=ALU.add), [BK(b), "gb"], [("zg", t)])
            if STOP <= 3.4:
                return True
            for s in range(0 if "Q" in os.environ.get("KSKIP", "") else 4):
                views, wt = load_slab([("win", 0, 8, 4104 + s * 256, 256)])
                for jj in range(2):
                    c = s * 2 + jj
                    b = bank()
                    mm_group(ps[:, b, 0:Gt], [(views[0][:, k, jj * 128:(jj + 1) * 128], curT[h1][:, k, 0:Gt])
                                              for k in range(8)], h1_r + [wt], [BK(b)])
                    copy_op(evac_eng(), aqT[:, c, 0:Gt], ps[:, b, 0:Gt], [BK(b)], [("aqT", c)], scale=0.125)
                conv_step()
            if STOP <= 3.5:
                return True
            for s in range(4):
                views, wt = load_slab([("win", 0, 8, (0 if "Z" in os.environ.get("KSKIP", "") else 5128) + s * 256, 256)])
                for jj in range(2):
                    c = s * 2 + jj
                    b = bank()
                    mm_group(ps[:, b, 0:Gt], [(views[0][:, k, jj * 128:(jj + 1) * 128], curT[h1][:, k, 0:Gt])
                                              for k in range(8)], h1_r + [wt], [BK(b)])
                    for t, tl in enumerate(tiles):
                        copy_op(evac_eng(), (sigo[:, c, t * 128:(t + 1) * 128] if "R" in os.environ.get("KSKIP", "")
                                             else kring[:, c, tl["own_slot"], :]), ps[:, b, t * 128:(t + 1) * 128],
                                [BK(b)], [("kring", tl["own_slot"])])
                for t, tl in enumerate(tiles):
                    if tl["kv_out"] is not None and "K" not in os.environ.get("KSKIP", ""):
                        b = bank()
                        mm_group(ps[:, b, 0:256], [(curT[h1][:, k, t * 128:(t + 1) * 128], views[0][:, k, :])
                                                   for k in range(8)], h1_r + [wt], [BK(b)])
                        copy_op(evac_eng(), stg[t][:, s * 256:(s + 1) * 256], ps[:, b, 0:256], [BK(b)], [("stg", t)])
                conv_step()
            for t, tl in enumerate(tiles):
                if tl["kv_out"] is not None and "D" not in os.environ.get("KSKIP", ""):
                    r0 = tl["kv_out"]
                    dma("sp", o_k[tl["kind"]][tl["b"], r0:r0 + tl["nrows"], :], stg[t][0:tl["nrows"], :],
                        [("stg", t)], [("stg_out", t)])
            if STOP <= 3.6:
                return True
            for s in range(2):
                views, wt = load_slab([("win", 0, 8, 6152 + s * 512, 512)])
                for t, tl in enumerate(tiles):
                    b = bank()
                    mm_group(ps[:, b, :], [(curT[h1][:, k, t * 128:(t + 1) * 128], views[0][:, k, :]) for k in range(8)],
                             h1_r + [wt], [BK(b)])
                    copy_op("act", vring[:, tl["own_slot"], s * 512:(s + 1) * 512], ps[:, b, :], [BK(b)],
                            [("vring", tl["own_slot"])])
                    if tl["kv_out"] is not None:
                        copy_op("dve", cstage[:, t * 1024 + s * 512:t * 1024 + (s + 1) * 512], ps[:, b, :], [BK(b)],
                                ["cstage"])
                conv_step()
            for t, tl in enumerate(tiles):
                if tl["kv_out"] is not None:
                    r0 = tl["kv_out"]
                    dma("sp", o_v[tl["kind"]][tl["b"], r0:r0 + tl["nrows"], :],
                        cstage[0:tl["nrows"], t * 1024:(t + 1) * 1024], ["cstage"], ["cstage_out"])

            if STOP <= 4:
                return True
            while conv_todo:
                emit_conv(conv_todo.pop(0))
            for c in []:
                i = c % 2
                P.add("dve", lambda e, c=c, i=i: e.tensor_scalar(
                    out=cacc[i][:, 0:Gt], in0=qk_raw[:, c, 0:Gt], scalar1=cw[:, 0, c:c + 1], scalar2=cb[:, c:c + 1],
                    op0=ALU.mult, op1=ALU.add), [("qk_raw", c), "cw", "cb"], [("cacc", i)])
                for j in range(1, 4):
                    P.add("dve", lambda e, c=c, i=i, j=j: e.scalar_tensor_tensor(
                        out=cacc[i][:, 0:Gt], in0=qk_raw[:, c, j:j + Gt], scalar=cw[:, j, c:c + 1], in1=cacc[i][:, 0:Gt],
                        op0=ALU.mult, op1=ALU.add), [("qk_raw", c), "cw", ("cacc", i)], [("cacc", i)])
                P.add("act", lambda e, i=i: e.activation(out=cth[i][:, 0:Gt], in_=cacc[i][:, 0:Gt], func=AF.Tanh,
                                                         scale=0.5), [("cacc", i)], [("cth", i)])
                P.add("dve", lambda e, c=c, i=i: e.scalar_tensor_tensor(
                    out=qkc[:, c, 0:Gt], in0=cth[i][:, 0:Gt], scalar=1.0, in1=cacc[i][:, 0:Gt],
                    op0=ALU.add, op1=ALU.mult), [("cth", i), ("cacc", i)], [("qkc", c)])
            if tiles[-1]["kind"] == "p" and not tiles[-1]["last"]:
                P.add("pool", lambda e: e.tensor_copy(out=qk_raw[:, :, 0:3], in_=qk_raw[:, :, Gt:Gt + 3]),
                      [("qk_raw", c) for c in range(16)], [("qk_raw", c) for c in range(16)])

            if STOP <= 5:
                return True
            nch = 2 * ntl
            zr = [("zg", t) for t in range(ntl)]
            P.add("act", lambda e: e.activation(out=spt[:, 0:ntl, :], in_=zg[:, 0:ntl, 4:8], func=AF.Exp, scale=-1.0),
                  zr, ["spt"])
            P.add("act", lambda e: e.activation(out=spt[:, 0:ntl, :], in_=spt[:, 0:ntl, :], func=AF.Ln, bias=1.0),
                  ["spt"], ["spt"])
            for t in range(ntl):
                b = bank()
                mm_group(ps[:, b, 0:4], [(tri[:], spt[:, t, :])], ["tri", "spt"], [BK(b)])
                P.add("dve", lambda e, t=t, b=b: e.tensor_copy(out=nbc[:, t, :], in_=ps[:, b, 0:4]), [BK(b)], [("nbc", t)])
                P.add("dve", lambda e, t=t: e.tensor_tensor(out=agt[:, t, :], in0=zg[:, t, 0:4], in1=nbc[:, t, :],
                                                            op=ALU.add), [("zg", t), ("nbc", t)], [("agt", t)])
                b2 = bank()

                def ftr(e, t=t, b2=b2):
                    e.transpose(out=ps[0:4, b2, 0:128], in_=agt[:, t, :], identity=identf[:])
                    return e.transpose(out=ps[0:4, b2, 128:256], in_=spt[:, t, :], identity=identf[:])
                P.add("pe", ftr, [("agt", t), "spt", "identf"], [BK(b2)])
                P.add("dve", lambda e, t=t, b2=b2: e.tensor_reduce(
                    out=amaxT[:, 2 * t:2 * t + 2], in_=ps[0:4, b2, 0:128].rearrange("p (c s) -> p c s", s=64),
                    axis=AX.X, op=ALU.max), [BK(b2)], ["amaxT"])
                P.add("dve", lambda e, t=t, b2=b2: e.tensor_reduce(
                    out=blT[:, 2 * t:2 * t + 2], in_=ps[0:4, b2, 128:256].rearrange("p (c s) -> p c s", s=64),
                    axis=AX.X, op=ALU.add, negate=True), [BK(b2)], ["blT"])
            for t, tl in enumerate(tiles):
                if tl["first"]:
                    if tl["kind"] == "p":
                        P.add("dve", lambda e: e.memset(mT[:], 0.0), [], ["mT"])
                    else:
                        dma("sp", mT[:], st_m[tl["b"]].rearrange("(h o) -> h o", o=1), [], ["mT"], slow=True)
                for ci in range(2):
                    c = 2 * t + ci
                    P.add("dve", lambda e, c=c: e.tensor_tensor(out=MD[:, 0, c:c + 1], in0=mT[:], in1=amaxT[:, c:c + 1],
                                                                op=ALU.max), ["mT", "amaxT"], ["MD"])
                    P.add("dve", lambda e, c=c: e.tensor_tensor(out=MD[:, 1, c:c + 1], in0=mT[:], in1=MD[:, 0, c:c + 1],
                                                                op=ALU.subtract), ["mT", "MD"], ["MD"])
                    if ci < tl["nchunks"]:
                        P.add("dve", lambda e, c=c: e.tensor_tensor(out=mT[:], in0=blT[:, c:c + 1], in1=MD[:, 0, c:c + 1],
                                                                    op=ALU.add), ["blT", "MD"], ["mT"])
            for hh in range(4):
                P.add("dve", lambda e, hh=hh: e.tensor_scalar(out=dx[:, hh, :, 0:nch], in0=MD[:, :, 0:nch],
                                                              scalar1=i4[:, hh:hh + 1], scalar2=None, op0=ALU.mult),
                      ["MD", "i4"], ["dx"])
            b = bank()
            mm_group(ps[:, b, 0:8 * NCH], [(onesf[0:4, :], dx[:].rearrange("p a b c -> p (a b c)"))],
                     ["onesf", "dx"], [BK(b)])
            pv = ps[:, b, 0:8 * NCH].rearrange("p (a b c) -> p a b c", a=4, b=2)
            P.add("dve", lambda e, pv=pv: e.tensor_copy(out=Mb[:], in_=pv[:, :, 0, :]), [BK(b)], ["Mb"])
            P.add("act", lambda e, pv=pv: e.activation(out=rho[:], in_=pv[:, :, 1, :], func=AF.Exp), [BK(b)], ["rho"])
            P.add("dve", lambda e: e.tensor_scalar(out=rhoh[:], in0=rho[:], scalar1=0.5, scalar2=None, op0=ALU.mult),
                  ["rho"], ["rhoh"])

            if STOP <= 6:
                return True
            for t, tl in enumerate(tiles):
                tc = slice(t * 128, (t + 1) * 128)
                for ci in range(2):
                    c = 2 * t + ci
                    P.add("dve", lambda e, ci=ci, c=c: e.tensor_copy(out=mcol[ci * 64:(ci + 1) * 64, :],
                                                                     in_=Mb[ci * 64:(ci + 1) * 64, :, c]), ["Mb"], ["mcol"])
                P.add("dve", lambda e, t=t: e.tensor_tensor(out=e8[:, 0:4], in0=agt[:, t, :], in1=mcol[:], op=ALU.subtract),
                      [("agt", t), "mcol"], ["e8"])
                P.add("dve", lambda e, t=t: e.tensor_tensor(out=e8[:, 4:8], in0=nbc[:, t, :], in1=mcol[:], op=ALU.subtract),
                      [("nbc", t), "mcol"], ["e8"])
                P.add("act", lambda e: e.activation(out=ucl[:], in_=e8[:], func=AF.Exp), ["e8"], ["ucl"])
                P.add("dve", lambda e: e.tensor_scalar(out=u32[:], in0=ucl[:, 0:4], scalar1=1.0 / 32.0, scalar2=None,
                                                       op0=ALU.mult), ["ucl"], ["u32"])
                bS = bank()
                for h in range(4):
                    mm_group(ps[:, bS, h * 128:(h + 1) * 128],
                             [(qkc[:, 8 + 2 * h + j, tc], qkc[:, 2 * h + j, tc]) for j in range(2)],
                             [("qkc", cc) for cc in (8 + 2 * h, 9 + 2 * h, 2 * h, 2 * h + 1)], [BK(bS)])
                for h in range(4):
                    P.add("dve", lambda e, h=h, bS=bS: e.scalar_tensor_tensor(
                        out=ptm[:, h, :], in0=ps[:, bS, h * 128:(h + 1) * 128], scalar=ucl[:, h:h + 1], in1=tri64[:],
                        op0=ALU.mult, op1=ALU.mult), [BK(bS), "ucl", "tri64"], ["ptm"])
                bK = tbank()
                transposes(bK, lambda i, tc=tc: qkc[:, 8 + i, tc], 8, [("qkc", 8 + i) for i in range(8)])
                for h in range(4):
                    P.add("act", lambda e, h=h, bK=bK: e.activation(
                        out=ks_tm[:, 2 * h:2 * h + 2, :],
                        in_=psb[:, bK, 2 * h * 128:(2 * h + 2) * 128].rearrange("p (a n) -> p a n", n=128),
                        func=AF.Copy, scale=u32[:, h:h + 1]), [BK(bK), "u32"], ["ks_tm"])
                P.add("pool", lambda e, t=t: e.tensor_copy(out=qz[0][:, :, 0:64], in_=qkc[:, 0:8, t * 128:t * 128 + 64]),
                      [("qkc", i) for i in range(8)], ["qz"])
                P.add("pool", lambda e, t=t: e.tensor_copy(out=qz[1][:, :, 64:128], in_=qkc[:, 0:8, t * 128 + 64:t * 128 + 128]),
                      [("qkc", i) for i in range(8)], ["qz"])
                if tl["first"]:
                    if tl["kind"] == "p":
                        P.add("pool", lambda e: e.memset(Cst[:], 0.0), [], ["Cst"])
                        P.add("pool", lambda e: e.memset(nst[:], 0.0), [], ["nst"])
                    else:
                        dma("sp", Cst[:], st_C[tl["b"]].rearrange("h (j p) e -> p (h j) e", p=128), [], ["Cst"])
                for ci in range(2):
                    c = 2 * t + ci
                    r = slice(ci * 64, (ci + 1) * 64)
                    for h in range(4):
                        P.add("act", lambda e, h=h, ci=ci, c=c: e.activation(
                            out=Cs[ci][:, 2 * h:2 * h + 2, :], in_=Cst[:, 2 * h:2 * h + 2, :], func=AF.Copy,
                            scale=rhoh[:, h, c:c + 1]), ["Cst", "rhoh"], [("Cs", ci)])
                        P.add("dve", lambda e, h=h, ci=ci, c=c: e.tensor_scalar(
                            out=ns[ci][:, 2 * h:2 * h + 2], in0=nst[:, 2 * h:2 * h + 2], scalar1=rhoh[:, h, c:c + 1],
                            scalar2=None, op0=ALU.mult), ["nst", "rhoh"], [("ns", ci)])
                    if ci < tl["nchunks"]:
                        for hp in range(2):
                            bb = [bank(), bank()]
                            for hi in range(2):
                                h = 2 * hp + hi
                                for j in range(2):
                                    mm_group(ps[:, bb[hi], j * 256:(j + 1) * 256],
                                             [(ks_tm[r, 2 * h + j, :], v_tm[r, t, h * 256:(h + 1) * 256])],
                                             ["ks_tm", ("v_tm", t)], [BK(bb[hi])])
                            for hi in range(2):
                                h = 2 * hp + hi
                                P.add("dve", lambda e, h=h, c=c, bq=bb[hi]: e.scalar_tensor_tensor(
                                    out=Cst[:, 2 * h:2 * h + 2, :], in0=Cst[:, 2 * h:2 * h + 2, :], scalar=rho[:, h, c:c + 1],
                                    in1=ps[:, bq, :].rearrange("p (a n) -> p a n", n=256), op0=ALU.mult, op1=ALU.add),
                                    ["Cst", "rho", BK(bb[hi])], ["Cst"])
                        bn_ = bank()
                        for h in range(4):
                            for j in range(2):
                                mm_group(ps[:, bn_, 2 * h + j:2 * h + j + 1], [(ks_tm[r, 2 * h + j, :], onesb[r, 0:1])],
                                         ["ks_tm", "onesb"], [BK(bn_)])
                        for h in range(4):
                            P.add("dve", lambda e, h=h, c=c, bn_=bn_: e.scalar_tensor_tensor(
                                out=nst[:, 2 * h:2 * h + 2], in0=nst[:, 2 * h:2 * h + 2], scalar=rho[:, h, c:c + 1],
                                in1=ps[:, bn_, 2 * h:2 * h + 2], op0=ALU.mult, op1=ALU.add),
                                ["nst", "rho", BK(bn_)], ["nst"])
                for h in range(4):
                    bo = bank()
                    pairs = [(qz[0][:, 2 * h + j, :], Cs[0][:, 2 * h + j, :]) for j in range(2)] + \
                            [(qz[1][:, 2 * h + j, :], Cs[1][:, 2 * h + j, :]) for j in range(2)] + \
                            [(ptm[:, h, :], v_tm[:, t, h * 256:(h + 1) * 256])]
                    mm_group(ps[:, bo, 0:256], pairs, ["qz", ("Cs", 0), ("Cs", 1), "ptm", ("v_tm", t)], [BK(bo)])
                    pairs2 = [(qz[0][:, 2 * h + j, :], ns[0][:, 2 * h + j:2 * h + j + 1]) for j in range(2)] + \
                             [(qz[1][:, 2 * h + j, :], ns[1][:, 2 * h + j:2 * h + j + 1]) for j in range(2)] + \
                             [(ptm[:, h, :], onesb[:, 0:1])]
                    mm_group(ps[:, bo, 256:257], pairs2, ["qz", ("ns", 0), ("ns", 1), "ptm", "onesb"], [BK(bo)])
                    P.add("dve", lambda e, h=h, bo=bo: e.tensor_copy(out=dens[:, h:h + 1], in_=ps[:, bo, 256:257]),
                          [BK(bo)], ["dens"])
                    P.add("dve", lambda e, h=h, bo=bo: e.bn_stats(out=bst[:, h, :], in_=ps[:, bo, 0:256]), [BK(bo)], ["bst"])
                    P.add("dve", lambda e, h=h: e.bn_aggr(out=mv[:, h, :], in_=bst[:, h, :]), ["bst"], ["mv"])
                    if h == 0:
                        pass
                    tl.setdefault("_bo", []).append(bo)
                S = sm
                P.add("dve", lambda e: e.tensor_tensor(out=S["dabs"][:], in0=dens[:], in1=ucl[:, 4:8], op=ALU.max),
                      ["dens", "ucl"], ["s_dabs"])
                P.add("dve", lambda e: e.scalar_tensor_tensor(out=S["dn"][:], in0=dens[:], scalar=-1.0, in1=S["dabs"][:],
                                                              op0=ALU.mult, op1=ALU.max), ["dens", "s_dabs"], ["s_dn"])
                P.add("dve", lambda e: e.reciprocal(out=S["rden"][:], in_=S["dn"][:]), ["s_dn"], ["s_rden"])
                P.add("dve", lambda e: e.tensor_tensor(out=S["t1"][:], in0=S["rden"][:], in1=S["rden"][:], op=ALU.mult),
                      ["s_rden"], ["s_t1"])
                P.add("dve", lambda e: e.tensor_tensor(out=S["t1"][:], in0=S["t1"][:], in1=mv[:, :, 1], op=ALU.mult),
                      ["s_t1", "mv"], ["s_t1"])
                P.add("dve", lambda e: e.tensor_scalar(out=S["t1"][:], in0=S["t1"][:], scalar1=EPS, scalar2=None,
                                                       op0=ALU.add), ["s_t1"], ["s_t1"])
                P.add("act", lambda e: e.activation(out=S["t1"][:], in_=S["t1"][:], func=AF.Ln), ["s_t1"], ["s_t1"])
                P.add("act", lambda e: e.activation(out=S["rs"][:], in_=S["t1"][:], func=AF.Exp, scale=-0.5),
                      ["s_t1"], ["s_rs"])
                P.add("dve", lambda e: e.tensor_tensor(out=S["rs"][:], in0=S["rs"][:], in1=S["rden"][:], op=ALU.mult),
                      ["s_rs", "s_rden"], ["s_rs"])
                P.add("dve", lambda e: e.scalar_tensor_tensor(out=S["nb2"][:], in0=mv[:, :, 0], scalar=-1.0, in1=S["rs"][:],
                                                              op0=ALU.mult, op1=ALU.mult), ["mv", "s_rs"], ["s_nb2"])
                for h in range(4):
                    bo = tl["_bo"][h]
                    P.add("act", lambda e, h=h, bo=bo: e.activation(
                        out=h_tm[:, h * 256:(h + 1) * 256], in_=ps[:, bo, 0:256], func=AF.Identity,
                        bias=S["nb2"][:, h:h + 1], scale=S["rs"][:, h:h + 1]), [BK(bo), "s_rs", "s_nb2"], ["h_tm"])
                bT = tbank()
                transposes(bT, lambda i: h_tm[:, i * 128:(i + 1) * 128], 8, ["h_tm"])
                P.add("dve", lambda e, bT=bT, tc=tc: e.tensor_tensor(
                    out=actT[:, 0:8, tc], in0=psb[:, bT, :].rearrange("p (k n) -> p k n", n=128), in1=sigo[:, :, tc],
                    op=ALU.mult), [BK(bT)] + [("sigo", c) for c in range(8)], [("actT", c) for c in range(8)])

                if STOP <= 7:
                    continue
                slots = tl["slots"]
                valid = [kt for kt in range(5) if slots[kt] is not None]
                j0 = valid[0] * 128
                bO = bank(2)
                pinned.update((bO, bO + 1))
                for h in range(AH):
                    hp = slice((h % 2) * 64, (h % 2) * 64 + 64)
                    c = h // 2
                    bS2 = bank(2)
                    S2 = ps[:, bS2:bS2 + 2, :].rearrange("p a n -> p (a n)")

                    def fS(e, h=h, hp=hp, c=c, S2=S2, tc=tc):
                        ins = None
                        for kt in valid:
                            extra = None
                            if kt == 0:
                                extra = mb0[:]
                            elif kt == 3:
                                extra = btm[:, h, 0:128]
                            elif kt == 4:
                                extra = btm[:, h, 128:256]
                            ins = e.matmul(S2[:, kt * 128:(kt + 1) * 128], lhsT=aqT[hp, c, tc],
                                           rhs=kring[hp, c, slots[kt], :], start=True, stop=(extra is None))
                            if extra is not None:
                                ins = e.matmul(S2[:, kt * 128:(kt + 1) * 128], lhsT=ident[:], rhs=extra,
                                               start=False, stop=True)
                        return ins
                    P.add("pe", fS, [("aqT", c), "ident", "mb0", "btm"] + [("kring", slots[kt]) for kt in valid],
                          [BK(bS2), BK(bS2 + 1)])
                    P.add("dve", lambda e, h=h, S2=S2: e.tensor_reduce(out=nmax[:, h:h + 1], in_=S2[:, j0:640], axis=AX.X,
                                                                       op=ALU.max, negate=True),
                          [BK(bS2), BK(bS2 + 1)], [("nmax", h)])
                    i = h % 2
                    P.add("act", lambda e, h=h, S2=S2, i=i: e.activation(
                        out=pexp[i][:, j0:640], in_=S2[:, j0:640], func=AF.Exp, bias=nmax[:, h:h + 1],
                        accum_out=rsum[:, h:h + 1]), [BK(bS2), BK(bS2 + 1), ("nmax", h)], [("pexp", i), ("rsum", h)])
                    bP = tbank()

                    def fT(e, i=i, bP=bP):
                        ins = None
                        for kt in valid:
                            ins = e.transpose(out=psb[:, bP, kt * 128:(kt + 1) * 128], in_=pexp[i][:, kt * 128:(kt + 1) * 128],
                                              identity=ident[:])
                        return ins
                    P.add("pe", fT, [("pexp", i), "ident"], [BK(bP)])
                    copy_op(evac_eng(), pts[i][:, valid[0]:5, :],
                            psb[:, bP, j0:640].rearrange("p (k n) -> p k n", n=128), [BK(bP)], [("pts", i)])
                    ob = bO + (h // 8)
                    oc = (h % 8) * 64
                    mm_group(ps[:, ob, oc:oc + 64],
                             [(pts[i][:, kt, :], vring[:, slots[kt], h * 64:(h + 1) * 64]) for kt in valid],
                             [("pts", i)] + [("vring", slots[kt]) for kt in valid], [BK(ob)])
                P.add("dve", lambda e: e.reciprocal(out=rinv[:], in_=rsum[:]), [("rsum", h) for h in range(AH)], ["rinv"])
                for h in range(AH):
                    ob = bO + (h // 8)
                    oc = (h % 8) * 64
                    copy_op(evac_eng(), o_tm[:, h * 64:(h + 1) * 64], ps[:, ob, oc:oc + 64], [BK(ob), "rinv"], ["o_tm"],
                            scale=rinv[:, h:h + 1])
                pinned.clear()
                bT2 = tbank()
                transposes(bT2, lambda i: o_tm[:, i * 128:(i + 1) * 128], 8, ["o_tm"])
                copy_op(evac_eng(), actT[:, 8:16, tc], psb[:, bT2, :].rearrange("p (k n) -> p k n", n=128),
                        [BK(bT2)], [("actT", c) for c in range(8, 16)])

                if tl["last"]:
                    kd, bi = tl["kind"], tl["b"]
                    dma("sp", o_C[kd][bi].rearrange("h (j p) e -> p (h j) e", p=128), Cst[:], ["Cst"], ["Cout"])
                    bn2 = bank()
                    P.add("pe", lambda e, bn2=bn2: e.transpose(out=ps[0:8, bn2, 0:128], in_=nst[:], identity=identf[:]),
                          ["nst", "identf"], [BK(bn2)])
                    P.add("dve", lambda e, bn2=bn2: e.tensor_copy(out=nT[:], in_=ps[0:8, bn2, 0:128]), [BK(bn2)], ["nT"])
                    dma("sp", o_n[kd][bi].rearrange("h (j p) -> (h j) p", p=128), nT[:], ["nT"], ["nout"])
                    dma("sp", o_m[kd][bi].rearrange("(h o) -> h o", o=1), mT[:], ["mT"], ["mout"], slow=True)

            if STOP <= 8:
                return True
            for q4 in range(4):
                vX, wtX = load_slab([("wml", 0, 8, q4 * 256, 256), ("wat", 0, 8, q4 * 256, 256)])
                vY, wtY = load_slab([("win", 0, 8, 7176 + q4 * 256, 256), ("win", 0, 8, 8200 + q4 * 256, 256)])
                for jj in range(2):
                    c = q4 * 2 + jj
                    b1 = bank()
                    b2 = bank()
                    cs = slice(jj * 128, (jj + 1) * 128)
                    mm_group(ps[:, b1, 0:Gt], [(vY[0][:, k, cs], curT[h1][:, k, 0:Gt]) for k in range(8)], h1_r + [wtY], [BK(b1)])
                    mm_group(ps[:, b1, 256:256 + Gt], [(vY[1][:, k, cs], curT[h1][:, k, 0:Gt]) for k in range(8)],
                             h1_r + [wtY], [BK(b1)])
                    mm_group(ps[:, b2, 0:Gt], [(vX[0][:, k, cs], actT[:, k, 0:Gt]) for k in range(8)],
                             [("actT", k) for k in range(8)] + [wtX], [BK(b2)])
                    mm_group(ps[:, b2, 256:256 + Gt], [(vX[1][:, k, cs], actT[:, 8 + k, 0:Gt]) for k in range(8)],
                             [("actT", 8 + k) for k in range(8)] + [wtX], [BK(b2)])
                    i = c % 2
                    P.add("act", lambda e, b1=b1, i=i: e.activation(out=th[i][:], in_=ps[:, b1, :], func=AF.Tanh, scale=0.5),
                          [BK(b1)], [("th", i)])
                    P.add("dve", lambda e, b2=b2, i=i: e.scalar_tensor_tensor(
                        out=uu[i][:], in0=th[i][:], scalar=1.0, in1=ps[:, b2, :], op0=ALU.add, op1=ALU.mult),
                        [("th", i), BK(b2)], [("uu", i)])
                    P.add("dve", lambda e, c=c, i=i: e.tensor_tensor(
                        out=curT[0][:, c, 0:Gt], in0=uu[i][:, 0:Gt], in1=uu[i][:, 256:256 + Gt], op=ALU.add),
                        [("uu", i)], [("curT", 0, t) for t in range(ntl)])
            m_r = [("curT", 0, t) for t in range(ntl)]
            csc2 = 0.5 / ALPHA
            for half in range(2):
                views, wt = load_slab([("wout", 0, 8, half * 512, 512)])
                for t in range(ntl):
                    b = bank()
                    mm_group(ps[:, b, :], [(curT[0][:, k, t * 128:(t + 1) * 128], views[0][:, k, :]) for k in range(8)],
                             m_r + [wt], [BK(b)])
                    P.add("dve", lambda e, t=t, b=b, half=half: e.scalar_tensor_tensor(
                        out=resid[:, t, half * 512:(half + 1) * 512], in0=ps[:, b, :], scalar=csc2,
                        in1=resid[:, t, half * 512:(half + 1) * 512], op0=ALU.mult, op1=ALU.add),
                        [BK(b), ("resid", t)], [("resid", t)])
            for t in range(ntl):
                layernorm(t, 1, EPS_A, 1, False, None, nrows)
            if STOP <= 9:
                return True
            ffn("gu2", "dn2", 2, 1, 0, ntl, True, [tl["ydst"] for tl in tiles], nrows)

        setup()
        for b in range(NSP):
            for g in range(NG):
                tiles = []
                for t in range(NT):
                    Tg = g * NT + t
                    slots = [((Tg - 4 + kt) % RING) if (Tg - 4 + kt) >= 0 else None for kt in range(5)]
                    kvo = None
                    if Tg >= NTS - KEEP_T:
                        kvo = (Tg - (NTS - KEEP_T)) * 128
                    tiles.append(dict(xsrc=x_p[b, Tg * 128:(Tg + 1) * 128, :], nrows=128,
                                      ydst=y_p[b, Tg * 128:(Tg + 1) * 128, :], kind="p", b=b, Tg=Tg,
                                      slots=slots, own_slot=Tg % RING, first=(Tg == 0),
                                      conv_out=(Tg == NTS - 1), kv_out=kvo, nchunks=2, last=(Tg == NTS - 1)))
                if group(tiles):
                    break
            if STOP < 99:
                break
        for b in range(NSS if STOP >= 99 else 0):
            for kt in range(4):
                dma("pool", hb[:], ck[b, kt * 128:(kt + 1) * 128, :], [], ["hb"])
                bb = tbank()
                transposes(bb, lambda i: hb[:, i * 128:(i + 1) * 128], 8, ["hb"])
                copy_op(evac_eng(), kring[:, :, kt, :], psb[:, bb, :].rearrange("p (k n) -> p k n", n=128),
                        [BK(bb)], [("kring", kt)])
                dma("pool", vring[:, kt, :], cv[b, kt * 128:(kt + 1) * 128, :], [], [("vring", kt)])
            group([dict(xsrc=x_s[b], nrows=64, ydst=y_s[b], kind="s", b=b, Tg=None, slots=[0, 1, 2, 3, 4], own_slot=4,
                        first=True, conv_out=True, kv_out=0, nchunks=1, last=True)])

        P.assign(sems, dsems)
        block = es.enter_context(nc.Block())

        def fin_waits(e, eng):
            last = {}
            for op in P.ops[eng]:
                if op.dma:
                    last[id(op.sem)] = op
            for op in last.values():
                e.wait_ge(op.sem, op.val)

        @block.sync
        def _(e):
            P.emit("sp", e)
            fin_waits(e, "sp")
            fin_waits(e, "pool")

        @block.gpsimd
        def _(e):
            P.emit("pool", e)

        @block.vector
        def _(e):
            P.emit("dve", e)

        @block.scalar
        def _(e):
            P.emit("act", e)

        @block.tensor
        def _(e):
            P.emit("pe", e)
    return nc


def _consts():
    ident = np.eye(128, dtype=np.float32)
    s = np.arange(128)[:, None]
    t = np.arange(128)[None, :]
    tri = ((s // 64 == t // 64) & (s <= t)).astype(np.float32)
    q = np.arange(128)[:, None]
    jhi = 384 + np.arange(256)[None, :]
    mhi = np.where((q < 64) & (jhi >= 576), NEG, 0.0).astype(np.float32)
    jlo = np.arange(128)[None, :]
    mb0 = np.where((q >= 64) & (jlo < 64), NEG, 0.0).astype(np.float32)
    i4 = np.eye(4, dtype=np.float32)
    rel = np.clip(q + 512 - jhi, -128, 128) + 128
    return ident, tri, mhi, mb0, i4, rel


_CACHE = {}


def kernel(x_prompt, x_sample, state_ml_conv, state_ml_C, state_ml_n, state_ml_m, cache_att_k, cache_att_v,
           w_in, b_ml_i, b_ml_f, ml_conv_w, ml_conv_b, ml_norm_g, att_rel_bias, w_ml_proj, w_att_proj, w_out,
           ffn1_w_gu, ffn1_w_down, ffn2_w_gu, ffn2_w_down, ln1_g, ln1_b, ln2_g, ln2_b, ln3_g, ln3_b):
    NC = 8
    f = lambda a: np.ascontiguousarray(np.asarray(a, dtype=np.float32))
    B, SEQ, _ = x_prompt.shape
    BS = x_sample.shape[0]
    NSP, NSS = B // NC, BS // NC
    key = (NSP, SEQ, NSS)
    if key not in _CACHE:
        _CACHE[key] = build(NSP, SEQ, NSS, STOP=float(os.environ.get('KSTOP', '99')))
    nc = _CACHE[key]
    ident, tri, mhi, mb0, i4, rel = _consts()
    table = f(att_rel_bias)[0]
    btg = np.ascontiguousarray(table[:, rel].transpose(1, 0, 2))
    btc = np.ascontiguousarray(np.broadcast_to(table[:, 256][None, :], (128, AH)))
    shared = {
        "w_gu1": f(ffn1_w_gu)[0], "w_dn1": f(ffn1_w_down)[0], "w_win": f(w_in)[0], "w_wml": f(w_ml_proj)[0],
        "w_wat": f(w_att_proj)[0], "w_wout": f(w_out)[0], "w_gu2": f(ffn2_w_gu)[0], "w_dn2": f(ffn2_w_down)[0],
        "lnp": np.ascontiguousarray(np.broadcast_to(
            np.stack([f(ln1_g)[0], f(ln1_b)[0], f(ln2_g)[0], f(ln2_b)[0], f(ln3_g)[0], f(ln3_b)[0]])[None], (128, 6, D))),
        "gbias": np.ascontiguousarray(np.broadcast_to(np.concatenate([f(b_ml_i)[0], f(b_ml_f)[0]])[None], (128, 8))),
        "conv_w": f(ml_conv_w)[0], "conv_b": f(ml_conv_b)[0], "norm_g": f(ml_norm_g)[0],
        "btg": btg, "btc": btc, "c_ident": ident, "c_tri": tri, "c_mhi": mhi, "c_mb0": mb0, "c_i4": i4,
    }
    xp, xs = f(x_prompt), f(x_sample)
    sc, sC, sn, smm = f(state_ml_conv)[0], f(state_ml_C)[0], f(state_ml_n)[0], f(state_ml_m)[0]
    kk = f(cache_att_k)[0].reshape(BS, 512, D)
    vv = f(cache_att_v)[0].reshape(BS, 512, D)
    if os.environ.get("KNOW", "") == "1":
        shared = {k: v for k, v in shared.items() if not k.startswith("w_")}
    in_maps = []
    for c in range(NC):
        m = dict(shared)
        ps_, ss_ = slice(c * NSP, (c + 1) * NSP), slice(c * NSS, (c + 1) * NSS)
        m.update({"x_p": xp[ps_], "x_s": xs[ss_], "st_conv": sc[ss_], "st_C": sC[ss_], "st_n": sn[ss_],
                  "st_m": smm[ss_], "ck": kk[ss_], "cv": vv[ss_]})
        in_maps.append(m)
    res = run_bass_kernel_spmd(nc, in_maps, core_ids=list(range(NC)))
    R = res.results

    def cat(name):
        return np.concatenate([np.asarray(R[c][name], dtype=np.float32) for c in range(NC)], axis=0)
    keep = cat("p_k").shape[1]
    outs = (
        cat("y_p"), cat("y_s"),
        cat("p_conv")[None], cat("s_conv")[None],
        cat("p_C")[None], cat("s_C")[None],
        cat("p_n")[None], cat("s_n")[None],
        cat("p_m")[None], cat("s_m")[None],
        cat("p_k").reshape(1, B, keep, AH, 64), cat("s_k").reshape(1, BS, 64, AH, 64),
        cat("p_v").reshape(1, B, keep, AH, 64), cat("s_v").reshape(1, BS, 64, AH, 64),
    )
    return outs
```
